# Optimizing a Trainium2 kernel written in Bass

```python
import math
import jax, jax.numpy as jnp
from jax import lax
import numpy as np

D_MODEL = 1024
BATCH = 4
SEQ = 4096
DEPTH = 4
DEC_BATCH = 2
DEC_SEQ = 16384
PAST_LEN = 128

GRID_W = 64
N_MIXERS = 3
D_FF = 4 * D_MODEL
EPS = 1e-6
ROPE_THETA = 10000.0
Q_BLOCK = 128
A_HEAD_DIM = 64
A_HEADS = D_MODEL // A_HEAD_DIM
A_WIN_ROWS = 8
A_WIN_COLS = 16
B_HEAD_DIM = 64
B_HEADS = D_MODEL // B_HEAD_DIM
B_PAIRS = ((128, 1), (512, 4), (2048, 16))
B_GROUPS = len(B_PAIRS)
C_HEAD_DIM = 128
C_Q_HEADS = D_MODEL // C_HEAD_DIM
C_KV_HEADS = 2
C_GROUP = C_Q_HEADS // C_KV_HEADS
C_QKV_WIDTH = (C_Q_HEADS + 2 * C_KV_HEADS) * C_HEAD_DIM
N_A = (DEPTH + 2) // 3
N_B = (DEPTH + 1) // 3
N_C = DEPTH // 3

kernel_name = "hybrid_bidir_encoder_natten_dilated_axial_gqa"


def rms_norm(x, g):
    x32 = x.astype(jnp.float32)
    y = x32 * lax.rsqrt(jnp.mean(x32 * x32, axis=-1, keepdims=True) + EPS)
    return (y * g.astype(jnp.float32)).astype(x.dtype)


def rope(x, pos):
    dim = x.shape[-1]
    half = dim // 2
    inv = ROPE_THETA ** (-jnp.arange(half, dtype=jnp.float32) / half)
    ang = pos.astype(jnp.float32)[:, None] * inv[None, :]
    cos = jnp.cos(ang).astype(x.dtype)
    sin = jnp.sin(ang).astype(x.dtype)
    x1, x2 = x[..., :half], x[..., half:]
    return jnp.concatenate([x1 * cos - x2 * sin, x1 * sin + x2 * cos], axis=-1)


def neighbourhood_attention(h, w_qkv, rpb, w_o):
    bn, seq_len, _ = h.shape
    rows = seq_len // GRID_W
    kr = min(A_WIN_ROWS, rows)
    kc = A_WIN_COLS
    qkv = (h @ w_qkv).reshape(bn, seq_len, 3, A_HEADS, A_HEAD_DIM)
    q = jnp.moveaxis(qkv[:, :, 0], 1, 2)
    k = jnp.moveaxis(qkv[:, :, 1], 1, 2)
    v = jnp.moveaxis(qkv[:, :, 2], 1, 2)
    cols = np.arange(GRID_W)
    col_start = np.clip(cols - kc // 2, 0, GRID_W - kc)
    col_idx = col_start[:, None] + np.arange(kc)[None, :]
    dc_idx = col_idx - cols[:, None] + (A_WIN_COLS - 1)
    scale = A_HEAD_DIM ** -0.5

    def row_fn(r):
        rs = jnp.clip(r - kr // 2, 0, rows - kr)
        key_rows = rs + jnp.arange(kr)
        kidx = (key_rows[None, :, None] * GRID_W + col_idx[:, None, :]).reshape(GRID_W, kr * kc)
        qr = lax.dynamic_slice_in_dim(q, r * GRID_W, GRID_W, axis=2)
        kg = k[:, :, kidx]
        vg = v[:, :, kidx]
        dr_idx = key_rows - r + (A_WIN_ROWS - 1)
        bias = rpb[:, dr_idx[None, :, None], dc_idx[:, None, :]].reshape(A_HEADS, GRID_W, kr * kc)
        s = jnp.einsum("bhqd,bhqkd->bhqk", qr, kg).astype(jnp.float32) * scale + bias.astype(jnp.float32)
        p = jax.nn.softmax(s, axis=-1).astype(v.dtype)
        return jnp.einsum("bhqk,bhqkd->bhqd", p, vg)

    o = lax.map(row_fn, jnp.arange(rows))
    o = jnp.transpose(o, (1, 0, 3, 2, 4)).reshape(bn, seq_len, A_HEADS * A_HEAD_DIM)
    return o @ w_o


def dilated_attention(h, w_qkv, w_o, pos):
    bn, seq_len, _ = h.shape
    nb = seq_len // Q_BLOCK
    qkv = (h @ w_qkv).reshape(bn, seq_len, B_GROUPS, 3, B_HEADS, B_HEAD_DIM)
    scale = B_HEAD_DIM ** -0.5
    outs, lses = [], []
    for g, (win, dil) in enumerate(B_PAIRS):
        q = rope(jnp.moveaxis(qkv[:, :, g, 0], 1, 2), pos)
        k = rope(jnp.moveaxis(qkv[:, :, g, 1], 1, 2), pos)
        v = jnp.moveaxis(qkv[:, :, g, 2], 1, 2)
        n_side = win // (2 * dil)
        offs = dil * np.arange(-n_side, n_side + 1)

        def blk(i, q=q, k=k, v=v, offs=offs):
            t0 = i * Q_BLOCK
            idx = t0 + jnp.arange(Q_BLOCK)[:, None] + offs[None, :]
            valid = (idx >= 0) & (idx < seq_len)
            idxc = jnp.clip(idx, 0, seq_len - 1)
            qb = lax.dynamic_slice_in_dim(q, t0, Q_BLOCK, axis=2)
            kb = k[:, :, idxc]
            vb = v[:, :, idxc]
            s = jnp.einsum("bhqd,bhqkd->bhqk", qb, kb).astype(jnp.float32) * scale
            s = jnp.where(valid[None, None], s, -jnp.inf)
            lse = jax.nn.logsumexp(s, axis=-1)
            p = jnp.exp(s - lse[..., None]).astype(v.dtype)
            return jnp.einsum("bhqk,bhqkd->bhqd", p, vb), lse

        o, lse = lax.map(blk, jnp.arange(nb))
        outs.append(jnp.transpose(o, (1, 2, 0, 3, 4)).reshape(bn, B_HEADS, seq_len, B_HEAD_DIM))
        lses.append(jnp.transpose(lse, (1, 2, 0, 3)).reshape(bn, B_HEADS, seq_len))
    wgt = jax.nn.softmax(jnp.stack(lses), axis=0).astype(h.dtype)
    o = jnp.einsum("gbhl,gbhld->bhld", wgt, jnp.stack(outs))
    o = jnp.moveaxis(o, 1, 2).reshape(bn, seq_len, B_HEADS * B_HEAD_DIM)
    return o @ w_o


def axial_rope(x, row, col):
    half = x.shape[-1] // 2
    return jnp.concatenate([rope(x[..., :half], row), rope(x[..., half:], col)], axis=-1)


def axial_gqa_attention(h, w_qkv, q_g, k_g, w_o, row, col):
    bn, seq_len, _ = h.shape
    nb = seq_len // Q_BLOCK
    qw = C_Q_HEADS * C_HEAD_DIM
    kw = C_KV_HEADS * C_HEAD_DIM
    proj = h @ w_qkv
    q = proj[..., :qw].reshape(bn, seq_len, C_Q_HEADS, C_HEAD_DIM)
    k = proj[..., qw:qw + kw].reshape(bn, seq_len, C_KV_HEADS, C_HEAD_DIM)
    v = proj[..., qw + kw:].reshape(bn, seq_len, C_KV_HEADS, C_HEAD_DIM)
    q = axial_rope(jnp.moveaxis(rms_norm(q, q_g), 1, 2), row, col)
    k = axial_rope(jnp.moveaxis(rms_norm(k, k_g), 1, 2), row, col)
    v = jnp.moveaxis(v, 1, 2)
    q = q.reshape(bn, C_KV_HEADS, C_GROUP, seq_len, C_HEAD_DIM)
    scale = C_HEAD_DIM ** -0.5

    def blk(i):
        qb = lax.dynamic_slice_in_dim(q, i * Q_BLOCK, Q_BLOCK, axis=3)
        s = jnp.einsum("bkgqd,bksd->bkgqs", qb, k).astype(jnp.float32) * scale
        p = jax.nn.softmax(s, axis=-1).astype(v.dtype)
        return jnp.einsum("bkgqs,bksd->bkgqd", p, v)

    o = lax.map(blk, jnp.arange(nb))
    o = jnp.transpose(o, (1, 0, 4, 2, 3, 5)).reshape(bn, seq_len, qw)
    return o @ w_o


def sq_relu_mlp(h, w1, w2):
    return jnp.square(jax.nn.relu(h @ w1)) @ w2


def trunk(x, c, w_mod, b_mod, norm_g, final_g, a_w_qkv, a_rpb, a_w_o, b_w_qkv, b_w_o,
          c_w_qkv, c_q_g, c_k_g, c_w_o, mlp_w1, mlp_w2):
    seq_len = x.shape[1]
    t = jnp.arange(seq_len)
    row = t // GRID_W
    col = t % GRID_W
    c_act = jax.nn.silu(c)
    for i in range(DEPTH):
        mod = (c_act @ w_mod[i] + b_mod[i])[:, None, :]
        sh1, sc1, g1, sh2, sc2, g2 = jnp.split(mod, 6, axis=-1)
        h = rms_norm(x, norm_g[i, 0]) * (1 + sc1) + sh1
        kind, j = i % N_MIXERS, i // N_MIXERS
        if kind == 0:
            m = neighbourhood_attention(h, a_w_qkv[j], a_rpb[j], a_w_o[j])
        elif kind == 1:
            m = dilated_attention(h, b_w_qkv[j], b_w_o[j], t)
        else:
            m = axial_gqa_attention(h, c_w_qkv[j], c_q_g[j], c_k_g[j], c_w_o[j], row, col)
        x = x + g1 * m
        h = rms_norm(x, norm_g[i, 1]) * (1 + sc2) + sh2
        x = x + g2 * sq_relu_mlp(h, mlp_w1[i], mlp_w2[i])
    return rms_norm(x, final_g)


def setup_inputs(seed: int = 0) -> dict:
    key = jax.random.key(seed)
    ks = jax.random.split(key, 20)
    D = D_MODEL

    def nrm(k, shape, scale):
        return jax.random.normal(k, shape, jnp.float32) * scale

    return {
        "x_prompt": nrm(ks[0], (BATCH, SEQ, D), 1.0),
        "x_sample": nrm(ks[1], (DEC_BATCH, DEC_SEQ, D), 1.0),
        "c_prompt": nrm(ks[2], (BATCH, D), 1.0),
        "c_sample": nrm(ks[3], (DEC_BATCH, D), 1.0),
        "w_mod": nrm(ks[4], (DEPTH, D, 6 * D), 0.5 * D ** -0.5),
        "b_mod": nrm(ks[5], (DEPTH, 6 * D), 0.02),
        "norm_g": 1.0 + nrm(ks[6], (DEPTH, 2, D), 0.02),
        "final_g": 1.0 + nrm(ks[7], (D,), 0.02),
        "a_w_qkv": nrm(ks[8], (N_A, D, 3 * A_HEADS * A_HEAD_DIM), D ** -0.5),
        "a_rpb": nrm(ks[9], (N_A, A_HEADS, 2 * A_WIN_ROWS - 1, 2 * A_WIN_COLS - 1), 0.1),
        "a_w_o": nrm(ks[10], (N_A, A_HEADS * A_HEAD_DIM, D), (A_HEADS * A_HEAD_DIM) ** -0.5),
        "b_w_qkv": nrm(ks[11], (N_B, D, B_GROUPS * 3 * B_HEADS * B_HEAD_DIM), D ** -0.5),
        "b_w_o": nrm(ks[12], (N_B, B_HEADS * B_HEAD_DIM, D), (B_HEADS * B_HEAD_DIM) ** -0.5),
        "c_w_qkv": nrm(ks[13], (N_C, D, C_QKV_WIDTH), D ** -0.5),
        "c_q_g": 1.0 + nrm(ks[14], (N_C, C_HEAD_DIM), 0.02),
        "c_k_g": 1.0 + nrm(ks[15], (N_C, C_HEAD_DIM), 0.02),
        "c_w_o": nrm(ks[16], (N_C, C_Q_HEADS * C_HEAD_DIM, D), (C_Q_HEADS * C_HEAD_DIM) ** -0.5),
        "mlp_w1": nrm(ks[17], (DEPTH, D, D_FF), D ** -0.5),
        "mlp_w2": nrm(ks[18], (DEPTH, D_FF, D), D_FF ** -0.5),
    }


def reference(x_prompt, x_sample, c_prompt, c_sample, w_mod, b_mod, norm_g, final_g,
              a_w_qkv, a_rpb, a_w_o, b_w_qkv, b_w_o, c_w_qkv, c_q_g, c_k_g, c_w_o,
              mlp_w1, mlp_w2):
    y_prompt = trunk(x_prompt, c_prompt, w_mod, b_mod, norm_g, final_g, a_w_qkv, a_rpb, a_w_o,
                     b_w_qkv, b_w_o, c_w_qkv, c_q_g, c_k_g, c_w_o, mlp_w1, mlp_w2)
    y_sample = trunk(x_sample, c_sample, w_mod, b_mod, norm_g, final_g, a_w_qkv, a_rpb, a_w_o,
                     b_w_qkv, b_w_o, c_w_qkv, c_q_g, c_k_g, c_w_o, mlp_w1, mlp_w2)
    return (y_prompt, y_sample)
```

```python
import numpy as np
from contextlib import ExitStack
import concourse.bass as bass
import concourse.mybir as mybir
from concourse.bass_utils import run_bass_kernel_spmd

F32 = mybir.dt.float32
BF16 = mybir.dt.bfloat16
AF = mybir.ActivationFunctionType
ALU = mybir.AluOpType
D = 1024
DFF = 4096
NEG = -30000.0
EPS = 1e-6
PADK = 1024


class Sem:
    def __init__(self, h):
        self.h = h
        self.total = 0


class Eng:
    def __init__(self, name, h, sem):
        self.name, self.h, self.sem, self.n, self.waited = name, h, sem, 0, {}


class Buf:
    def __init__(self, name, dram=False):
        self.name, self.dram = name, dram
        self.w = None
        self.rd = {}
        self.dw = set()
        self.dr = set()
        self.sem = None
        self.strict = False


class K:
    def __init__(self, nc, stack, n_dma_sems=90):
        self.nc = nc
        self.engs = {}
        for name, h in (("pe", nc.tensor), ("act", nc.scalar), ("dve", nc.vector),
                        ("pool", nc.gpsimd), ("sp", nc.sync)):
            s = Sem(stack.enter_context(nc.semaphore("s_" + name)))
            self.engs[name] = Eng(name, h, s)
        self.dsems = [Sem(stack.enter_context(nc.semaphore("d%d" % i))) for i in range(n_dma_sems)]
        self.next_ds = 0

    def _get_sem(self, b):
        if b.sem is None:
            b.sem = self.dsems[self.next_ds % len(self.dsems)]
            self.next_ds += 1
        return b.sem

    def _emit_waits(self, E, need):
        for sem, val in need.items():
            if E.waited.get(sem, 0) < val:
                E.h.wait_ge(sem.h, val)
                E.waited[sem] = val

    def op(self, e, fn, reads=(), writes=()):
        E = self.engs[e]
        need = {}

        def add(sem, val):
            if need.get(sem, 0) < val:
                need[sem] = val
        for b in reads:
            if b.w is not None and (b.w[0] is not E or b.strict or E.name != "pe"):
                add(b.w[0].sem, b.w[1])
            for s in b.dw:
                add(s, s.total)
        for b in writes:
            if b.w is not None and (b.w[0] is not E or b.strict):
                add(b.w[0].sem, b.w[1])
            for s in b.dw:
                add(s, s.total)
        for b in writes:
            for F, n in b.rd.items():
                if F is not E or b.strict:
                    add(F.sem, n)
            for s in b.dr:
                add(s, s.total)
        self._emit_waits(E, need)
        inst = fn(E.h)
        inst.then_inc(E.sem.h, 1)
        E.n += 1
        E.sem.total = E.n
        for b in writes:
            b.w = (E, E.n)
            b.rd = {}
            b.dw = set()
            b.dr = set()
        for b in reads:
            if b not in writes:
                b.rd[E] = E.n

    def dma(self, q, out, in_, dst, src, **kw):
        Q = self.engs[q]
        need = {}

        def add(sem, val):
            if need.get(sem, 0) < val:
                need[sem] = val
        if src.w is not None:
            add(src.w[0].sem, src.w[1])
        for s in src.dw:
            add(s, s.total)
        if not dst.dram:
            if dst.w is not None:
                add(dst.w[0].sem, dst.w[1])
            for s in dst.dw:
                add(s, s.total)
            for F, n in dst.rd.items():
                add(F.sem, n)
            for s in dst.dr:
                add(s, s.total)
        self._emit_waits(Q, need)
        sem = self._get_sem(src if dst.dram else dst)
        inst = Q.h.dma_start(out=out, in_=in_, **kw)
        inst.then_inc(sem.h, 16)
        sem.total += 16
        if dst.dram:
            dst.dw.add(sem)
        else:
            dst.w = None
            dst.rd = {}
            dst.dr = set()
            dst.dw = {sem}
        if not src.dram:
            src.dr.add(sem)

    def barrier(self):
        for E in self.engs.values():
            need = {}
            for Fe in self.engs.values():
                if Fe is not E and Fe.n > 0:
                    need[Fe.sem] = Fe.n
            for s in self.dsems:
                if s.total > 0:
                    need[s] = s.total
            self._emit_waits(E, need)


class T:
    def __init__(self, h, name, strict=False):
        self.h = h
        self.b = Buf(name)
        self.b.strict = strict

    def __getitem__(self, k):
        return self.h[k]


def build(cfg):
    L = cfg["L"]
    LP = cfg["LP"]
    kinds = cfg["kinds"]
    NL = len(kinds)
    R = L // 64
    RB = LP // 64
    NT = L // 128
    NB = L // 512
    nc = bass.Bass("TRN2", target_bir_lowering=False)

    def din(name, shape, dt=F32):
        return nc.dram_tensor(name, list(shape), dt, kind="ExternalInput").ap()

    def dsc(name, shape, dt):
        return nc.dram_tensor(name, list(shape), dt, kind="Internal").ap()

    nA = sum(1 for k in kinds if k == 0)
    nB = sum(1 for k in kinds if k == 1)
    nC = sum(1 for k in kinds if k == 2)
    x_in = din("x", [L, D])
    cT = din("cT", [128, 8])
    w_mod = din("w_mod", [NL, D, 6 * D])
    b_mod = din("b_mod", [NL, 6 * D])
    norm_g = din("norm_g", [NL, 2, D])
    final_g = din("final_g", [D])
    a_w_qkv = din("a_w_qkv", [max(nA, 1), D, 3 * D])
    a_biasT = din("a_biasT", [max(nA, 1), 16, 64, 15 * 64])
    a_w_o = din("a_w_o", [max(nA, 1), D, D])
    b_w_qkv = din("b_w_qkv", [max(nB, 1), D, 9 * D])
    b_w_o = din("b_w_o", [max(nB, 1), D, D])
    c_w_qkv = din("c_w_qkv", [max(nC, 1), D, 1536])
    c_gcol = din("c_gcol", [max(nC, 1), 128, 4])
    c_w_o = din("c_w_o", [max(nC, 1), D, D])
    mlp_w1 = din("mlp_w1", [NL, D, DFF])
    mlp_w2 = din("mlp_w2", [NL, DFF, D])
    ident_in = din("ident", [128, 128])
    cosB = din("cosB", [128, L])
    sinB = din("sinB", [128, L])
    cosC = din("cosC", [128, L])
    sinC = din("sinC", [128, L])
    bandm = din("bandm", [128, 256])
    NQT = L // 128
    bbias = din("bbias", [128, 3 * NQT * 2])
    cbias = din("cbias", [128, NT])
    flags = din("flags", [128, 2])
    y_out = nc.dram_tensor("y", [L, D], F32, kind="ExternalOutput").ap()

    xA = dsc("xA", [L, D], F32)
    xB = dsc("xB", [L, D], F32)
    modD = dsc("modD", [NL, 6, D], F32)
    QT = [dsc("QT%d" % g, [D, L], BF16) for g in range(3)]
    KT = [dsc("KT%d" % g, [D, L + 2 * PADK], BF16) for g in range(3)]
    VX = [dsc("VX%d" % g, [L + 2 * PADK, 1040], BF16) for g in range(3)]
    ON = [dsc("ON%d" % g, [L, 1040], F32) for g in range(3)]
    VXC = dsc("VXC", [L, 258], BF16)
    ONC = dsc("ONC", [L, 1032], F32)

    with ExitStack() as top:
        k = K(nc, top)
        dbuf = {}

        def DB(ap_name):
            if ap_name not in dbuf:
                dbuf[ap_name] = Buf(ap_name, dram=True)
            return dbuf[ap_name]
        IN = DB("inputs")

        uid = [0]

        def sb(stack, name, shape, dt=F32, strict=False):
            uid[0] += 1
            nm = "t%d_%s" % (uid[0], name)
            return T(stack.enter_context(nc.sbuf_tensor(nm, list(shape), dt)), nm, strict)

        ps = [T(top.enter_context(nc.psum_tensor("ps%d" % i, [128, 512], F32)), "ps%d" % i) for i in range(8)]
        ident = sb(top, "ident", [128, 128])
        k.dma("sp", ident[:], ident_in[:, :], ident.b, IN)
        ones_f = sb(top, "ones_f", [128, 128])
        k.op("dve", lambda h: h.memset(ones_f[:], 1.0), writes=[ones_f.b])
        epsc = sb(top, "epsc", [128, 1])
        k.op("dve", lambda h: h.memset(epsc[:], EPS), writes=[epsc.b])
        flg = sb(top, "flg", [128, 2])
        k.dma("sp", flg[:], flags[:, :], flg.b, IN)

        with ExitStack() as st:
            zt = sb(st, "zt", [128, 1040], BF16)
            k.op("dve", lambda h: h.memset(zt[:], 0.0), writes=[zt.b])
            for g in range(3):
                for side in range(2):
                    c0 = side * (PADK + L)
                    for ch in range(8):
                        k.dma("pool", KT[g][ch * 128:(ch + 1) * 128, c0:c0 + PADK], zt[:, 0:PADK],
                              DB("KT%d" % g), zt.b)
                        k.dma("pool", VX[g][c0 + ch * 128:c0 + (ch + 1) * 128, :], zt[:, :],
                              DB("VX%d" % g), zt.b)
            sc = sb(st, "sc", [128, 8], strict=True)
            k.dma("sp", sc[:], cT[:, :], sc.b, IN)
            k.op("act", lambda h: h.activation(out=sc[:], in_=sc[:], func=AF.Silu), reads=[sc.b], writes=[sc.b])
            wm = [sb(st, "wm%d" % i, [128, 8, 512]) for i in range(2)]
            modrow = sb(st, "modrow", [1, 6 * D])
            brow = sb(st, "brow", [1, 6 * D])
            grow = sb(st, "grow", [1, 2 * D])
            outrow = sb(st, "outrow", [1, 6 * D])
            for li in range(NL):
                k.dma("sp", brow[:], b_mod[li:li + 1, :], brow.b, IN)
                k.dma("sp", grow[:], norm_g[li:li + 1].rearrange("o t d -> o (t d)"), grow.b, IN)
                for n in range(12):
                    w = wm[n % 2]
                    k.dma("sp", w[:], w_mod[li].rearrange("(k p) n -> p k n", p=128)[:, :, n * 512:(n + 1) * 512], w.b, IN)
                    p = ps[n % 2]
                    for kk in range(8):
                        k.op("pe", lambda h, kk=kk, w=w, p=p: h.matmul(p[0:1, :], lhsT=sc[:, kk:kk + 1], rhs=w[:, kk, :],
                                                                     start=(kk == 0), stop=(kk == 7)),
                             reads=[sc.b, w.b], writes=[p.b])
                    k.op("dve", lambda h, n=n, p=p: h.tensor_tensor(out=modrow[0:1, n * 512:(n + 1) * 512], in0=p[0:1, :],
                                                                  in1=brow[0:1, n * 512:(n + 1) * 512], op=ALU.add),
                         reads=[p.b, brow.b], writes=[modrow.b])
                for sub in range(2):
                    o = sub * 3 * D
                    k.op("dve", lambda h, o=o, sub=sub: h.scalar_tensor_tensor(
                        out=outrow[0:1, o:o + D], in0=modrow[0:1, o + D:o + 2 * D], scalar=1.0,
                        in1=grow[0:1, sub * D:(sub + 1) * D], op0=ALU.add, op1=ALU.mult),
                        reads=[modrow.b, grow.b], writes=[outrow.b])
                    k.op("dve", lambda h, o=o: h.tensor_copy(out=outrow[0:1, o + D:o + 2 * D], in_=modrow[0:1, o:o + D]),
                         reads=[modrow.b], writes=[outrow.b])
                    k.op("dve", lambda h, o=o: h.tensor_copy(out=outrow[0:1, o + 2 * D:o + 3 * D], in_=modrow[0:1, o + 2 * D:o + 3 * D]),
                         reads=[modrow.b], writes=[outrow.b])
                k.dma("pool", modD[li:li + 1].rearrange("o s d -> o (s d)"), outrow[:], DB("modD"), outrow.b)
            k.barrier()

        def load_cols(t, li, kind):
            src = bass.AP(tensor=modD.tensor, offset=(li * 6 + kind) * D, ap=[[1, 128], [128, 8]])
            k.dma("sp", t[:], src, t.b, DB("modD"), allow_slow_non_contiguous=True)

        def load_bcast(t, li, kind):
            src = bass.AP(tensor=modD.tensor, offset=(li * 6 + kind) * D, ap=[[0, 128], [1, D]])
            k.dma("sp", t[:], src, t.b, DB("modD"))

        class Norm:
            def __init__(self, st, li, sub, alloc_xt=True):
                self.xt = [sb(st, "xt%d" % i, [128, D]) for i in range(2)] if alloc_xt else None
                self.junk = sb(st, "junk", [128, D], BF16)
                self.xn = [sb(st, "xn%d" % i, [128, D]) for i in range(2)]
                self.ssq = [sb(st, "ssq%d" % i, [128, 1], strict=True) for i in range(2)]
                self.Ac = sb(st, "Ac", [128, 8])
                self.Bc = sb(st, "Bc", [128, 8])
                load_cols(self.Ac, li, sub * 3 + 0)
                load_cols(self.Bc, li, sub * 3 + 1)
                self.i = 0

            def rstd_of(self, xt, ssq, n_el):
                junk = self.junk
                k.op("act", lambda h: h.activation(out=junk[:], in_=xt[:], func=AF.Square, accum_out=ssq[:]),
                     reads=[xt.b], writes=[junk.b, ssq.b])
                k.op("act", lambda h: h.activation(out=ssq[:], in_=ssq[:], func=AF.Sqrt, scale=1.0 / n_el, bias=epsc[:, 0:1]),
                     reads=[ssq.b, epsc.b], writes=[ssq.b])
                k.op("dve", lambda h: h.reciprocal(out=ssq[:], in_=ssq[:]), reads=[ssq.b], writes=[ssq.b])

            def run(self, xsrc_ap, xsrc_buf, hT, col0, keep=None):
                i = self.i
                self.i += 1
                xt = keep if keep is not None else self.xt[i % 2]
                xn, ssq = self.xn[i % 2], self.ssq[i % 2]
                k.dma("sp", xt[:], xsrc_ap, xt.b, xsrc_buf)
                self.rstd_of(xt, ssq, D)
                k.op("act", lambda h: h.activation(out=xn[:], in_=xt[:], func=AF.Copy, scale=ssq[:, 0:1]),
                     reads=[xt.b, ssq.b], writes=[xn.b])
                for half in range(2):
                    p = ps[6 + half]
                    for j in range(4):
                        kk = half * 4 + j
                        k.op("pe", lambda h, kk=kk, j=j, p=p: h.transpose(p[:, j * 128:(j + 1) * 128], xn[:, kk * 128:(kk + 1) * 128], ident[:]),
                             reads=[xn.b, ident.b], writes=[p.b])
                    for j in range(4):
                        kk = half * 4 + j
                        k.op("act", lambda h, kk=kk, j=j, p=p: h.activation(
                            out=hT[:, kk, col0:col0 + 128], in_=p[:, j * 128:(j + 1) * 128], func=AF.Identity,
                            scale=self.Ac[:, kk:kk + 1], bias=self.Bc[:, kk:kk + 1]),
                            reads=[p.b, self.Ac.b, self.Bc.b], writes=[hT.b])

        def load_w_bf16(t, w_ap, ncols, kchunks=8, piece=1024):
            v = w_ap.rearrange("(k p) n -> p k n", p=128)
            for c0 in range(0, ncols, piece):
                c1 = min(ncols, c0 + piece)
                for k0 in range(0, kchunks, 8):
                    k.dma("pool", t[:, k0:k0 + 8, c0:c1], v[:, k0:k0 + 8, c0:c1], t.b, IN)

        def phase1(li, x_ap, x_buf, kind, j):
            groups = 3 if kind == 1 else 1
            for g in range(groups):
                with ExitStack() as st:
                    if kind == 0:
                        wsrc, nq, nk, nv, hd = a_w_qkv[j], 1024, 1024, 1024, 64
                    elif kind == 1:
                        wsrc, nq, nk, nv, hd = b_w_qkv[j][:, g * 3072:(g + 1) * 3072], 1024, 1024, 1024, 64
                    else:
                        wsrc, nq, nk, nv, hd = c_w_qkv[j], 1024, 256, 256, 128
                    ncol = nq + nk + nv
                    nqk = nq + nk
                    W = sb(st, "W", [128, 8, ncol], BF16)
                    load_w_bf16(W, wsrc, ncol)
                    rope = kind != 0
                    if rope:
                        Wp = sb(st, "Wp", [128, 8, nqk], BF16)
                        for kk in range(8):
                            for (base, ncs) in ((0, nq), (nq, nk)):
                                gi = ncs // 256
                                vi = W[:, kk, base:base + ncs].rearrange("p (g h t j) -> p g h t j", g=gi, h=4, t=2, j=32)
                                vo = Wp[:, kk, base:base + ncs].rearrange("p (g t h j) -> p g t h j", g=gi, h=4, t=2, j=32)
                                for t in range(2):
                                    k.op("dve" if t == 0 else "pool", lambda h, vi=vi, vo=vo, t=t: h.tensor_copy(out=vo[:, :, t, :, :], in_=vi[:, :, :, t, :]),
                                         reads=[W.b], writes=[Wp.b])
                        if kind == 2:
                            gcol = sb(st, "gcol", [128, 4])
                            k.dma("sp", gcol[:], c_gcol[j], gcol.b, IN)
                            Mblk = sb(st, "Mblk", [128, 128])
                            k.op("dve", lambda h: h.memset(Mblk[:], 0.0), writes=[Mblk.b])
                            k.op("dve", lambda h: h.memset(Mblk[0:64, 0:64], 1.0), writes=[Mblk.b])
                            k.op("dve", lambda h: h.memset(Mblk[64:128, 64:128], 1.0), writes=[Mblk.b])
                    nrm = Norm(st, li, 0)
                    hT = [sb(st, "hT%d" % i, [128, 8, 512], BF16) for i in range(2)]
                    qst = [sb(st, "qst%d" % i, [128, 512], BF16) for i in range(4)]
                    H = 16 if kind != 2 else 2
                    vst = [sb(st, "vst%d" % i, [128, H, hd + 1], BF16) for i in range(2)]
                    for v in vst:
                        k.op("dve", lambda h, v=v: h.memset(v[:], 1.0), writes=[v.b])
                    if rope:
                        cs = [sb(st, "cs%d" % i, [128, 512]) for i in range(2)]
                        sn = [sb(st, "sn%d" % i, [128, 512]) for i in range(2)]
                        tA = [sb(st, "tA%d" % i, [128, 512]) for i in range(2)]
                        tB = [sb(st, "tB%d" % i, [128, 512]) for i in range(2)]
                        tC = [sb(st, "tC%d" % i, [128, 512]) for i in range(2)]
                        tD = [sb(st, "tD%d" % i, [128, 512]) for i in range(2)]
                        if kind == 2:
                            tabs = [[sb(st, "tab%d_%d" % (i, n_), [128, 512]) for n_ in range(8)] for i in range(2)]
                            sqa = [sb(st, "sqa%d" % i, [128, 512]) for i in range(2)]
                            sqb = [sb(st, "sqb%d" % i, [128, 512]) for i in range(2)]
                            rs = [sb(st, "rs%d" % i, [128, 512]) for i in range(2)]
                    qi = 0
                    vi_ = 0
                    KTg, QTg = KT[g], QT[g]
                    for b in range(NB):
                        h_ = hT[b % 2]
                        for tt in range(4):
                            t0 = b * 512 + tt * 128
                            nrm.run(x_ap[t0:t0 + 128, :], x_buf, h_, tt * 128)
                        if not rope:
                            for c in range(nqk // 128):
                                isq = c < nq // 128
                                pq = ps[qi % 4]
                                for kk in range(8):
                                    k.op("pe", lambda h, kk=kk, c=c, pq=pq: h.matmul(pq[:, :], lhsT=W[:, kk, c * 128:(c + 1) * 128], rhs=h_[:, kk, :],
                                                                                   start=(kk == 0), stop=(kk == 7)),
                                         reads=[W.b, h_.b], writes=[pq.b])
                                q_ = qst[qi % 4]
                                if qi % 2:
                                    k.op("act", lambda h, q_=q_, pq=pq: h.copy(out=q_[:], in_=pq[:, :]), reads=[pq.b], writes=[q_.b])
                                else:
                                    k.op("dve", lambda h, q_=q_, pq=pq: h.tensor_copy(out=q_[:], in_=pq[:, :]), reads=[pq.b], writes=[q_.b])
                                if isq:
                                    k.dma("pool", QTg[c * 128:(c + 1) * 128, b * 512:(b + 1) * 512], q_[:], DB("QT%d" % g), q_.b)
                                else:
                                    ck = c - nq // 128
                                    k.dma("pool", KTg[ck * 128:(ck + 1) * 128, PADK + b * 512:PADK + (b + 1) * 512], q_[:], DB("KT%d" % g), q_.b)
                                qi += 1
                        else:
                            c_, s_ = cs[b % 2], sn[b % 2]
                            ctab, stab = (cosB, sinB) if kind == 1 else (cosC, sinC)
                            k.dma("sp", c_[:], ctab[:, b * 512:(b + 1) * 512], c_.b, IN)
                            k.dma("sp", s_[:], stab[:, b * 512:(b + 1) * 512], s_.b, IN)
                            if kind == 2:
                                tb_ = tabs[b % 2]
                                spec = [(c_, 0), (s_, 1), (s_, 0), (c_, 1), (c_, 2), (s_, 3), (s_, 2), (c_, 3)]
                                for n_, (src_, gc) in enumerate(spec):
                                    k.op("pool" if n_ % 2 else "dve", lambda h, n_=n_, src_=src_, gc=gc: h.tensor_scalar(
                                        out=tb_[n_][:], in0=src_[:], scalar1=gcol[:, gc:gc + 1], scalar2=None, op0=ALU.mult),
                                        reads=[src_.b, gcol.b], writes=[tb_[n_].b])
                            for pi in range(nqk // 256):
                                isq = pi < nq // 256
                                pl = pi if isq else pi - nq // 256
                                cb0 = pi * 256
                                pA = ps[(qi % 2) * 2]
                                pB = ps[(qi % 2) * 2 + 1]
                                for (pp, off) in ((pA, 0), (pB, 128)):
                                    for kk in range(8):
                                        k.op("pe", lambda h, kk=kk, pp=pp, off=off: h.matmul(pp[:, :], lhsT=Wp[:, kk, cb0 + off:cb0 + off + 128], rhs=h_[:, kk, :],
                                                                                            start=(kk == 0), stop=(kk == 7)),
                                             reads=[Wp.b, h_.b], writes=[pp.b])
                                a1, a2, a3, a4 = tA[qi % 2], tB[qi % 2], tC[qi % 2], tD[qi % 2]
                                q1, q2 = qst[(2 * qi) % 4], qst[(2 * qi + 1) % 4]
                                if kind == 1:
                                    m1, m2, m3, m4 = c_, s_, s_, c_
                                else:
                                    o8 = 0 if isq else 4
                                    m1, m2, m3, m4 = tb_[o8 + 0], tb_[o8 + 1], tb_[o8 + 2], tb_[o8 + 3]
                                    s2a, s2b, r2 = sqa[qi % 2], sqb[qi % 2], rs[qi % 2]
                                    k.op("act", lambda h: h.activation(out=s2a[:], in_=pA[:, :], func=AF.Square), reads=[pA.b], writes=[s2a.b])
                                    k.op("act", lambda h: h.activation(out=s2b[:], in_=pB[:, :], func=AF.Square), reads=[pB.b], writes=[s2b.b])
                                    pss = ps[4]
                                    k.op("pe", lambda h: h.matmul(pss[:, :], lhsT=Mblk[:], rhs=s2a[:], start=True, stop=False),
                                         reads=[Mblk.b, s2a.b], writes=[pss.b])
                                    k.op("pe", lambda h: h.matmul(pss[:, :], lhsT=Mblk[:], rhs=s2b[:], start=False, stop=True),
                                         reads=[Mblk.b, s2b.b], writes=[pss.b])
                                    k.op("act", lambda h: h.activation(out=r2[:], in_=pss[:, :], func=AF.Sqrt, scale=1.0 / 128, bias=epsc[:, 0:1]),
                                         reads=[pss.b, epsc.b], writes=[r2.b])
                                    k.op("dve", lambda h: h.reciprocal(out=r2[:], in_=r2[:]), reads=[r2.b], writes=[r2.b])
                                k.op("dve", lambda h: h.tensor_tensor(out=a1[:], in0=pA[:, :], in1=m1[:], op=ALU.mult), reads=[pA.b, m1.b], writes=[a1.b])
                                k.op("dve", lambda h: h.tensor_tensor(out=a2[:], in0=pB[:, :], in1=m2[:], op=ALU.mult), reads=[pB.b, m2.b], writes=[a2.b])
                                k.op("dve", lambda h: h.tensor_tensor(out=a3[:], in0=pA[:, :], in1=m3[:], op=ALU.mult), reads=[pA.b, m3.b], writes=[a3.b])
                                k.op("dve", lambda h: h.tensor_tensor(out=a4[:], in0=pB[:, :], in1=m4[:], op=ALU.mult), reads=[pB.b, m4.b], writes=[a4.b])
                                if kind == 1:
                                    k.op("pool", lambda h: h.tensor_tensor(out=q1[:], in0=a1[:], in1=a2[:], op=ALU.subtract), reads=[a1.b, a2.b], writes=[q1.b])
                                    k.op("pool", lambda h: h.tensor_tensor(out=q2[:], in0=a3[:], in1=a4[:], op=ALU.add), reads=[a3.b, a4.b], writes=[q2.b])
                                else:
                                    k.op("pool", lambda h: h.tensor_tensor(out=a1[:], in0=a1[:], in1=a2[:], op=ALU.subtract), reads=[a1.b, a2.b], writes=[a1.b])
                                    k.op("pool", lambda h: h.tensor_tensor(out=q1[:], in0=a1[:], in1=r2[:], op=ALU.mult), reads=[a1.b, r2.b], writes=[q1.b])
                                    k.op("pool", lambda h: h.tensor_tensor(out=a3[:], in0=a3[:], in1=a4[:], op=ALU.add), reads=[a3.b, a4.b], writes=[a3.b])
                                    k.op("pool", lambda h: h.tensor_tensor(out=q2[:], in0=a3[:], in1=r2[:], op=ALU.mult), reads=[a3.b, r2.b], writes=[q2.b])
                                for (qq, off) in ((q1, 0), (q2, 128)):
                                    r0_ = pl * 256 + off
                                    if isq:
                                        k.dma("pool", QTg[r0_:r0_ + 128, b * 512:(b + 1) * 512], qq[:], DB("QT%d" % g), qq.b)
                                    else:
                                        k.dma("pool", KTg[r0_:r0_ + 128, PADK + b * 512:PADK + (b + 1) * 512], qq[:], DB("KT%d" % g), qq.b)
                                qi += 1
                        for tt in range(4):
                            v_ = vst[vi_ % 2]
                            vi_ += 1
                            t0 = b * 512 + tt * 128
                            for cc in range(max(1, nv // 512)):
                                w_ = min(512, nv)
                                pv = ps[5]
                                for kk in range(8):
                                    k.op("pe", lambda h, kk=kk, cc=cc, pv=pv, w_=w_: h.matmul(
                                        pv[:, 0:w_], lhsT=h_[:, kk, tt * 128:(tt + 1) * 128], rhs=W[:, kk, nqk + cc * 512:nqk + cc * 512 + w_],
                                        start=(kk == 0), stop=(kk == 7)), reads=[W.b, h_.b], writes=[pv.b])
                                nh = w_ // hd
                                k.op("act", lambda h, v_=v_, pv=pv, cc=cc, nh=nh, w_=w_: h.copy(
                                    out=v_[:, cc * nh:(cc + 1) * nh, 0:hd], in_=pv[:, 0:w_].rearrange("p (h d) -> p h d", d=hd)),
                                    reads=[pv.b], writes=[v_.b])
                            if kind == 2:
                                k.dma("pool", VXC[t0:t0 + 128, :], v_[:].rearrange("p h e -> p (h e)"), DB("VXC"), v_.b)
                            else:
                                k.dma("pool", VX[g][PADK + t0:PADK + t0 + 128, :], v_[:].rearrange("p h e -> p (h e)"), DB("VX%d" % g), v_.b)
                    k.barrier()

        def load_split(t, src, cols, blocks, dbname):
            for i, blk in enumerate(blocks):
                pi, b4 = blk // 4, blk % 4
                for two in range(2):
                    r0_ = pi * 256 + two * 128 + b4 * 32
                    k.dma("sp", t[i * 64 + two * 32:i * 64 + two * 32 + 32, :], src[r0_:r0_ + 32, cols], t.b, DB(dbname))

        def pipeline(items, stage1, stage2, depth):
            n = len(items)
            for i in range(min(depth, n)):
                stage1(items[i])
            for i in range(n):
                if i + depth < n:
                    stage1(items[i + depth])
                stage2(items[i])

        def attn_A(j):
            scale = 64 ** -0.5
            R2 = R // 2
            with ExitStack() as st:
                nbuf = 2 if L <= 4096 else 1
                KTc = [sb(st, "KTc%d" % i, [128, L], BF16) for i in range(nbuf)] * (2 // nbuf)
                QTc = [sb(st, "QTc%d" % i, [128, L], BF16) for i in range(nbuf)] * (2 // nbuf)
                VpE = [sb(st, "VpE%d" % i, [128, R2, 130], BF16) for i in range(nbuf)] * (2 // nbuf)
                VpO = [sb(st, "VpO%d" % i, [128, R2, 130], BF16) for i in range(nbuf)] * (2 // nbuf)
                bt = sb(st, "bt", [128, 2, 14 * 64])
                Et = [sb(st, "Et%d" % i, [128, 2, 14 * 64], BF16) for i in range(2)]
                pt = [sb(st, "pt%d" % i, [128, 6 * 64], BF16) for i in range(4)]
                ost = [sb(st, "ost%d" % i, [64, 2, 65]) for i in range(3)]
                itc = [0]
                for pr in range(8):
                    kt_, qt_, ve_, vo_, et_ = KTc[pr % 2], QTc[pr % 2], VpE[pr % 2], VpO[pr % 2], Et[pr % 2]
                    k.dma("sp", kt_[:], KT[0][pr * 128:(pr + 1) * 128, PADK:PADK + L], kt_.b, DB("KT0"))
                    k.dma("sp", qt_[:], QT[0][pr * 128:(pr + 1) * 128, :], qt_.b, DB("QT0"))
                    vsE = VX[0][PADK:PADK + L, pr * 130:(pr + 1) * 130].rearrange("(i p) e -> p i e", p=128)
                    vsO = VX[0][PADK + 64:PADK + 64 + L, pr * 130:(pr + 1) * 130].rearrange("(i p) e -> p i e", p=128)
                    for r0 in range(0, R2, 8):
                        k.dma("sp", ve_[:, r0:r0 + 8, :], vsE[:, r0:r0 + 8, :], ve_.b, DB("VX0"))
                        k.dma("sp", vo_[:, r0:r0 + 8, :], vsO[:, r0:r0 + 8, :], vo_.b, DB("VX0"))
                    for hh in range(2):
                        k.dma("sp", bt[0:64, hh, :], a_biasT[j, pr * 2 + hh][:, 0:14 * 64], bt.b, IN)
                        k.dma("sp", bt[64:128, hh, :], a_biasT[j, pr * 2 + hh][:, 64:15 * 64], bt.b, IN)
                    k.op("act", lambda h, et_=et_: h.activation(out=et_[:], in_=bt[:], func=AF.Exp), reads=[bt.b], writes=[et_.b])
                    items = []
                    for r in range(R):
                        rs0 = min(max(r - 4, 0), R - 8)
                        S = list(range(rs0, rs0 + 8))
                        tags = {kr: 0 for kr in S}
                        if RB < R and RB - 3 <= r <= RB - 1:
                            P = list(range(RB - 8, RB))
                            for kr in S:
                                if kr not in P:
                                    tags[kr] = 1
                            for kr in P:
                                if kr not in tags:
                                    tags[kr] = 2
                        krs = sorted(tags)
                        n = len(krs)
                        assert krs == list(range(krs[0], krs[0] + n)) and n <= 11
                        if n % 2:
                            tags[krs[-1] + 1] = 3
                            krs = krs + [krs[-1] + 1]
                            n += 1
                            assert krs[-1] < R
                        npair = n // 2
                        dr0 = krs[0] - r + 7
                        assert 0 <= dr0 and dr0 + 2 * (npair - 1) <= 13
                        for hh in range(2):
                            it = itc[0]
                            itc[0] += 1
                            items.append(dict(r=r, hh=hh, krs=krs, npair=npair, dr0=dr0, tags=tags, p_s=ps[it % 3],
                                              p_o=ps[3 + it % 2], p_=pt[it % 4], o_=ost[(it // 2) % 3]))

                    def stage1(d):
                        r, hh, krs, npair, dr0, tags, p_s, p_ = d["r"], d["hh"], d["krs"], d["npair"], d["dr0"], d["tags"], d["p_s"], d["p_"]
                        for i in range(npair):
                            a_ = krs[2 * i]
                            k.op("pe", lambda h: h.matmul(
                                p_s[:, i * 64:(i + 1) * 64], lhsT=kt_[hh * 64:(hh + 1) * 64, a_ * 64:a_ * 64 + 128],
                                rhs=qt_[hh * 64:(hh + 1) * 64, r * 64:(r + 1) * 64], start=True, stop=True),
                                reads=[kt_.b, qt_.b], writes=[p_s.b])
                        k.op("act", lambda h: h.activation(out=p_[:, 0:npair * 64], in_=p_s[:, 0:npair * 64], func=AF.Exp, scale=scale),
                             reads=[p_s.b], writes=[p_.b])
                        ev = et_[:, hh, :].rearrange("p (d q) -> p d q", q=64)[:, dr0:dr0 + 2 * (npair - 1) + 1:2, :]
                        k.op("dve", lambda h: h.tensor_tensor(
                            out=p_[:, 0:npair * 64].rearrange("p (i q) -> p i q", q=64), in0=p_[:, 0:npair * 64].rearrange("p (i q) -> p i q", q=64),
                            in1=ev, op=ALU.mult), reads=[p_.b, et_.b], writes=[p_.b])
                        for jj, kr in enumerate(krs):
                            tg = tags[kr]
                            if tg:
                                i, half = jj // 2, jj % 2
                                blk = p_[half * 64:(half + 1) * 64, i * 64:(i + 1) * 64]
                                if tg == 3:
                                    k.op("dve", lambda h: h.memset(blk, 0.0), writes=[p_.b])
                                else:
                                    fc = flg[half * 64:(half + 1) * 64, tg - 1:tg]
                                    k.op("dve", lambda h: h.tensor_scalar(out=blk, in0=blk, scalar1=fc, scalar2=None, op0=ALU.mult),
                                         reads=[p_.b, flg.b], writes=[p_.b])

                    def stage2(d):
                        r, hh, krs, npair, p_o, p_, o_ = d["r"], d["hh"], d["krs"], d["npair"], d["p_o"], d["p_"], d["o_"]
                        for i in range(npair):
                            a_ = krs[2 * i]
                            vt = ve_[:, a_ // 2, hh * 65:(hh + 1) * 65] if a_ % 2 == 0 else vo_[:, (a_ - 1) // 2, hh * 65:(hh + 1) * 65]
                            vb_ = ve_.b if a_ % 2 == 0 else vo_.b
                            k.op("pe", lambda h: h.matmul(p_o[0:64, 0:65], lhsT=p_[:, i * 64:(i + 1) * 64], rhs=vt,
                                                          start=(i == 0), stop=(i == npair - 1)), reads=[p_.b, vb_], writes=[p_o.b])
                        k.op("act", lambda h: h.copy(out=o_[:, hh, :], in_=p_o[0:64, 0:65]), reads=[p_o.b], writes=[o_.b])
                        if hh == 1:
                            k.dma("pool", ON[0][r * 64:(r + 1) * 64, pr * 130:(pr + 1) * 130], o_[:].rearrange("p h e -> p (h e)"), DB("ON0"), o_.b)
                    pipeline(items, stage1, stage2, 2)
                k.barrier()

        def attn_B(j):
            scale = 64 ** -0.5
            with ExitStack() as st:
                KTc = [sb(st, "KTc%d" % i, [128, L + 2 * PADK], BF16) for i in range(2)]
                QTc = [sb(st, "QTc%d" % i, [128, L], BF16) for i in range(2)]
                VH = (NT + 2) // 2
                Vall = [sb(st, "Vall%d" % i, [128, VH, 130], BF16) for i in range(2)]
                band = sb(st, "band", [128, 256], BF16)
                bandf = sb(st, "bandf", [128, 256])
                k.dma("sp", bandf[:], bandm[:, :], bandf.b, IN)
                k.op("dve", lambda h: h.tensor_copy(out=band[:], in_=bandf[:]), reads=[bandf.b], writes=[band.b])
                bb = sb(st, "bb", [128, 3 * NQT * 2])
                k.dma("sp", bb[:], bbias[:, :], bb.b, IN)
                pt = [sb(st, "pt%d" % i, [128, 256], BF16) for i in range(4)]
                NQB = 4
                ost = [sb(st, "ost%d" % i, [128, NQB, 130]) for i in range(3)]
                itc = [0]
                ci = 0
                rc = [0]
                sgc = [0]
                for pr in range(8):
                    for g, dil in enumerate((1, 4, 16)):
                        kt_, qt_ = KTc[ci % 2], QTc[ci % 2]
                        ci += 1
                        load_split(kt_, KT[g], slice(0, L + 2 * PADK), [2 * pr, 2 * pr + 1], "KT%d" % g)
                        load_split(qt_, QT[g], slice(0, L), [2 * pr, 2 * pr + 1], "QT%d" % g)
                        Mc = L // dil
                        nmb = Mc // 128
                        nqb = min(NQB, nmb)
                        items = []
                        for r in range(dil):
                            if dil == 1:
                                vmap = lambda s_: (Vall[0], s_) if s_ < VH else (Vall[1], s_ - VH)
                            else:
                                vb = Vall[rc[0] % 2]
                                rc[0] += 1
                                vmap = lambda s_, vb=vb: (vb, s_)
                            for mb in range(nmb):
                                if mb % nqb == 0:
                                    sgc[0] += 1
                                for hh in range(2):
                                    it = itc[0]
                                    itc[0] += 1
                                    items.append(dict(r=r, mb=mb, hh=hh, qtid=r * nmb + mb, vmap=vmap, o_=ost[sgc[0] % 3],
                                                      p_s=ps[it % 3], p_o=ps[3 + it % 2], p_=pt[it % 4]))

                        def load_v(r, vmap):
                            nt1 = nmb + 1
                            s0 = 0
                            while s0 < nt1:
                                vb, loc = vmap(s0)
                                n_ = min(8, nt1 - s0, VH - loc)
                                tok0 = PADK + dil * (128 * s0 - 64) + r
                                vsrc = bass.AP(tensor=VX[g].tensor, offset=tok0 * 1040 + pr * 130,
                                               ap=[[dil * 1040, 128], [128 * dil * 1040, n_], [1, 130]])
                                k.dma("sp", vb[:, loc:loc + n_, :], vsrc, vb.b, DB("VX%d" % g))
                                s0 += n_

                        def stage1(d):
                            r, mb, hh, qtid, p_s, p_ = d["r"], d["mb"], d["hh"], d["qtid"], d["p_s"], d["p_"]
                            if hh == 0 and mb == 0:
                                load_v(r, d["vmap"])
                            q0 = dil * 128 * mb + r
                            qap = qt_[hh * 64:(hh + 1) * 64, q0:q0 + 127 * dil + 1:dil]
                            for slot in range(2):
                                k0 = PADK + dil * (128 * mb - 64 + 128 * slot) + r
                                kap = kt_[hh * 64:(hh + 1) * 64, k0:k0 + 127 * dil + 1:dil]
                                k.op("pe", lambda h: h.matmul(p_s[:, slot * 128:(slot + 1) * 128], lhsT=kap, rhs=qap, start=True, stop=True),
                                     reads=[kt_.b, qt_.b], writes=[p_s.b])
                            for slot in range(2):
                                bc = bb[:, (g * NQT + qtid) * 2 + slot:(g * NQT + qtid) * 2 + slot + 1]
                                k.op("act", lambda h: h.activation(
                                    out=p_[:, slot * 128:(slot + 1) * 128], in_=p_s[:, slot * 128:(slot + 1) * 128],
                                    func=AF.Exp, scale=scale, bias=bc), reads=[p_s.b, bb.b], writes=[p_.b])
                            k.op("dve", lambda h: h.tensor_tensor(out=p_[:], in0=p_[:], in1=band[:], op=ALU.mult),
                                 reads=[p_.b, band.b], writes=[p_.b])

                        def stage2(d):
                            r, mb, hh, p_o, p_, o_ = d["r"], d["mb"], d["hh"], d["p_o"], d["p_"], d["o_"]
                            for slot in range(2):
                                vb, loc = d["vmap"](mb + slot)
                                k.op("pe", lambda h: h.matmul(
                                    p_o[:, 0:65], lhsT=p_[:, slot * 128:(slot + 1) * 128], rhs=vb[:, loc, hh * 65:(hh + 1) * 65],
                                    start=(slot == 0), stop=(slot == 1)), reads=[p_.b, vb.b], writes=[p_o.b])
                            qi_ = mb % nqb
                            k.op("act", lambda h: h.copy(out=o_[:, qi_, hh * 65:(hh + 1) * 65], in_=p_o[:, 0:65]), reads=[p_o.b], writes=[o_.b])
                            if hh == 1 and qi_ == nqb - 1:
                                mb0 = mb - (nqb - 1)
                                odst = bass.AP(tensor=ON[g].tensor, offset=(dil * 128 * mb0 + r) * 1040 + pr * 130,
                                               ap=[[dil * 1040, 128], [128 * dil * 1040, nqb], [1, 130]])
                                k.dma("pool", odst, o_[:, 0:nqb, :], DB("ON%d" % g), o_.b)
                        pipeline(items, stage1, stage2, 2)
                k.barrier()

        def attn_C(j):
            scale = 128 ** -0.5
            with ExitStack() as st:
                KTc = sb(st, "KTc", [128, L], BF16)
                Vc = sb(st, "Vc", [128, NT, 129], BF16)
                QTc = [sb(st, "QTc%d" % i, [128, L], BF16) for i in range(2)]
                cb = sb(st, "cb", [128, NT])
                k.dma("sp", cb[:], cbias[:, :], cb.b, IN)
                pt = [sb(st, "pt%d" % i, [128, 512], BF16) for i in range(4)]
                ost = [sb(st, "ost%d" % i, [128, 4, 129]) for i in range(2)]
                itc = [0]
                oic = [0]
                for kh in range(2):
                    load_split(KTc, KT[0], slice(PADK, PADK + L), [2 * kh, 2 * kh + 1], "KT0")
                    vcs = VXC[:, kh * 129:(kh + 1) * 129].rearrange("(t p) e -> p t e", p=128)
                    for t0_ in range(0, NT, 8):
                        k.dma("sp", Vc[:, t0_:t0_ + 8, :], vcs[:, t0_:t0_ + 8, :], Vc.b, DB("VXC"))
                    for qh in range(4):
                        head = kh * 4 + qh
                        qt_ = QTc[head % 2]
                        load_split(qt_, QT[0], slice(0, L), [2 * head, 2 * head + 1], "QT0")
                        items = []
                        for qb in range(NB):
                            for kt in range(NT):
                                it = itc[0]
                                itc[0] += 1
                                items.append(dict(qb=qb, kt=kt, p_s=ps[4 + it % 4], p_=pt[it % 4]))

                        def stage1(d):
                            qb, kt, p_s, p_ = d["qb"], d["kt"], d["p_s"], d["p_"]
                            k.op("pe", lambda h: h.matmul(
                                p_s[:, :], lhsT=KTc[:, kt * 128:(kt + 1) * 128], rhs=qt_[:, qb * 512:(qb + 1) * 512], start=True, stop=True),
                                reads=[KTc.b, qt_.b], writes=[p_s.b])
                            k.op("act", lambda h: h.activation(out=p_[:], in_=p_s[:, :], func=AF.Exp, scale=scale, bias=cb[:, kt:kt + 1]),
                                 reads=[p_s.b, cb.b], writes=[p_.b])

                        def stage2(d):
                            qb, kt, p_ = d["qb"], d["kt"], d["p_"]
                            for jq in range(4):
                                k.op("pe", lambda h: h.matmul(
                                    ps[jq][:, 0:129], lhsT=p_[:, jq * 128:(jq + 1) * 128], rhs=Vc[:, kt, :],
                                    start=(kt == 0), stop=(kt == NT - 1)), reads=[p_.b, Vc.b], writes=[ps[jq].b])
                            if kt == NT - 1:
                                o_ = ost[oic[0] % 2]
                                oic[0] += 1
                                for jq in range(4):
                                    k.op("dve", lambda h: h.tensor_copy(out=o_[:, jq, :], in_=ps[jq][:, 0:129]),
                                         reads=[ps[jq].b], writes=[o_.b])
                                odst = ONC[qb * 512:(qb + 1) * 512, head * 129:(head + 1) * 129].rearrange("(j p) e -> p j e", p=128)
                                k.dma("pool", odst, o_[:], DB("ONC"), o_.b)
                        pipeline(items, stage1, stage2, 2)
                k.barrier()

        def phase2b(li, kind, j, xs_ap, xs_buf, xd_ap, xd_buf):
            with ExitStack() as st:
                narr = 3 if kind == 1 else 1
                H, hd = (8, 128) if kind == 2 else (16, 64)
                Wd = H * (hd + 1)
                wo_src = (a_w_o, b_w_o, c_w_o)[kind][j]
                Wo = sb(st, "Wo", [128, 8, D], BF16)
                load_w_bf16(Wo, wo_src, D)
                G = sb(st, "G", [128, D])
                load_bcast(G, li, 2)
                nb_ = [[sb(st, "nb%d_%d" % (a, i), [128, H, hd + 1]) for a in range(narr)] for i in range(3)]
                rl = [sb(st, "rl%d" % i, [128, H], strict=True) for i in range(3)]
                Of = [sb(st, "Of%d" % i, [128, D]) for i in range(3)]
                oT = [sb(st, "oT%d" % i, [128, 8, 128], BF16) for i in range(3)]
                xt = [sb(st, "xt%d" % i, [128, D]) for i in range(3)]
                tmp = [sb(st, "tmp%d" % i, [128, D]) for i in range(3)]
                for tt in range(NT):
                    i = tt % 3
                    t0 = tt * 128
                    for a in range(narr):
                        src = (ONC if kind == 2 else ON[a])[t0:t0 + 128, 0:Wd]
                        k.dma("sp", nb_[i][a][:].rearrange("p h e -> p (h e)"), src, nb_[i][a].b, DB("ONC" if kind == 2 else "ON%d" % a))
                    k.dma("sp", xt[i][:], xs_ap[t0:t0 + 128, :], xt[i].b, xs_buf)
                    acc = nb_[i][0]
                    for a in range(1, narr):
                        k.op("dve", lambda h, acc=acc, o=nb_[i][a]: h.tensor_tensor(out=acc[:], in0=acc[:], in1=o[:], op=ALU.add),
                             reads=[acc.b, nb_[i][a].b], writes=[acc.b])
                    r_ = rl[i]
                    k.op("dve", lambda h, r_=r_, acc=acc: h.tensor_scalar(out=r_[:], in0=acc[:, :, hd], scalar1=1e-30, scalar2=None, op0=ALU.max),
                         reads=[acc.b], writes=[r_.b])
                    k.op("dve", lambda h, r_=r_: h.reciprocal(out=r_[:], in_=r_[:]), reads=[r_.b], writes=[r_.b])
                    o_ = Of[i]
                    rb = bass.AP(tensor=r_[:].tensor, offset=r_[:].offset, ap=[list(r_[:].ap[0]), [1, H], [0, hd]])
                    k.op("dve", lambda h, o_=o_, acc=acc, rb=rb: h.tensor_tensor(
                        out=o_[:].rearrange("p (h d) -> p h d", d=hd), in0=acc[:, :, 0:hd], in1=rb, op=ALU.mult),
                        reads=[acc.b, r_.b], writes=[o_.b])
                    oT_ = oT[i]
                    for half in range(2):
                        p = ps[4 + (2 * tt + half) % 4]
                        for jj in range(4):
                            kk = half * 4 + jj
                            k.op("pe", lambda h, kk=kk, jj=jj, p=p, o_=o_: h.transpose(p[:, jj * 128:(jj + 1) * 128], o_[:, kk * 128:(kk + 1) * 128], ident[:]),
                                 reads=[o_.b, ident.b], writes=[p.b])
                        k.op("act", lambda h, half=half, p=p, oT_=oT_: h.copy(out=oT_[:, half * 4:(half + 1) * 4, :].rearrange("p k t -> p (k t)"), in_=p[:, :]),
                             reads=[p.b], writes=[oT_.b])
                    tm = tmp[i]
                    for half in range(2):
                        p = ps[(2 * tt + half) % 4]
                        for kk in range(8):
                            k.op("pe", lambda h, kk=kk, half=half, p=p, oT_=oT_: h.matmul(p[:, :], lhsT=oT_[:, kk, :], rhs=Wo[:, kk, half * 512:(half + 1) * 512],
                                                                                       start=(kk == 0), stop=(kk == 7)), reads=[oT_.b, Wo.b], writes=[p.b])
                        k.op("dve", lambda h, half=half, p=p, tm=tm: h.tensor_tensor(out=tm[:, half * 512:(half + 1) * 512], in0=p[:, :],
                                                                                   in1=G[:, half * 512:(half + 1) * 512], op=ALU.mult),
                             reads=[p.b, G.b], writes=[tm.b])
                    k.op("pool", lambda h, tm=tm, x_=xt[i]: h.tensor_tensor(out=tm[:], in0=tm[:], in1=x_[:], op=ALU.add),
                         reads=[tm.b, xt[i].b], writes=[tm.b])
                    k.dma("pool", xd_ap[t0:t0 + 128, :], tm[:], xd_buf, tm.b)
                k.barrier()

        def phase3(li, xs_ap, xs_buf, xd_ap, xd_buf):
            TB = 256
            with ExitStack() as st:
                W1 = sb(st, "W1", [128, 8, DFF], BF16)
                W2 = sb(st, "W2", [128, 32, D], BF16)
                load_w_bf16(W1, mlp_w1[li], DFF)
                load_w_bf16(W2, mlp_w2[li], D, kchunks=32)
                G = sb(st, "G", [128, D])
                load_bcast(G, li, 5)
                nrm = Norm(st, li, 1, alloc_xt=False)
                hT = [sb(st, "hT%d" % i, [128, 8, TB], BF16) for i in range(2)]
                uT = sb(st, "uT", [128, 32, TB], BF16)
                xk = [[sb(st, "xk%d_%d" % (i, t), [128, D]) for t in range(TB // 128)] for i in range(2)]
                rb_ = [sb(st, "rb%d" % i, [128, TB]) for i in range(2)]
                tmp = [sb(st, "tmp%d" % i, [128, D]) for i in range(2)]
                ti = 0
                for b in range(L // TB):
                    h_ = hT[b % 2]
                    for tt in range(TB // 128):
                        t0 = b * TB + tt * 128
                        nrm.run(xs_ap[t0:t0 + 128, :], xs_buf, h_, tt * 128, keep=xk[b % 2][tt])
                    for f in range(32):
                        p = ps[f % 2]
                        r_ = rb_[f % 2]
                        for kk in range(8):
                            k.op("pe", lambda h, kk=kk, f=f, p=p: h.matmul(p[:, 0:TB], lhsT=W1[:, kk, f * 128:(f + 1) * 128], rhs=h_[:, kk, :],
                                                                         start=(kk == 0), stop=(kk == 7)), reads=[W1.b, h_.b], writes=[p.b])
                        k.op("act", lambda h, p=p, r_=r_: h.activation(out=r_[:], in_=p[:, 0:TB], func=AF.Relu), reads=[p.b], writes=[r_.b])
                        k.op("dve" if f % 2 else "pool", lambda h, f=f, r_=r_: h.tensor_tensor(out=uT[:, f, :], in0=r_[:], in1=r_[:], op=ALU.mult),
                             reads=[r_.b], writes=[uT.b])
                    for tt in range(TB // 128):
                        t0 = b * TB + tt * 128
                        tm = tmp[ti % 2]
                        ti += 1
                        for half in range(2):
                            p = ps[2 + half]
                            for f in range(32):
                                k.op("pe", lambda h, f=f, half=half, p=p, tt=tt: h.matmul(p[:, :], lhsT=uT[:, f, tt * 128:(tt + 1) * 128],
                                                                                       rhs=W2[:, f, half * 512:(half + 1) * 512],
                                                                                       start=(f == 0), stop=(f == 31)), reads=[uT.b, W2.b], writes=[p.b])
                            k.op("dve", lambda h, half=half, p=p, tm=tm: h.tensor_tensor(out=tm[:, half * 512:(half + 1) * 512], in0=p[:, :],
                                                                                       in1=G[:, half * 512:(half + 1) * 512], op=ALU.mult),
                                 reads=[p.b, G.b], writes=[tm.b])
                        x_ = xk[b % 2][tt]
                        k.op("pool", lambda h, tm=tm, x_=x_: h.tensor_tensor(out=tm[:], in0=tm[:], in1=x_[:], op=ALU.add),
                             reads=[tm.b, x_.b], writes=[tm.b])
                        k.dma("pool", xd_ap[t0:t0 + 128, :], tm[:], xd_buf, tm.b)
                k.barrier()

        def final_phase(xs_ap, xs_buf):
            with ExitStack() as st:
                fg = sb(st, "fg", [128, D])
                k.dma("sp", fg[:], bass.AP(tensor=final_g.tensor, offset=0, ap=[[0, 128], [1, D]]), fg.b, IN)
                xt = [sb(st, "xt%d" % i, [128, D]) for i in range(2)]
                junk = sb(st, "junk", [128, D])
                ssq = [sb(st, "ssq%d" % i, [128, 1], strict=True) for i in range(2)]
                yo = [sb(st, "yo%d" % i, [128, D]) for i in range(2)]
                for tt in range(NT):
                    i = tt % 2
                    t0 = tt * 128
                    x_, s_, y_ = xt[i], ssq[i], yo[i]
                    k.dma("sp", x_[:], xs_ap[t0:t0 + 128, :], x_.b, xs_buf)
                    k.op("act", lambda h, x_=x_, s_=s_: h.activation(out=junk[:], in_=x_[:], func=AF.Square, accum_out=s_[:]),
                         reads=[x_.b], writes=[junk.b, s_.b])
                    k.op("act", lambda h, s_=s_: h.activation(out=s_[:], in_=s_[:], func=AF.Sqrt, scale=1.0 / D, bias=epsc[:, 0:1]),
                         reads=[s_.b, epsc.b], writes=[s_.b])
                    k.op("dve", lambda h, s_=s_: h.reciprocal(out=s_[:], in_=s_[:]), reads=[s_.b], writes=[s_.b])
                    k.op("act", lambda h, x_=x_, s_=s_, y_=y_: h.activation(out=y_[:], in_=x_[:], func=AF.Copy, scale=s_[:, 0:1]),
                         reads=[x_.b, s_.b], writes=[y_.b])
                    k.op("pool", lambda h, y_=y_: h.tensor_tensor(out=y_[:], in0=y_[:], in1=fg[:], op=ALU.mult), reads=[y_.b, fg.b], writes=[y_.b])
                    k.dma("pool", y_out[t0:t0 + 128, :], y_[:], DB("y"), y_.b)
                k.barrier()

        cur_ap, cur_buf = x_in, IN
        cnt = {0: 0, 1: 0, 2: 0}
        for li, kind in enumerate(kinds):
            j = cnt[kind]
            cnt[kind] += 1
            phase1(li, cur_ap, cur_buf, kind, j)
            (attn_A, attn_B, attn_C)[kind](j)
            phase2b(li, kind, j, cur_ap, cur_buf, xA, DB("xA"))
            phase3(li, xA, DB("xA"), xB, DB("xB"))
            cur_ap, cur_buf = xB, DB("xB")
        final_phase(cur_ap, cur_buf)
    return nc


def host_tables(L, LT, is_prompt):
    t = np.arange(L)
    inv32 = (10000.0 ** (-np.arange(32, dtype=np.float32) / 32)).astype(np.float32)
    d = np.arange(128)
    angB = t[None, :].astype(np.float32) * inv32[d % 32][:, None]
    row, col = (t // 64).astype(np.float32), (t % 64).astype(np.float32)
    inv16 = (10000.0 ** (-np.arange(32, dtype=np.float32) / 32)).astype(np.float32)
    angC = np.where(((d // 32) % 2 == 0)[:, None], row[None, :] * inv16[d % 32][:, None], col[None, :] * inv16[d % 32][:, None]).astype(np.float32)
    out = {
        "cosB": np.cos(angB).astype(np.float32), "sinB": np.sin(angB).astype(np.float32),
        "cosC": np.cos(angC).astype(np.float32), "sinC": np.sin(angC).astype(np.float32),
        "ident": np.eye(128, dtype=np.float32),
    }
    p = np.arange(128)[:, None]
    i = np.arange(128)[None, :]
    out["bandm"] = np.concatenate([(p >= i), (p <= i)], axis=1).astype(np.float32)
    NQT = L // 128
    bb = np.zeros((128, 3, NQT, 2), np.float32)
    for g, dil in enumerate((1, 4, 16)):
        nmb = (L // dil) // 128
        for r in range(dil):
            for mb in range(nmb):
                for slot in range(2):
                    tok = dil * (128 * mb - 64 + 128 * slot + np.arange(128)) + r
                    bb[:, g, r * nmb + mb, slot] = np.where((tok >= 0) & (tok < LT), 0.0, NEG)
    out["bbias"] = bb.reshape(128, -1)
    NT = L // 128
    cb = np.zeros((128, NT), np.float32)
    cb[:, LT // 128:] = NEG
    out["cbias"] = cb
    fl = np.zeros((128, 2), np.float32)
    fl[:, 0] = 0.0 if is_prompt else 1.0
    fl[:, 1] = 1.0 if is_prompt else 0.0
    out["flags"] = fl
    return out


def a_bias_layout(a_rpb):
    n = a_rpb.shape[0]
    kc = np.arange(64)[:, None]
    qc = np.arange(64)[None, :]
    cs = np.clip(qc - 8, 0, 48)
    inwin = (kc >= cs) & (kc < cs + 16)
    dc = np.clip(kc - qc + 15, 0, 30)
    g = a_rpb[:, :, :, dc]
    g = np.where(inwin[None, None, None], g, np.float32(NEG)).astype(np.float32)
    g = np.transpose(g, (0, 1, 3, 2, 4)).reshape(n, 16, 64, 15 * 64)
    return np.ascontiguousarray(g)


_CACHE = {}


def run_slots(slots, weights, L, LP, kinds, n_cores):
    key = (L, LP, tuple(kinds))
    if key not in _CACHE:
        _CACHE[key] = build({"L": L, "LP": LP, "kinds": list(kinds)})
    nc = _CACHE[key]
    common = dict(weights)
    common["a_biasT"] = a_bias_layout(np.asarray(weights["a_rpb"], np.float32))
    del common["a_rpb"]
    pidx = np.arange(128)
    i1 = ((pidx // 32) % 2) * 64 + (pidx % 32)
    qg, kg = common.pop("c_q_g"), common.pop("c_k_g")
    common["c_gcol"] = np.ascontiguousarray(np.stack([qg[:, i1], qg[:, i1 + 32], kg[:, i1], kg[:, i1 + 32]], axis=-1).astype(np.float32))
    in_maps = []
    for (x, c, isp) in slots:
        LT = x.shape[0]
        xs = np.zeros((L, D), np.float32)
        xs[:LT] = x
        m = dict(common)
        m["x"] = xs
        m["cT"] = np.ascontiguousarray(np.asarray(c, np.float32).reshape(8, 128).T)
        m.update(host_tables(L, LT, isp))
        in_maps.append(m)
    while len(in_maps) < n_cores:
        in_maps.append(in_maps[-1])
    res = run_bass_kernel_spmd(nc, in_maps, core_ids=list(range(n_cores)))
    return [r["y"] for r in res.results]


def kernel(x_prompt, x_sample, c_prompt, c_sample, w_mod, b_mod, norm_g, final_g,
           a_w_qkv, a_rpb, a_w_o, b_w_qkv, b_w_o, c_w_qkv, c_q_g, c_k_g, c_w_o, mlp_w1, mlp_w2):
    f = lambda a: np.ascontiguousarray(np.asarray(a, np.float32))
    weights = dict(w_mod=f(w_mod), b_mod=f(b_mod), norm_g=f(norm_g), final_g=f(final_g), a_w_qkv=f(a_w_qkv),
                   a_rpb=f(a_rpb), a_w_o=f(a_w_o), b_w_qkv=f(b_w_qkv), b_w_o=f(b_w_o), c_w_qkv=f(c_w_qkv),
                   c_q_g=f(c_q_g), c_k_g=f(c_k_g), c_w_o=f(c_w_o), mlp_w1=f(mlp_w1), mlp_w2=f(mlp_w2))
    x_prompt, x_sample = f(x_prompt), f(x_sample)
    c_prompt, c_sample = f(c_prompt), f(c_sample)
    L = x_sample.shape[1]
    LP = x_prompt.shape[1]
    slots = [(x_sample[b], c_sample[b], False) for b in range(x_sample.shape[0])]
    slots += [(x_prompt[b], c_prompt[b], True) for b in range(x_prompt.shape[0])]
    ys = run_slots(slots, weights, L, LP, [0, 1, 2, 0], 8)
    ns = x_sample.shape[0]
    y_sample = np.stack([ys[b] for b in range(ns)]).astype(np.float32)
    y_prompt = np.stack([ys[ns + b][:LP] for b in range(x_prompt.shape[0])]).astype(np.float32)
    return (y_prompt, y_sample)
```

```python
import numpy as np
from contextlib import ExitStack
import concourse.bass as bass
import concourse.mybir as mybir
from concourse.bass_utils import run_bass_kernel_spmd

F32 = mybir.dt.float32
BF16 = mybir.dt.bfloat16
AF = mybir.ActivationFunctionType
ALU = mybir.AluOpType
D = 1024
DFF = 4096
NEG = -30000.0
EPS = 1e-6
PADK = 1024


class Sem:
    def __init__(self, h):
        self.h = h
        self.total = 0


class Eng:
    def __init__(self, name, h, sem):
        self.name, self.h, self.sem, self.n, self.waited = name, h, sem, 0, {}


class Buf:
    def __init__(self, name, dram=False):
        self.name, self.dram = name, dram
        self.w = None
        self.rd = {}
        self.dw = set()
        self.dr = set()
        self.sem = None
        self.strict = False


class K:
    def __init__(self, nc, stack, n_dma_sems=90):
        self.nc = nc
        self.engs = {}
        for name, h in (("pe", nc.tensor), ("act", nc.scalar), ("dve", nc.vector),
                        ("pool", nc.gpsimd), ("sp", nc.sync)):
            s = Sem(stack.enter_context(nc.semaphore("s_" + name)))
            self.engs[name] = Eng(name, h, s)
        self.dsems = [Sem(stack.enter_context(nc.semaphore("d%d" % i))) for i in range(n_dma_sems)]
        self.next_ds = 0

    def _get_sem(self, b):
        if b.sem is None:
            b.sem = self.dsems[self.next_ds % len(self.dsems)]
            self.next_ds += 1
        return b.sem

    def _emit_waits(self, E, need):
        for sem, val in need.items():
            if E.waited.get(sem, 0) < val:
                E.h.wait_ge(sem.h, val)
                E.waited[sem] = val

    def op(self, e, fn, reads=(), writes=()):
        E = self.engs[e]
        need = {}

        def add(sem, val):
            if need.get(sem, 0) < val:
                need[sem] = val
        for b in reads:
            if b.w is not None and (b.w[0] is not E or b.strict or E.name != "pe"):
                add(b.w[0].sem, b.w[1])
            for s in b.dw:
                add(s, s.total)
        for b in writes:
            if b.w is not None and (b.w[0] is not E or b.strict):
                add(b.w[0].sem, b.w[1])
            for s in b.dw:
                add(s, s.total)
        for b in writes:
            for F, n in b.rd.items():
                if F is not E or b.strict:
                    add(F.sem, n)
            for s in b.dr:
                add(s, s.total)
        self._emit_waits(E, need)
        inst = fn(E.h)
        inst.then_inc(E.sem.h, 1)
        E.n += 1
        E.sem.total = E.n
        for b in writes:
            b.w = (E, E.n)
            b.rd = {}
            b.dw = set()
            b.dr = set()
        for b in reads:
            if b not in writes:
                b.rd[E] = E.n

    def dma(self, q, out, in_, dst, src, **kw):
        Q = self.engs[q]
        need = {}

        def add(sem, val):
            if need.get(sem, 0) < val:
                need[sem] = val
        if src.w is not None:
            add(src.w[0].sem, src.w[1])
        for s in src.dw:
            add(s, s.total)
        if not dst.dram:
            if dst.w is not None:
                add(dst.w[0].sem, dst.w[1])
            for s in dst.dw:
                add(s, s.total)
            for F, n in dst.rd.items():
                add(F.sem, n)
            for s in dst.dr:
                add(s, s.total)
        self._emit_waits(Q, need)
        sem = self._get_sem(src if dst.dram else dst)
        inst = Q.h.dma_start(out=out, in_=in_, **kw)
        inst.then_inc(sem.h, 16)
        sem.total += 16
        if dst.dram:
            dst.dw.add(sem)
        else:
            dst.w = None
            dst.rd = {}
            dst.dr = set()
            dst.dw = {sem}
        if not src.dram:
            src.dr.add(sem)

    def barrier(self):
        for E in self.engs.values():
            need = {}
            for Fe in self.engs.values():
                if Fe is not E and Fe.n > 0:
                    need[Fe.sem] = Fe.n
            for s in self.dsems:
                if s.total > 0:
                    need[s] = s.total
            self._emit_waits(E, need)


class T:
    def __init__(self, h, name, strict=False):
        self.h = h
        self.b = Buf(name)
        self.b.strict = strict

    def __getitem__(self, k):
        return self.h[k]


def build(cfg):
    L = cfg["L"]
    LP = cfg["LP"]
    kinds = cfg["kinds"]
    NL = len(kinds)
    R = L // 64
    RB = LP // 64
    NT = L // 128
    NB = L // 512
    nc = bass.Bass("TRN2", target_bir_lowering=False)

    def din(name, shape, dt=F32):
        return nc.dram_tensor(name, list(shape), dt, kind="ExternalInput").ap()

    def dsc(name, shape, dt):
        return nc.dram_tensor(name, list(shape), dt, kind="Internal").ap()

    nA = sum(1 for k in kinds if k == 0)
    nB = sum(1 for k in kinds if k == 1)
    nC = sum(1 for k in kinds if k == 2)
    x_in = din("x", [L, D])
    cT = din("cT", [128, 8])
    w_mod = din("w_mod", [NL, D, 6 * D])
    b_mod = din("b_mod", [NL, 6 * D])
    norm_g = din("norm_g", [NL, 2, D])
    final_g = din("final_g", [D])
    a_w_qkv = din("a_w_qkv", [max(nA, 1), D, 3 * D])
    a_biasT = din("a_biasT", [max(nA, 1), 16, 64, 15 * 64])
    a_w_o = din("a_w_o", [max(nA, 1), D, D])
    b_w_qkv = din("b_w_qkv", [max(nB, 1), D, 9 * D])
    b_w_o = din("b_w_o", [max(nB, 1), D, D])
    c_w_qkv = din("c_w_qkv", [max(nC, 1), D, 1536])
    c_gcol = din("c_gcol", [max(nC, 1), 128, 4])
    c_w_o = din("c_w_o", [max(nC, 1), D, D])
    mlp_w1 = din("mlp_w1", [NL, D, DFF])
    mlp_w2 = din("mlp_w2", [NL, DFF, D])
    ident_in = din("ident", [128, 128])
    cosB = din("cosB", [128, L])
    sinB = din("sinB", [128, L])
    cosC = din("cosC", [128, L])
    sinC = din("sinC", [128, L])
    bandm = din("bandm", [128, 256])
    NQT = L // 128
    bbias = din("bbias", [128, 3 * NQT * 2])
    cbias = din("cbias", [128, NT])
    flags = din("flags", [128, 2])
    y_out = nc.dram_tensor("y", [L, D], F32, kind="ExternalOutput").ap()

    xA = dsc("xA", [L, D], F32)
    xB = dsc("xB", [L, D], F32)
    modD = dsc("modD", [NL, 6, D], F32)
    QT = [dsc("QT%d" % g, [D, L], BF16) for g in range(3)]
    KT = [dsc("KT%d" % g, [D, L + 2 * PADK], BF16) for g in range(3)]
    VX = [dsc("VX%d" % g, [L + 2 * PADK, 1040], BF16) for g in range(3)]
    ON = [dsc("ON%d" % g, [L, 1040], F32) for g in range(3)]
    VXC = dsc("VXC", [L, 258], BF16)
    ONC = dsc("ONC", [L, 1032], F32)

    with ExitStack() as top:
        k = K(nc, top)
        dbuf = {}

        def DB(ap_name):
            if ap_name not in dbuf:
                dbuf[ap_name] = Buf(ap_name, dram=True)
            return dbuf[ap_name]
        IN = DB("inputs")

        uid = [0]

        def sb(stack, name, shape, dt=F32, strict=False):
            uid[0] += 1
            nm = "t%d_%s" % (uid[0], name)
            return T(stack.enter_context(nc.sbuf_tensor(nm, list(shape), dt)), nm, strict)

        ps = [T(top.enter_context(nc.psum_tensor("ps%d" % i, [128, 512], F32)), "ps%d" % i) for i in range(8)]
        ident = sb(top, "ident", [128, 128])
        k.dma("sp", ident[:], ident_in[:, :], ident.b, IN)
        ones_f = sb(top, "ones_f", [128, 128])
        k.op("dve", lambda h: h.memset(ones_f[:], 1.0), writes=[ones_f.b])
        epsc = sb(top, "epsc", [128, 1])
        k.op("dve", lambda h: h.memset(epsc[:], EPS), writes=[epsc.b])
        flg = sb(top, "flg", [128, 2])
        k.dma("sp", flg[:], flags[:, :], flg.b, IN)

        with ExitStack() as st:
            zt = sb(st, "zt", [128, 1040], BF16)
            k.op("dve", lambda h: h.memset(zt[:], 0.0), writes=[zt.b])
            for g in range(3):
                for side in range(2):
                    c0 = side * (PADK + L)
                    for ch in range(8):
                        k.dma("pool", KT[g][ch * 128:(ch + 1) * 128, c0:c0 + PADK], zt[:, 0:PADK],
                              DB("KT%d" % g), zt.b)
                        k.dma("pool", VX[g][c0 + ch * 128:c0 + (ch + 1) * 128, :], zt[:, :],
                              DB("VX%d" % g), zt.b)
            sc = sb(st, "sc", [128, 8], strict=True)
            k.dma("sp", sc[:], cT[:, :], sc.b, IN)
            k.op("act", lambda h: h.activation(out=sc[:], in_=sc[:], func=AF.Silu), reads=[sc.b], writes=[sc.b])
            wm = [sb(st, "wm%d" % i, [128, 8, 512]) for i in range(2)]
            modrow = sb(st, "modrow", [1, 6 * D])
            brow = sb(st, "brow", [1, 6 * D])
            grow = sb(st, "grow", [1, 2 * D])
            outrow = sb(st, "outrow", [1, 6 * D])
            for li in range(NL):
                k.dma("sp", brow[:], b_mod[li:li + 1, :], brow.b, IN)
                k.dma("sp", grow[:], norm_g[li:li + 1].rearrange("o t d -> o (t d)"), grow.b, IN)
                for n in range(12):
                    w = wm[n % 2]
                    k.dma("sp", w[:], w_mod[li].rearrange("(k p) n -> p k n", p=128)[:, :, n * 512:(n + 1) * 512], w.b, IN)
                    p = ps[n % 2]
                    for kk in range(8):
                        k.op("pe", lambda h, kk=kk, w=w, p=p: h.matmul(p[0:1, :], lhsT=sc[:, kk:kk + 1], rhs=w[:, kk, :],
                                                                     start=(kk == 0), stop=(kk == 7)),
                             reads=[sc.b, w.b], writes=[p.b])
                    k.op("dve", lambda h, n=n, p=p: h.tensor_tensor(out=modrow[0:1, n * 512:(n + 1) * 512], in0=p[0:1, :],
                                                                  in1=brow[0:1, n * 512:(n + 1) * 512], op=ALU.add),
                         reads=[p.b, brow.b], writes=[modrow.b])
                for sub in range(2):
                    o = sub * 3 * D
                    k.op("dve", lambda h, o=o, sub=sub: h.scalar_tensor_tensor(
                        out=outrow[0:1, o:o + D], in0=modrow[0:1, o + D:o + 2 * D], scalar=1.0,
                        in1=grow[0:1, sub * D:(sub + 1) * D], op0=ALU.add, op1=ALU.mult),
                        reads=[modrow.b, grow.b], writes=[outrow.b])
                    k.op("dve", lambda h, o=o: h.tensor_copy(out=outrow[0:1, o + D:o + 2 * D], in_=modrow[0:1, o:o + D]),
                         reads=[modrow.b], writes=[outrow.b])
                    k.op("dve", lambda h, o=o: h.tensor_copy(out=outrow[0:1, o + 2 * D:o + 3 * D], in_=modrow[0:1, o + 2 * D:o + 3 * D]),
                         reads=[modrow.b], writes=[outrow.b])
                k.dma("pool", modD[li:li + 1].rearrange("o s d -> o (s d)"), outrow[:], DB("modD"), outrow.b)
            k.barrier()

        def load_cols(t, li, kind):
            src = bass.AP(tensor=modD.tensor, offset=(li * 6 + kind) * D, ap=[[1, 128], [128, 8]])
            k.dma("sp", t[:], src, t.b, DB("modD"), allow_slow_non_contiguous=True)

        def load_bcast(t, li, kind):
            src = bass.AP(tensor=modD.tensor, offset=(li * 6 + kind) * D, ap=[[0, 128], [1, D]])
            k.dma("sp", t[:], src, t.b, DB("modD"))

        class Norm:
            def __init__(self, st, li, sub, alloc_xt=True):
                self.xt = [sb(st, "xt%d" % i, [128, D]) for i in range(2)] if alloc_xt else None
                self.junk = sb(st, "junk", [128, D], BF16)
                self.xn = [sb(st, "xn%d" % i, [128, D]) for i in range(2)]
                self.ssq = [sb(st, "ssq%d" % i, [128, 1], strict=True) for i in range(2)]
                self.Ac = sb(st, "Ac", [128, 8])
                self.Bc = sb(st, "Bc", [128, 8])
                load_cols(self.Ac, li, sub * 3 + 0)
                load_cols(self.Bc, li, sub * 3 + 1)
                self.i = 0

            def rstd_of(self, xt, ssq, n_el):
                junk = self.junk
                k.op("act", lambda h: h.activation(out=junk[:], in_=xt[:], func=AF.Square, accum_out=ssq[:]),
                     reads=[xt.b], writes=[junk.b, ssq.b])
                k.op("act", lambda h: h.activation(out=ssq[:], in_=ssq[:], func=AF.Sqrt, scale=1.0 / n_el, bias=epsc[:, 0:1]),
                     reads=[ssq.b, epsc.b], writes=[ssq.b])
                k.op("dve", lambda h: h.reciprocal(out=ssq[:], in_=ssq[:]), reads=[ssq.b], writes=[ssq.b])

            def run(self, xsrc_ap, xsrc_buf, hT, col0, keep=None):
                i = self.i
                self.i += 1
                xt = keep if keep is not None else self.xt[i % 2]
                xn, ssq = self.xn[i % 2], self.ssq[i % 2]
                k.dma("sp", xt[:], xsrc_ap, xt.b, xsrc_buf)
                self.rstd_of(xt, ssq, D)
                k.op("act", lambda h: h.activation(out=xn[:], in_=xt[:], func=AF.Copy, scale=ssq[:, 0:1]),
                     reads=[xt.b, ssq.b], writes=[xn.b])
                for half in range(2):
                    p = ps[6 + half]
                    for j in range(4):
                        kk = half * 4 + j
                        k.op("pe", lambda h, kk=kk, j=j, p=p: h.transpose(p[:, j * 128:(j + 1) * 128], xn[:, kk * 128:(kk + 1) * 128], ident[:]),
                             reads=[xn.b, ident.b], writes=[p.b])
                    for j in range(4):
                        kk = half * 4 + j
                        k.op("act", lambda h, kk=kk, j=j, p=p: h.activation(
                            out=hT[:, kk, col0:col0 + 128], in_=p[:, j * 128:(j + 1) * 128], func=AF.Identity,
                            scale=self.Ac[:, kk:kk + 1], bias=self.Bc[:, kk:kk + 1]),
                            reads=[p.b, self.Ac.b, self.Bc.b], writes=[hT.b])

        def load_w_bf16(t, w_ap, ncols, kchunks=8, piece=1024):
            v = w_ap.rearrange("(k p) n -> p k n", p=128)
            for c0 in range(0, ncols, piece):
                c1 = min(ncols, c0 + piece)
                for k0 in range(0, kchunks, 8):
                    k.dma("pool", t[:, k0:k0 + 8, c0:c1], v[:, k0:k0 + 8, c0:c1], t.b, IN)

        def phase1(li, x_ap, x_buf, kind, j):
            groups = 3 if kind == 1 else 1
            for g in range(groups):
                with ExitStack() as st:
                    if kind == 0:
                        wsrc, nq, nk, nv, hd = a_w_qkv[j], 1024, 1024, 1024, 64
                    elif kind == 1:
                        wsrc, nq, nk, nv, hd = b_w_qkv[j][:, g * 3072:(g + 1) * 3072], 1024, 1024, 1024, 64
                    else:
                        wsrc, nq, nk, nv, hd = c_w_qkv[j], 1024, 256, 256, 128
                    ncol = nq + nk + nv
                    nqk = nq + nk
                    W = sb(st, "W", [128, 8, ncol], BF16)
                    load_w_bf16(W, wsrc, ncol)
                    rope = kind != 0
                    if rope:
                        Wp = sb(st, "Wp", [128, 8, nqk], BF16)
                        for kk in range(8):
                            for (base, ncs) in ((0, nq), (nq, nk)):
                                gi = ncs // 256
                                vi = W[:, kk, base:base + ncs].rearrange("p (g h t j) -> p g h t j", g=gi, h=4, t=2, j=32)
                                vo = Wp[:, kk, base:base + ncs].rearrange("p (g t h j) -> p g t h j", g=gi, h=4, t=2, j=32)
                                for t in range(2):
                                    k.op("dve" if t == 0 else "pool", lambda h, vi=vi, vo=vo, t=t: h.tensor_copy(out=vo[:, :, t, :, :], in_=vi[:, :, :, t, :]),
                                         reads=[W.b], writes=[Wp.b])
                        if kind == 2:
                            gcol = sb(st, "gcol", [128, 4])
                            k.dma("sp", gcol[:], c_gcol[j], gcol.b, IN)
                            Mblk = sb(st, "Mblk", [128, 128])
                            k.op("dve", lambda h: h.memset(Mblk[:], 0.0), writes=[Mblk.b])
                            k.op("dve", lambda h: h.memset(Mblk[0:64, 0:64], 1.0), writes=[Mblk.b])
                            k.op("dve", lambda h: h.memset(Mblk[64:128, 64:128], 1.0), writes=[Mblk.b])
                    nrm = Norm(st, li, 0)
                    hT = [sb(st, "hT%d" % i, [128, 8, 512], BF16) for i in range(2)]
                    qst = [sb(st, "qst%d" % i, [128, 512], BF16) for i in range(4)]
                    H = 16 if kind != 2 else 2
                    vst = [sb(st, "vst%d" % i, [128, H, hd + 1], BF16) for i in range(2)]
                    for v in vst:
                        k.op("dve", lambda h, v=v: h.memset(v[:], 1.0), writes=[v.b])
                    if rope:
                        cs = [sb(st, "cs%d" % i, [128, 512]) for i in range(2)]
                        sn = [sb(st, "sn%d" % i, [128, 512]) for i in range(2)]
                        tA = [sb(st, "tA%d" % i, [128, 512]) for i in range(2)]
                        tB = [sb(st, "tB%d" % i, [128, 512]) for i in range(2)]
                        tC = [sb(st, "tC%d" % i, [128, 512]) for i in range(2)]
                        tD = [sb(st, "tD%d" % i, [128, 512]) for i in range(2)]
                        if kind == 2:
                            tabs = [[sb(st, "tab%d_%d" % (i, n_), [128, 512]) for n_ in range(8)] for i in range(2)]
                            sqa = [sb(st, "sqa%d" % i, [128, 512]) for i in range(2)]
                            sqb = [sb(st, "sqb%d" % i, [128, 512]) for i in range(2)]
                            rs = [sb(st, "rs%d" % i, [128, 512]) for i in range(2)]
                    qi = 0
                    vi_ = 0
                    KTg, QTg = KT[g], QT[g]
                    def norm_block(b):
                        for tt in range(4):
                            t0 = b * 512 + tt * 128
                            nrm.run(x_ap[t0:t0 + 128, :], x_buf, hT[b % 2], tt * 128)
                    norm_block(0)
                    for b in range(NB):
                        h_ = hT[b % 2]
                        if not rope:
                            for c in range(nqk // 128):
                                isq = c < nq // 128
                                pq = ps[qi % 4]
                                for kk in range(8):
                                    k.op("pe", lambda h, kk=kk, c=c, pq=pq: h.matmul(pq[:, :], lhsT=W[:, kk, c * 128:(c + 1) * 128], rhs=h_[:, kk, :],
                                                                                   start=(kk == 0), stop=(kk == 7)),
                                         reads=[W.b, h_.b], writes=[pq.b])
                                q_ = qst[qi % 4]
                                if qi % 2:
                                    k.op("act", lambda h, q_=q_, pq=pq: h.copy(out=q_[:], in_=pq[:, :]), reads=[pq.b], writes=[q_.b])
                                else:
                                    k.op("dve", lambda h, q_=q_, pq=pq: h.tensor_copy(out=q_[:], in_=pq[:, :]), reads=[pq.b], writes=[q_.b])
                                if isq:
                                    k.dma("pool", QTg[c * 128:(c + 1) * 128, b * 512:(b + 1) * 512], q_[:], DB("QT%d" % g), q_.b)
                                else:
                                    ck = c - nq // 128
                                    k.dma("pool", KTg[ck * 128:(ck + 1) * 128, PADK + b * 512:PADK + (b + 1) * 512], q_[:], DB("KT%d" % g), q_.b)
                                qi += 1
                        else:
                            c_, s_ = cs[b % 2], sn[b % 2]
                            ctab, stab = (cosB, sinB) if kind == 1 else (cosC, sinC)
                            k.dma("sp", c_[:], ctab[:, b * 512:(b + 1) * 512], c_.b, IN)
                            k.dma("sp", s_[:], stab[:, b * 512:(b + 1) * 512], s_.b, IN)
                            if kind == 2:
                                tb_ = tabs[b % 2]
                                spec = [(c_, 0), (s_, 1), (s_, 0), (c_, 1), (c_, 2), (s_, 3), (s_, 2), (c_, 3)]
                                for n_, (src_, gc) in enumerate(spec):
                                    k.op("pool" if n_ % 2 else "dve", lambda h, n_=n_, src_=src_, gc=gc: h.tensor_scalar(
                                        out=tb_[n_][:], in0=src_[:], scalar1=gcol[:, gc:gc + 1], scalar2=None, op0=ALU.mult),
                                        reads=[src_.b, gcol.b], writes=[tb_[n_].b])
                            for pi in range(nqk // 256):
                                isq = pi < nq // 256
                                pl = pi if isq else pi - nq // 256
                                cb0 = pi * 256
                                pA = ps[(qi % 2) * 2]
                                pB = ps[(qi % 2) * 2 + 1]
                                for (pp, off) in ((pA, 0), (pB, 128)):
                                    for kk in range(8):
                                        k.op("pe", lambda h, kk=kk, pp=pp, off=off: h.matmul(pp[:, :], lhsT=Wp[:, kk, cb0 + off:cb0 + off + 128], rhs=h_[:, kk, :],
                                                                                            start=(kk == 0), stop=(kk == 7)),
                                             reads=[Wp.b, h_.b], writes=[pp.b])
                                a1, a2, a3, a4 = tA[qi % 2], tB[qi % 2], tC[qi % 2], tD[qi % 2]
                                q1, q2 = qst[(2 * qi) % 4], qst[(2 * qi + 1) % 4]
                                if kind == 1:
                                    m1, m2, m3, m4 = c_, s_, s_, c_
                                else:
                                    o8 = 0 if isq else 4
                                    m1, m2, m3, m4 = tb_[o8 + 0], tb_[o8 + 1], tb_[o8 + 2], tb_[o8 + 3]
                                    s2a, s2b, r2 = sqa[qi % 2], sqb[qi % 2], rs[qi % 2]
                                    k.op("act", lambda h: h.activation(out=s2a[:], in_=pA[:, :], func=AF.Square), reads=[pA.b], writes=[s2a.b])
                                    k.op("act", lambda h: h.activation(out=s2b[:], in_=pB[:, :], func=AF.Square), reads=[pB.b], writes=[s2b.b])
                                    pss = ps[4]
                                    k.op("pe", lambda h: h.matmul(pss[:, :], lhsT=Mblk[:], rhs=s2a[:], start=True, stop=False),
                                         reads=[Mblk.b, s2a.b], writes=[pss.b])
                                    k.op("pe", lambda h: h.matmul(pss[:, :], lhsT=Mblk[:], rhs=s2b[:], start=False, stop=True),
                                         reads=[Mblk.b, s2b.b], writes=[pss.b])
                                    k.op("act", lambda h: h.activation(out=r2[:], in_=pss[:, :], func=AF.Sqrt, scale=1.0 / 128, bias=epsc[:, 0:1]),
                                         reads=[pss.b, epsc.b], writes=[r2.b])
                                    k.op("dve", lambda h: h.reciprocal(out=r2[:], in_=r2[:]), reads=[r2.b], writes=[r2.b])
                                k.op("dve", lambda h: h.tensor_tensor(out=a1[:], in0=pA[:, :], in1=m1[:], op=ALU.mult), reads=[pA.b, m1.b], writes=[a1.b])
                                k.op("dve", lambda h: h.tensor_tensor(out=a2[:], in0=pB[:, :], in1=m2[:], op=ALU.mult), reads=[pB.b, m2.b], writes=[a2.b])
                                k.op("dve", lambda h: h.tensor_tensor(out=a3[:], in0=pA[:, :], in1=m3[:], op=ALU.mult), reads=[pA.b, m3.b], writes=[a3.b])
                                k.op("dve", lambda h: h.tensor_tensor(out=a4[:], in0=pB[:, :], in1=m4[:], op=ALU.mult), reads=[pB.b, m4.b], writes=[a4.b])
                                if kind == 1:
                                    k.op("dve", lambda h: h.tensor_tensor(out=q1[:], in0=a1[:], in1=a2[:], op=ALU.subtract), reads=[a1.b, a2.b], writes=[q1.b])
                                    k.op("pool", lambda h: h.tensor_tensor(out=q2[:], in0=a3[:], in1=a4[:], op=ALU.add), reads=[a3.b, a4.b], writes=[q2.b])
                                else:
                                    k.op("dve", lambda h: h.tensor_tensor(out=a1[:], in0=a1[:], in1=a2[:], op=ALU.subtract), reads=[a1.b, a2.b], writes=[a1.b])
                                    k.op("pool", lambda h: h.tensor_tensor(out=q1[:], in0=a1[:], in1=r2[:], op=ALU.mult), reads=[a1.b, r2.b], writes=[q1.b])
                                    k.op("dve", lambda h: h.tensor_tensor(out=a3[:], in0=a3[:], in1=a4[:], op=ALU.add), reads=[a3.b, a4.b], writes=[a3.b])
                                    k.op("pool", lambda h: h.tensor_tensor(out=q2[:], in0=a3[:], in1=r2[:], op=ALU.mult), reads=[a3.b, r2.b], writes=[q2.b])
                                for (qq, off) in ((q1, 0), (q2, 128)):
                                    r0_ = pl * 256 + off
                                    if isq:
                                        k.dma("pool", QTg[r0_:r0_ + 128, b * 512:(b + 1) * 512], qq[:], DB("QT%d" % g), qq.b)
                                    else:
                                        k.dma("pool", KTg[r0_:r0_ + 128, PADK + b * 512:PADK + (b + 1) * 512], qq[:], DB("KT%d" % g), qq.b)
                                qi += 1
                        if b + 1 < NB:
                            norm_block(b + 1)
                        for tt in range(4):
                            v_ = vst[vi_ % 2]
                            vi_ += 1
                            t0 = b * 512 + tt * 128
                            for cc in range(max(1, nv // 512)):
                                w_ = min(512, nv)
                                pv = ps[5]
                                for kk in range(8):
                                    k.op("pe", lambda h, kk=kk, cc=cc, pv=pv, w_=w_: h.matmul(
                                        pv[:, 0:w_], lhsT=h_[:, kk, tt * 128:(tt + 1) * 128], rhs=W[:, kk, nqk + cc * 512:nqk + cc * 512 + w_],
                                        start=(kk == 0), stop=(kk == 7)), reads=[W.b, h_.b], writes=[pv.b])
                                nh = w_ // hd
                                k.op("act", lambda h, v_=v_, pv=pv, cc=cc, nh=nh, w_=w_: h.copy(
                                    out=v_[:, cc * nh:(cc + 1) * nh, 0:hd], in_=pv[:, 0:w_].rearrange("p (h d) -> p h d", d=hd)),
                                    reads=[pv.b], writes=[v_.b])
                            if kind == 2:
                                k.dma("pool", VXC[t0:t0 + 128, :], v_[:].rearrange("p h e -> p (h e)"), DB("VXC"), v_.b)
                            else:
                                k.dma("pool", VX[g][PADK + t0:PADK + t0 + 128, :], v_[:].rearrange("p h e -> p (h e)"), DB("VX%d" % g), v_.b)
                    k.barrier()

        def load_split(t, src, cols, blocks, dbname):
            for i, blk in enumerate(blocks):
                pi, b4 = blk // 4, blk % 4
                for two in range(2):
                    r0_ = pi * 256 + two * 128 + b4 * 32
                    k.dma("sp", t[i * 64 + two * 32:i * 64 + two * 32 + 32, :], src[r0_:r0_ + 32, cols], t.b, DB(dbname))

        def pipeline(items, stage1, stage2, depth):
            n = len(items)
            for i in range(min(depth, n)):
                stage1(items[i])
            for i in range(n):
                if i + depth < n:
                    stage1(items[i + depth])
                stage2(items[i])

        def attn_A(j):
            scale = 64 ** -0.5
            R2 = R // 2
            with ExitStack() as st:
                nbuf = 2 if L <= 4096 else 1
                KTc = [sb(st, "KTc%d" % i, [128, L], BF16) for i in range(nbuf)] * (2 // nbuf)
                QTc = [sb(st, "QTc%d" % i, [128, L], BF16) for i in range(nbuf)] * (2 // nbuf)
                VpE = [sb(st, "VpE%d" % i, [128, R2, 130], BF16) for i in range(nbuf)] * (2 // nbuf)
                VpO = [sb(st, "VpO%d" % i, [128, R2, 130], BF16) for i in range(nbuf)] * (2 // nbuf)
                bt = sb(st, "bt", [128, 2, 14 * 64])
                Et = [sb(st, "Et%d" % i, [128, 2, 14 * 64], BF16) for i in range(2)]
                pt = [sb(st, "pt%d" % i, [128, 6 * 64], BF16) for i in range(4)]
                ost = [sb(st, "ost%d" % i, [64, 2, 65]) for i in range(3)]
                itc = [0]
                for pr in range(8):
                    kt_, qt_, ve_, vo_, et_ = KTc[pr % 2], QTc[pr % 2], VpE[pr % 2], VpO[pr % 2], Et[pr % 2]
                    k.dma("sp", kt_[:], KT[0][pr * 128:(pr + 1) * 128, PADK:PADK + L], kt_.b, DB("KT0"))
                    k.dma("sp", qt_[:], QT[0][pr * 128:(pr + 1) * 128, :], qt_.b, DB("QT0"))
                    vsE = VX[0][PADK:PADK + L, pr * 130:(pr + 1) * 130].rearrange("(i p) e -> p i e", p=128)
                    vsO = VX[0][PADK + 64:PADK + 64 + L, pr * 130:(pr + 1) * 130].rearrange("(i p) e -> p i e", p=128)
                    for r0 in range(0, R2, 8):
                        k.dma("sp", ve_[:, r0:r0 + 8, :], vsE[:, r0:r0 + 8, :], ve_.b, DB("VX0"))
                        k.dma("sp", vo_[:, r0:r0 + 8, :], vsO[:, r0:r0 + 8, :], vo_.b, DB("VX0"))
                    for hh in range(2):
                        k.dma("sp", bt[0:64, hh, :], a_biasT[j, pr * 2 + hh][:, 0:14 * 64], bt.b, IN)
                        k.dma("sp", bt[64:128, hh, :], a_biasT[j, pr * 2 + hh][:, 64:15 * 64], bt.b, IN)
                    k.op("act", lambda h, et_=et_: h.activation(out=et_[:], in_=bt[:], func=AF.Exp), reads=[bt.b], writes=[et_.b])
                    items = []
                    for r in range(R):
                        rs0 = min(max(r - 4, 0), R - 8)
                        S = list(range(rs0, rs0 + 8))
                        tags = {kr: 0 for kr in S}
                        if RB < R and RB - 3 <= r <= RB - 1:
                            P = list(range(RB - 8, RB))
                            for kr in S:
                                if kr not in P:
                                    tags[kr] = 1
                            for kr in P:
                                if kr not in tags:
                                    tags[kr] = 2
                        krs = sorted(tags)
                        n = len(krs)
                        assert krs == list(range(krs[0], krs[0] + n)) and n <= 11
                        if n % 2:
                            tags[krs[-1] + 1] = 3
                            krs = krs + [krs[-1] + 1]
                            n += 1
                            assert krs[-1] < R
                        npair = n // 2
                        dr0 = krs[0] - r + 7
                        assert 0 <= dr0 and dr0 + 2 * (npair - 1) <= 13
                        for hh in range(2):
                            it = itc[0]
                            itc[0] += 1
                            items.append(dict(r=r, hh=hh, krs=krs, npair=npair, dr0=dr0, tags=tags, p_s=ps[it % 3],
                                              p_o=ps[3 + it % 2], p_=pt[it % 4], o_=ost[(it // 2) % 3]))

                    def stage1(d):
                        r, hh, krs, npair, dr0, tags, p_s, p_ = d["r"], d["hh"], d["krs"], d["npair"], d["dr0"], d["tags"], d["p_s"], d["p_"]
                        for i in range(npair):
                            a_ = krs[2 * i]
                            k.op("pe", lambda h: h.matmul(
                                p_s[:, i * 64:(i + 1) * 64], lhsT=kt_[hh * 64:(hh + 1) * 64, a_ * 64:a_ * 64 + 128],
                                rhs=qt_[hh * 64:(hh + 1) * 64, r * 64:(r + 1) * 64], start=True, stop=True),
                                reads=[kt_.b, qt_.b], writes=[p_s.b])
                        k.op("act", lambda h: h.activation(out=p_[:, 0:npair * 64], in_=p_s[:, 0:npair * 64], func=AF.Exp, scale=scale),
                             reads=[p_s.b], writes=[p_.b])
                        ev = et_[:, hh, :].rearrange("p (d q) -> p d q", q=64)[:, dr0:dr0 + 2 * (npair - 1) + 1:2, :]
                        k.op("dve", lambda h: h.tensor_tensor(
                            out=p_[:, 0:npair * 64].rearrange("p (i q) -> p i q", q=64), in0=p_[:, 0:npair * 64].rearrange("p (i q) -> p i q", q=64),
                            in1=ev, op=ALU.mult), reads=[p_.b, et_.b], writes=[p_.b])
                        for jj, kr in enumerate(krs):
                            tg = tags[kr]
                            if tg:
                                i, half = jj // 2, jj % 2
                                blk = p_[half * 64:(half + 1) * 64, i * 64:(i + 1) * 64]
                                if tg == 3:
                                    k.op("dve", lambda h: h.memset(blk, 0.0), writes=[p_.b])
                                else:
                                    fc = flg[half * 64:(half + 1) * 64, tg - 1:tg]
                                    k.op("dve", lambda h: h.tensor_scalar(out=blk, in0=blk, scalar1=fc, scalar2=None, op0=ALU.mult),
                                         reads=[p_.b, flg.b], writes=[p_.b])

                    def stage2(d):
                        r, hh, krs, npair, p_o, p_, o_ = d["r"], d["hh"], d["krs"], d["npair"], d["p_o"], d["p_"], d["o_"]
                        for i in range(npair):
                            a_ = krs[2 * i]
                            vt = ve_[:, a_ // 2, hh * 65:(hh + 1) * 65] if a_ % 2 == 0 else vo_[:, (a_ - 1) // 2, hh * 65:(hh + 1) * 65]
                            vb_ = ve_.b if a_ % 2 == 0 else vo_.b
                            k.op("pe", lambda h: h.matmul(p_o[0:64, 0:65], lhsT=p_[:, i * 64:(i + 1) * 64], rhs=vt,
                                                          start=(i == 0), stop=(i == npair - 1)), reads=[p_.b, vb_], writes=[p_o.b])
                        k.op("act", lambda h: h.copy(out=o_[:, hh, :], in_=p_o[0:64, 0:65]), reads=[p_o.b], writes=[o_.b])
                        if hh == 1:
                            k.dma("pool", ON[0][r * 64:(r + 1) * 64, pr * 130:(pr + 1) * 130], o_[:].rearrange("p h e -> p (h e)"), DB("ON0"), o_.b)
                    pipeline(items, stage1, stage2, 2)
                k.barrier()

        def attn_B(j):
            scale = 64 ** -0.5
            with ExitStack() as st:
                KTc = [sb(st, "KTc%d" % i, [128, L + 2 * PADK], BF16) for i in range(2)]
                QTc = [sb(st, "QTc%d" % i, [128, L], BF16) for i in range(2)]
                VH = (NT + 2) // 2
                Vall = [sb(st, "Vall%d" % i, [128, VH, 130], BF16) for i in range(2)]
                band = sb(st, "band", [128, 256], BF16)
                bandf = sb(st, "bandf", [128, 256])
                k.dma("sp", bandf[:], bandm[:, :], bandf.b, IN)
                k.op("dve", lambda h: h.tensor_copy(out=band[:], in_=bandf[:]), reads=[bandf.b], writes=[band.b])
                bb = sb(st, "bb", [128, 3 * NQT * 2])
                k.dma("sp", bb[:], bbias[:, :], bb.b, IN)
                pt = [sb(st, "pt%d" % i, [128, 256], BF16) for i in range(4)]
                NQB = 4
                ost = [sb(st, "ost%d" % i, [128, NQB, 130]) for i in range(3)]
                itc = [0]
                ci = 0
                rc = [0]
                sgc = [0]
                for pr in range(8):
                    for g, dil in enumerate((1, 4, 16)):
                        kt_, qt_ = KTc[ci % 2], QTc[ci % 2]
                        ci += 1
                        load_split(kt_, KT[g], slice(0, L + 2 * PADK), [2 * pr, 2 * pr + 1], "KT%d" % g)
                        load_split(qt_, QT[g], slice(0, L), [2 * pr, 2 * pr + 1], "QT%d" % g)
                        Mc = L // dil
                        nmb = Mc // 128
                        nqb = min(NQB, nmb)
                        items = []
                        for r in range(dil):
                            if dil == 1:
                                vmap = lambda s_: (Vall[0], s_) if s_ < VH else (Vall[1], s_ - VH)
                            else:
                                vb = Vall[rc[0] % 2]
                                rc[0] += 1
                                vmap = lambda s_, vb=vb: (vb, s_)
                            for mb in range(nmb):
                                if mb % nqb == 0:
                                    sgc[0] += 1
                                for hh in range(2):
                                    it = itc[0]
                                    itc[0] += 1
                                    items.append(dict(r=r, mb=mb, hh=hh, qtid=r * nmb + mb, vmap=vmap, o_=ost[sgc[0] % 3],
                                                      p_s=ps[it % 3], p_o=ps[3 + it % 2], p_=pt[it % 4]))

                        def load_v(r, vmap):
                            nt1 = nmb + 1
                            s0 = 0
                            while s0 < nt1:
                                vb, loc = vmap(s0)
                                n_ = min(8, nt1 - s0, VH - loc)
                                tok0 = PADK + dil * (128 * s0 - 64) + r
                                vsrc = bass.AP(tensor=VX[g].tensor, offset=tok0 * 1040 + pr * 130,
                                               ap=[[dil * 1040, 128], [128 * dil * 1040, n_], [1, 130]])
                                k.dma("sp", vb[:, loc:loc + n_, :], vsrc, vb.b, DB("VX%d" % g))
                                s0 += n_

                        def stage1(d):
                            r, mb, hh, qtid, p_s, p_ = d["r"], d["mb"], d["hh"], d["qtid"], d["p_s"], d["p_"]
                            if hh == 0 and mb == 0:
                                load_v(r, d["vmap"])
                            q0 = dil * 128 * mb + r
                            qap = qt_[hh * 64:(hh + 1) * 64, q0:q0 + 127 * dil + 1:dil]
                            for slot in range(2):
                                k0 = PADK + dil * (128 * mb - 64 + 128 * slot) + r
                                kap = kt_[hh * 64:(hh + 1) * 64, k0:k0 + 127 * dil + 1:dil]
                                k.op("pe", lambda h: h.matmul(p_s[:, slot * 128:(slot + 1) * 128], lhsT=kap, rhs=qap, start=True, stop=True),
                                     reads=[kt_.b, qt_.b], writes=[p_s.b])
                            for slot in range(2):
                                bc = bb[:, (g * NQT + qtid) * 2 + slot:(g * NQT + qtid) * 2 + slot + 1]
                                k.op("act", lambda h: h.activation(
                                    out=p_[:, slot * 128:(slot + 1) * 128], in_=p_s[:, slot * 128:(slot + 1) * 128],
                                    func=AF.Exp, scale=scale, bias=bc), reads=[p_s.b, bb.b], writes=[p_.b])
                            k.op("dve", lambda h: h.tensor_tensor(out=p_[:], in0=p_[:], in1=band[:], op=ALU.mult),
                                 reads=[p_.b, band.b], writes=[p_.b])

                        def stage2(d):
                            r, mb, hh, p_o, p_, o_ = d["r"], d["mb"], d["hh"], d["p_o"], d["p_"], d["o_"]
                            for slot in range(2):
                                vb, loc = d["vmap"](mb + slot)
                                k.op("pe", lambda h: h.matmul(
                                    p_o[:, 0:65], lhsT=p_[:, slot * 128:(slot + 1) * 128], rhs=vb[:, loc, hh * 65:(hh + 1) * 65],
                                    start=(slot == 0), stop=(slot == 1)), reads=[p_.b, vb.b], writes=[p_o.b])
                            qi_ = mb % nqb
                            k.op("act", lambda h: h.copy(out=o_[:, qi_, hh * 65:(hh + 1) * 65], in_=p_o[:, 0:65]), reads=[p_o.b], writes=[o_.b])
                            if hh == 1 and qi_ == nqb - 1:
                                mb0 = mb - (nqb - 1)
                                odst = bass.AP(tensor=ON[g].tensor, offset=(dil * 128 * mb0 + r) * 1040 + pr * 130,
                                               ap=[[dil * 1040, 128], [128 * dil * 1040, nqb], [1, 130]])
                                k.dma("pool", odst, o_[:, 0:nqb, :], DB("ON%d" % g), o_.b)
                        pipeline(items, stage1, stage2, 2)
                k.barrier()

        def attn_C(j):
            scale = 128 ** -0.5
            with ExitStack() as st:
                KTc = sb(st, "KTc", [128, L], BF16)
                Vc = sb(st, "Vc", [128, NT, 129], BF16)
                QTc = [sb(st, "QTc%d" % i, [128, L], BF16) for i in range(2)]
                cb = sb(st, "cb", [128, NT])
                k.dma("sp", cb[:], cbias[:, :], cb.b, IN)
                pt = [sb(st, "pt%d" % i, [128, 512], BF16) for i in range(4)]
                ost = [sb(st, "ost%d" % i, [128, 4, 129]) for i in range(2)]
                itc = [0]
                oic = [0]
                for kh in range(2):
                    load_split(KTc, KT[0], slice(PADK, PADK + L), [2 * kh, 2 * kh + 1], "KT0")
                    vcs = VXC[:, kh * 129:(kh + 1) * 129].rearrange("(t p) e -> p t e", p=128)
                    for t0_ in range(0, NT, 8):
                        k.dma("sp", Vc[:, t0_:t0_ + 8, :], vcs[:, t0_:t0_ + 8, :], Vc.b, DB("VXC"))
                    for qh in range(4):
                        head = kh * 4 + qh
                        qt_ = QTc[head % 2]
                        load_split(qt_, QT[0], slice(0, L), [2 * head, 2 * head + 1], "QT0")
                        items = []
                        for qb in range(NB):
                            for kt in range(NT):
                                it = itc[0]
                                itc[0] += 1
                                items.append(dict(qb=qb, kt=kt, p_s=ps[4 + it % 4], p_=pt[it % 4]))

                        def stage1(d):
                            qb, kt, p_s, p_ = d["qb"], d["kt"], d["p_s"], d["p_"]
                            k.op("pe", lambda h: h.matmul(
                                p_s[:, :], lhsT=KTc[:, kt * 128:(kt + 1) * 128], rhs=qt_[:, qb * 512:(qb + 1) * 512], start=True, stop=True),
                                reads=[KTc.b, qt_.b], writes=[p_s.b])
                            k.op("act", lambda h: h.activation(out=p_[:], in_=p_s[:, :], func=AF.Exp, scale=scale, bias=cb[:, kt:kt + 1]),
                                 reads=[p_s.b, cb.b], writes=[p_.b])

                        def stage2(d):
                            qb, kt, p_ = d["qb"], d["kt"], d["p_"]
                            for jq in range(4):
                                k.op("pe", lambda h: h.matmul(
                                    ps[jq][:, 0:129], lhsT=p_[:, jq * 128:(jq + 1) * 128], rhs=Vc[:, kt, :],
                                    start=(kt == 0), stop=(kt == NT - 1)), reads=[p_.b, Vc.b], writes=[ps[jq].b])
                            if kt == NT - 1:
                                o_ = ost[oic[0] % 2]
                                oic[0] += 1
                                for jq in range(4):
                                    k.op("dve", lambda h: h.tensor_copy(out=o_[:, jq, :], in_=ps[jq][:, 0:129]),
                                         reads=[ps[jq].b], writes=[o_.b])
                                odst = ONC[qb * 512:(qb + 1) * 512, head * 129:(head + 1) * 129].rearrange("(j p) e -> p j e", p=128)
                                k.dma("pool", odst, o_[:], DB("ONC"), o_.b)
                        pipeline(items, stage1, stage2, 2)
                k.barrier()

        def phase2b(li, kind, j, xs_ap, xs_buf, xd_ap, xd_buf):
            with ExitStack() as st:
                narr = 3 if kind == 1 else 1
                H, hd = (8, 128) if kind == 2 else (16, 64)
                Wd = H * (hd + 1)
                wo_src = (a_w_o, b_w_o, c_w_o)[kind][j]
                Wo = sb(st, "Wo", [128, 8, D], BF16)
                load_w_bf16(Wo, wo_src, D)
                G = sb(st, "G", [128, D])
                load_bcast(G, li, 2)
                nb_ = [[sb(st, "nb%d_%d" % (a, i), [128, H, hd + 1]) for a in range(narr)] for i in range(3)]
                rl = [sb(st, "rl%d" % i, [128, H], strict=True) for i in range(3)]
                Of = [sb(st, "Of%d" % i, [128, D]) for i in range(3)]
                oT = [sb(st, "oT%d" % i, [128, 8, 128], BF16) for i in range(3)]
                xt = [sb(st, "xt%d" % i, [128, D]) for i in range(3)]
                tmp = [sb(st, "tmp%d" % i, [128, D]) for i in range(3)]
                for tt in range(NT):
                    i = tt % 3
                    t0 = tt * 128
                    for a in range(narr):
                        src = (ONC if kind == 2 else ON[a])[t0:t0 + 128, 0:Wd]
                        k.dma("sp", nb_[i][a][:].rearrange("p h e -> p (h e)"), src, nb_[i][a].b, DB("ONC" if kind == 2 else "ON%d" % a))
                    k.dma("sp", xt[i][:], xs_ap[t0:t0 + 128, :], xt[i].b, xs_buf)
                    acc = nb_[i][0]
                    for a in range(1, narr):
                        k.op("dve", lambda h, acc=acc, o=nb_[i][a]: h.tensor_tensor(out=acc[:], in0=acc[:], in1=o[:], op=ALU.add),
                             reads=[acc.b, nb_[i][a].b], writes=[acc.b])
                    r_ = rl[i]
                    k.op("dve", lambda h, r_=r_, acc=acc: h.tensor_scalar(out=r_[:], in0=acc[:, :, hd], scalar1=1e-30, scalar2=None, op0=ALU.max),
                         reads=[acc.b], writes=[r_.b])
                    k.op("dve", lambda h, r_=r_: h.reciprocal(out=r_[:], in_=r_[:]), reads=[r_.b], writes=[r_.b])
                    o_ = Of[i]
                    rb = bass.AP(tensor=r_[:].tensor, offset=r_[:].offset, ap=[list(r_[:].ap[0]), [1, H], [0, hd]])
                    k.op("dve", lambda h, o_=o_, acc=acc, rb=rb: h.tensor_tensor(
                        out=o_[:].rearrange("p (h d) -> p h d", d=hd), in0=acc[:, :, 0:hd], in1=rb, op=ALU.mult),
                        reads=[acc.b, r_.b], writes=[o_.b])
                    oT_ = oT[i]
                    for half in range(2):
                        p = ps[4 + (2 * tt + half) % 4]
                        for jj in range(4):
                            kk = half * 4 + jj
                            k.op("pe", lambda h, kk=kk, jj=jj, p=p, o_=o_: h.transpose(p[:, jj * 128:(jj + 1) * 128], o_[:, kk * 128:(kk + 1) * 128], ident[:]),
                                 reads=[o_.b, ident.b], writes=[p.b])
                        k.op("act", lambda h, half=half, p=p, oT_=oT_: h.copy(out=oT_[:, half * 4:(half + 1) * 4, :].rearrange("p k t -> p (k t)"), in_=p[:, :]),
                             reads=[p.b], writes=[oT_.b])
                    tm = tmp[i]
                    for half in range(2):
                        p = ps[(2 * tt + half) % 4]
                        for kk in range(8):
                            k.op("pe", lambda h, kk=kk, half=half, p=p, oT_=oT_: h.matmul(p[:, :], lhsT=oT_[:, kk, :], rhs=Wo[:, kk, half * 512:(half + 1) * 512],
                                                                                       start=(kk == 0), stop=(kk == 7)), reads=[oT_.b, Wo.b], writes=[p.b])
                        k.op("dve", lambda h, half=half, p=p, tm=tm: h.tensor_tensor(out=tm[:, half * 512:(half + 1) * 512], in0=p[:, :],
                                                                                   in1=G[:, half * 512:(half + 1) * 512], op=ALU.mult),
                             reads=[p.b, G.b], writes=[tm.b])
                    k.op("pool", lambda h, tm=tm, x_=xt[i]: h.tensor_tensor(out=tm[:], in0=tm[:], in1=x_[:], op=ALU.add),
                         reads=[tm.b, xt[i].b], writes=[tm.b])
                    k.dma("pool", xd_ap[t0:t0 + 128, :], tm[:], xd_buf, tm.b)
                k.barrier()

        def phase3(li, xs_ap, xs_buf, xd_ap, xd_buf):
            TB = 256
            with ExitStack() as st:
                W1 = sb(st, "W1", [128, 8, DFF], BF16)
                W2 = sb(st, "W2", [128, 32, D], BF16)
                load_w_bf16(W1, mlp_w1[li], DFF)
                load_w_bf16(W2, mlp_w2[li], D, kchunks=32)
                G = sb(st, "G", [128, D])
                load_bcast(G, li, 5)
                nrm = Norm(st, li, 1, alloc_xt=False)
                hT = [sb(st, "hT%d" % i, [128, 8, TB], BF16) for i in range(2)]
                uT = sb(st, "uT", [128, 32, TB], BF16)
                xk = [[sb(st, "xk%d_%d" % (i, t), [128, D]) for t in range(TB // 128)] for i in range(2)]
                rb_ = [sb(st, "rb%d" % i, [128, TB]) for i in range(2)]
                tmp = [sb(st, "tmp%d" % i, [128, D]) for i in range(2)]
                ti = 0
                def norm_block3(b):
                    for tt in range(TB // 128):
                        t0 = b * TB + tt * 128
                        nrm.run(xs_ap[t0:t0 + 128, :], xs_buf, hT[b % 2], tt * 128, keep=xk[b % 2][tt])
                norm_block3(0)
                for b in range(L // TB):
                    h_ = hT[b % 2]
                    for f in range(32):
                        p = ps[f % 2]
                        r_ = rb_[f % 2]
                        for kk in range(8):
                            k.op("pe", lambda h, kk=kk, f=f, p=p: h.matmul(p[:, 0:TB], lhsT=W1[:, kk, f * 128:(f + 1) * 128], rhs=h_[:, kk, :],
                                                                         start=(kk == 0), stop=(kk == 7)), reads=[W1.b, h_.b], writes=[p.b])
                        k.op("act", lambda h, p=p, r_=r_: h.activation(out=r_[:], in_=p[:, 0:TB], func=AF.Relu), reads=[p.b], writes=[r_.b])
                        k.op("dve" if f % 2 else "pool", lambda h, f=f, r_=r_: h.tensor_tensor(out=uT[:, f, :], in0=r_[:], in1=r_[:], op=ALU.mult),
                             reads=[r_.b], writes=[uT.b])
                    if b + 1 < L // TB:
                        norm_block3(b + 1)
                    for tt in range(TB // 128):
                        t0 = b * TB + tt * 128
                        tm = tmp[ti % 2]
                        ti += 1
                        for half in range(2):
                            p = ps[2 + half]
                            for f in range(32):
                                k.op("pe", lambda h, f=f, half=half, p=p, tt=tt: h.matmul(p[:, :], lhsT=uT[:, f, tt * 128:(tt + 1) * 128],
                                                                                       rhs=W2[:, f, half * 512:(half + 1) * 512],
                                                                                       start=(f == 0), stop=(f == 31)), reads=[uT.b, W2.b], writes=[p.b])
                            k.op("dve", lambda h, half=half, p=p, tm=tm: h.tensor_tensor(out=tm[:, half * 512:(half + 1) * 512], in0=p[:, :],
                                                                                       in1=G[:, half * 512:(half + 1) * 512], op=ALU.mult),
                                 reads=[p.b, G.b], writes=[tm.b])
                        x_ = xk[b % 2][tt]
                        k.op("pool", lambda h, tm=tm, x_=x_: h.tensor_tensor(out=tm[:], in0=tm[:], in1=x_[:], op=ALU.add),
                             reads=[tm.b, x_.b], writes=[tm.b])
                        k.dma("pool", xd_ap[t0:t0 + 128, :], tm[:], xd_buf, tm.b)
                k.barrier()

        def final_phase(xs_ap, xs_buf):
            with ExitStack() as st:
                fg = sb(st, "fg", [128, D])
                k.dma("sp", fg[:], bass.AP(tensor=final_g.tensor, offset=0, ap=[[0, 128], [1, D]]), fg.b, IN)
                xt = [sb(st, "xt%d" % i, [128, D]) for i in range(2)]
                junk = sb(st, "junk", [128, D])
                ssq = [sb(st, "ssq%d" % i, [128, 1], strict=True) for i in range(2)]
                yo = [sb(st, "yo%d" % i, [128, D]) for i in range(2)]
                for tt in range(NT):
                    i = tt % 2
                    t0 = tt * 128
                    x_, s_, y_ = xt[i], ssq[i], yo[i]
                    k.dma("sp", x_[:], xs_ap[t0:t0 + 128, :], x_.b, xs_buf)
                    k.op("act", lambda h, x_=x_, s_=s_: h.activation(out=junk[:], in_=x_[:], func=AF.Square, accum_out=s_[:]),
                         reads=[x_.b], writes=[junk.b, s_.b])
                    k.op("act", lambda h, s_=s_: h.activation(out=s_[:], in_=s_[:], func=AF.Sqrt, scale=1.0 / D, bias=epsc[:, 0:1]),
                         reads=[s_.b, epsc.b], writes=[s_.b])
                    k.op("dve", lambda h, s_=s_: h.reciprocal(out=s_[:], in_=s_[:]), reads=[s_.b], writes=[s_.b])
                    k.op("act", lambda h, x_=x_, s_=s_, y_=y_: h.activation(out=y_[:], in_=x_[:], func=AF.Copy, scale=s_[:, 0:1]),
                         reads=[x_.b, s_.b], writes=[y_.b])
                    k.op("pool", lambda h, y_=y_: h.tensor_tensor(out=y_[:], in0=y_[:], in1=fg[:], op=ALU.mult), reads=[y_.b, fg.b], writes=[y_.b])
                    k.dma("pool", y_out[t0:t0 + 128, :], y_[:], DB("y"), y_.b)
                k.barrier()

        cur_ap, cur_buf = x_in, IN
        cnt = {0: 0, 1: 0, 2: 0}
        for li, kind in enumerate(kinds):
            j = cnt[kind]
            cnt[kind] += 1
            phase1(li, cur_ap, cur_buf, kind, j)
            (attn_A, attn_B, attn_C)[kind](j)
            phase2b(li, kind, j, cur_ap, cur_buf, xA, DB("xA"))
            phase3(li, xA, DB("xA"), xB, DB("xB"))
            cur_ap, cur_buf = xB, DB("xB")
        final_phase(cur_ap, cur_buf)
    return nc


def host_tables(L, LT, is_prompt):
    t = np.arange(L)
    inv32 = (10000.0 ** (-np.arange(32, dtype=np.float32) / 32)).astype(np.float32)
    d = np.arange(128)
    angB = t[None, :].astype(np.float32) * inv32[d % 32][:, None]
    row, col = (t // 64).astype(np.float32), (t % 64).astype(np.float32)
    inv16 = (10000.0 ** (-np.arange(32, dtype=np.float32) / 32)).astype(np.float32)
    angC = np.where(((d // 32) % 2 == 0)[:, None], row[None, :] * inv16[d % 32][:, None], col[None, :] * inv16[d % 32][:, None]).astype(np.float32)
    out = {
        "cosB": np.cos(angB).astype(np.float32), "sinB": np.sin(angB).astype(np.float32),
        "cosC": np.cos(angC).astype(np.float32), "sinC": np.sin(angC).astype(np.float32),
        "ident": np.eye(128, dtype=np.float32),
    }
    p = np.arange(128)[:, None]
    i = np.arange(128)[None, :]
    out["bandm"] = np.concatenate([(p >= i), (p <= i)], axis=1).astype(np.float32)
    NQT = L // 128
    bb = np.zeros((128, 3, NQT, 2), np.float32)
    for g, dil in enumerate((1, 4, 16)):
        nmb = (L // dil) // 128
        for r in range(dil):
            for mb in range(nmb):
                for slot in range(2):
                    tok = dil * (128 * mb - 64 + 128 * slot + np.arange(128)) + r
                    bb[:, g, r * nmb + mb, slot] = np.where((tok >= 0) & (tok < LT), 0.0, NEG)
    out["bbias"] = bb.reshape(128, -1)
    NT = L // 128
    cb = np.zeros((128, NT), np.float32)
    cb[:, LT // 128:] = NEG
    out["cbias"] = cb
    fl = np.zeros((128, 2), np.float32)
    fl[:, 0] = 0.0 if is_prompt else 1.0
    fl[:, 1] = 1.0 if is_prompt else 0.0
    out["flags"] = fl
    return out


def a_bias_layout(a_rpb):
    n = a_rpb.shape[0]
    kc = np.arange(64)[:, None]
    qc = np.arange(64)[None, :]
    cs = np.clip(qc - 8, 0, 48)
    inwin = (kc >= cs) & (kc < cs + 16)
    dc = np.clip(kc - qc + 15, 0, 30)
    g = a_rpb[:, :, :, dc]
    g = np.where(inwin[None, None, None], g, np.float32(NEG)).astype(np.float32)
    g = np.transpose(g, (0, 1, 3, 2, 4)).reshape(n, 16, 64, 15 * 64)
    return np.ascontiguousarray(g)


_CACHE = {}


def run_slots(slots, weights, L, LP, kinds, n_cores):
    key = (L, LP, tuple(kinds))
    if key not in _CACHE:
        _CACHE[key] = build({"L": L, "LP": LP, "kinds": list(kinds)})
    nc = _CACHE[key]
    common = dict(weights)
    common["a_biasT"] = a_bias_layout(np.asarray(weights["a_rpb"], np.float32))
    del common["a_rpb"]
    pidx = np.arange(128)
    i1 = ((pidx // 32) % 2) * 64 + (pidx % 32)
    qg, kg = common.pop("c_q_g"), common.pop("c_k_g")
    common["c_gcol"] = np.ascontiguousarray(np.stack([qg[:, i1], qg[:, i1 + 32], kg[:, i1], kg[:, i1 + 32]], axis=-1).astype(np.float32))
    in_maps = []
    for (x, c, isp) in slots:
        LT = x.shape[0]
        xs = np.zeros((L, D), np.float32)
        xs[:LT] = x
        m = dict(common)
        m["x"] = xs
        m["cT"] = np.ascontiguousarray(np.asarray(c, np.float32).reshape(8, 128).T)
        m.update(host_tables(L, LT, isp))
        in_maps.append(m)
    while len(in_maps) < n_cores:
        in_maps.append(in_maps[-1])
    res = run_bass_kernel_spmd(nc, in_maps, core_ids=list(range(n_cores)))
    return [r["y"] for r in res.results]


def kernel(x_prompt, x_sample, c_prompt, c_sample, w_mod, b_mod, norm_g, final_g,
           a_w_qkv, a_rpb, a_w_o, b_w_qkv, b_w_o, c_w_qkv, c_q_g, c_k_g, c_w_o, mlp_w1, mlp_w2):
    f = lambda a: np.ascontiguousarray(np.asarray(a, np.float32))
    weights = dict(w_mod=f(w_mod), b_mod=f(b_mod), norm_g=f(norm_g), final_g=f(final_g), a_w_qkv=f(a_w_qkv),
                   a_rpb=f(a_rpb), a_w_o=f(a_w_o), b_w_qkv=f(b_w_qkv), b_w_o=f(b_w_o), c_w_qkv=f(c_w_qkv),
                   c_q_g=f(c_q_g), c_k_g=f(c_k_g), c_w_o=f(c_w_o), mlp_w1=f(mlp_w1), mlp_w2=f(mlp_w2))
    x_prompt, x_sample = f(x_prompt), f(x_sample)
    c_prompt, c_sample = f(c_prompt), f(c_sample)
    L = x_sample.shape[1]
    LP = x_prompt.shape[1]
    slots = [(x_sample[b], c_sample[b], False) for b in range(x_sample.shape[0])]
    slots += [(x_prompt[b], c_prompt[b], True) for b in range(x_prompt.shape[0])]
    ys = run_slots(slots, weights, L, LP, [0, 1, 2, 0], 8)
    ns = x_sample.shape[0]
    y_sample = np.stack([ys[b] for b in range(ns)]).astype(np.float32)
    y_prompt = np.stack([ys[ns + b][:LP] for b in range(x_prompt.shape[0])]).astype(np.float32)
    return (y_prompt, y_sample)
```

```python
import numpy as np
from contextlib import ExitStack
import concourse.bass as bass
import concourse.mybir as mybir
from concourse.bass_utils import run_bass_kernel_spmd

F32 = mybir.dt.float32
BF16 = mybir.dt.bfloat16
AF = mybir.ActivationFunctionType
ALU = mybir.AluOpType
D = 1024
DFF = 4096
NEG = -30000.0
EPS = 1e-6
PADK = 1024


class Sem:
    def __init__(self, h):
        self.h = h
        self.total = 0


class Eng:
    def __init__(self, name, h, sem):
        self.name, self.h, self.sem, self.n, self.waited = name, h, sem, 0, {}


class Buf:
    def __init__(self, name, dram=False):
        self.name, self.dram = name, dram
        self.w = None
        self.rd = {}
        self.dw = set()
        self.dr = set()
        self.sem = None
        self.strict = False


class K:
    def __init__(self, nc, stack, n_dma_sems=90):
        self.nc = nc
        self.engs = {}
        for name, h in (("pe", nc.tensor), ("act", nc.scalar), ("dve", nc.vector),
                        ("pool", nc.gpsimd), ("sp", nc.sync)):
            s = Sem(stack.enter_context(nc.semaphore("s_" + name)))
            self.engs[name] = Eng(name, h, s)
        self.dsems = [Sem(stack.enter_context(nc.semaphore("d%d" % i))) for i in range(n_dma_sems)]
        self.next_ds = 0

    def _get_sem(self, b):
        if b.sem is None:
            b.sem = self.dsems[self.next_ds % len(self.dsems)]
            self.next_ds += 1
        return b.sem

    def _emit_waits(self, E, need):
        for sem, val in need.items():
            if E.waited.get(sem, 0) < val:
                E.h.wait_ge(sem.h, val)
                E.waited[sem] = val

    def op(self, e, fn, reads=(), writes=()):
        E = self.engs[e]
        need = {}

        def add(sem, val):
            if need.get(sem, 0) < val:
                need[sem] = val
        for b in reads:
            if b.w is not None and (b.w[0] is not E or b.strict or E.name != "pe"):
                add(b.w[0].sem, b.w[1])
            for s in b.dw:
                add(s, s.total)
        for b in writes:
            if b.w is not None and (b.w[0] is not E or b.strict):
                add(b.w[0].sem, b.w[1])
            for s in b.dw:
                add(s, s.total)
        for b in writes:
            for F, n in b.rd.items():
                if F is not E or b.strict:
                    add(F.sem, n)
            for s in b.dr:
                add(s, s.total)
        self._emit_waits(E, need)
        inst = fn(E.h)
        inst.then_inc(E.sem.h, 1)
        E.n += 1
        E.sem.total = E.n
        for b in writes:
            b.w = (E, E.n)
            b.rd = {}
            b.dw = set()
            b.dr = set()
        for b in reads:
            if b not in writes:
                b.rd[E] = E.n

    def dma(self, q, out, in_, dst, src, **kw):
        Q = self.engs[q]
        need = {}

        def add(sem, val):
            if need.get(sem, 0) < val:
                need[sem] = val
        if src.w is not None:
            add(src.w[0].sem, src.w[1])
        for s in src.dw:
            add(s, s.total)
        if not dst.dram:
            if dst.w is not None:
                add(dst.w[0].sem, dst.w[1])
            for s in dst.dw:
                add(s, s.total)
            for F, n in dst.rd.items():
                add(F.sem, n)
            for s in dst.dr:
                add(s, s.total)
        self._emit_waits(Q, need)
        sem = self._get_sem(src if dst.dram else dst)
        inst = Q.h.dma_start(out=out, in_=in_, **kw)
        inst.then_inc(sem.h, 16)
        sem.total += 16
        if dst.dram:
            dst.dw.add(sem)
        else:
            dst.w = None
            dst.rd = {}
            dst.dr = set()
            dst.dw = {sem}
        if not src.dram:
            src.dr.add(sem)

    def barrier(self):
        for E in self.engs.values():
            need = {}
            for Fe in self.engs.values():
                if Fe is not E and Fe.n > 0:
                    need[Fe.sem] = Fe.n
            for s in self.dsems:
                if s.total > 0:
                    need[s] = s.total
            self._emit_waits(E, need)


class T:
    def __init__(self, h, name, strict=False):
        self.h = h
        self.b = Buf(name)
        self.b.strict = strict

    def __getitem__(self, k):
        return self.h[k]


def build(cfg):
    L = cfg["L"]
    LP = cfg["LP"]
    kinds = cfg["kinds"]
    NL = len(kinds)
    R = L // 64
    RB = LP // 64
    NT = L // 128
    NB = L // 512
    nc = bass.Bass("TRN2", target_bir_lowering=False)

    def din(name, shape, dt=F32):
        return nc.dram_tensor(name, list(shape), dt, kind="ExternalInput").ap()

    def dsc(name, shape, dt):
        return nc.dram_tensor(name, list(shape), dt, kind="Internal").ap()

    nA = sum(1 for k in kinds if k == 0)
    nB = sum(1 for k in kinds if k == 1)
    nC = sum(1 for k in kinds if k == 2)
    x_in = din("x", [L, D])
    cT = din("cT", [128, 8])
    w_mod = din("w_mod", [NL, D, 6 * D])
    b_mod = din("b_mod", [NL, 6 * D])
    norm_g = din("norm_g", [NL, 2, D])
    final_g = din("final_g", [D])
    a_w_qkv = din("a_w_qkv", [max(nA, 1), D, 3 * D])
    a_biasT = din("a_biasT", [max(nA, 1), 16, 64, 15 * 64])
    a_w_o = din("a_w_o", [max(nA, 1), D, D])
    b_w_qkv = din("b_w_qkv", [max(nB, 1), D, 9 * D])
    b_w_o = din("b_w_o", [max(nB, 1), D, D])
    c_w_qkv = din("c_w_qkv", [max(nC, 1), D, 1536])
    c_gcol = din("c_gcol", [max(nC, 1), 128, 4])
    c_w_o = din("c_w_o", [max(nC, 1), D, D])
    mlp_w1 = din("mlp_w1", [NL, D, DFF])
    mlp_w2 = din("mlp_w2", [NL, DFF, D])
    ident_in = din("ident", [128, 128])
    cosB = din("cosB", [128, L])
    sinB = din("sinB", [128, L])
    cosC = din("cosC", [128, L])
    sinC = din("sinC", [128, L])
    bandm = din("bandm", [128, 256])
    NQT = L // 128
    bbias = din("bbias", [128, 3 * NQT * 2])
    cbias = din("cbias", [128, NT])
    flags = din("flags", [128, 2])
    y_out = nc.dram_tensor("y", [L, D], F32, kind="ExternalOutput").ap()

    xA = dsc("xA", [L, D], F32)
    xB = dsc("xB", [L, D], F32)
    modD = dsc("modD", [NL, 6, D], F32)
    QT = [dsc("QT%d" % g, [D, L], BF16) for g in range(3)]
    KT = [dsc("KT%d" % g, [D, L + 2 * PADK], BF16) for g in range(3)]
    VX = [dsc("VX%d" % g, [L + 2 * PADK, 1040], BF16) for g in range(3)]
    ON = [dsc("ON%d" % g, [L, 1040], F32) for g in range(3)]
    VXC = dsc("VXC", [L, 258], BF16)
    ONC = dsc("ONC", [L, 1032], F32)

    with ExitStack() as top:
        k = K(nc, top)
        dbuf = {}

        def DB(ap_name):
            if ap_name not in dbuf:
                dbuf[ap_name] = Buf(ap_name, dram=True)
            return dbuf[ap_name]
        IN = DB("inputs")

        uid = [0]

        def sb(stack, name, shape, dt=F32, strict=False):
            uid[0] += 1
            nm = "t%d_%s" % (uid[0], name)
            return T(stack.enter_context(nc.sbuf_tensor(nm, list(shape), dt)), nm, strict)

        ps = [T(top.enter_context(nc.psum_tensor("ps%d" % i, [128, 512], F32)), "ps%d" % i) for i in range(8)]
        ident = sb(top, "ident", [128, 128])
        k.dma("sp", ident[:], ident_in[:, :], ident.b, IN)
        ones_f = sb(top, "ones_f", [128, 128])
        k.op("dve", lambda h: h.memset(ones_f[:], 1.0), writes=[ones_f.b])
        epsc = sb(top, "epsc", [128, 1])
        k.op("dve", lambda h: h.memset(epsc[:], EPS), writes=[epsc.b])
        flg = sb(top, "flg", [128, 2])
        k.dma("sp", flg[:], flags[:, :], flg.b, IN)

        with ExitStack() as st:
            zt = sb(st, "zt", [128, 1040], BF16)
            k.op("dve", lambda h: h.memset(zt[:], 0.0), writes=[zt.b])
            for g in range(3):
                for side in range(2):
                    c0 = side * (PADK + L)
                    for ch in range(8):
                        k.dma("pool", KT[g][ch * 128:(ch + 1) * 128, c0:c0 + PADK], zt[:, 0:PADK],
                              DB("KT%d" % g), zt.b)
                        k.dma("pool", VX[g][c0 + ch * 128:c0 + (ch + 1) * 128, :], zt[:, :],
                              DB("VX%d" % g), zt.b)
            sc = sb(st, "sc", [128, 8], strict=True)
            k.dma("sp", sc[:], cT[:, :], sc.b, IN)
            k.op("act", lambda h: h.activation(out=sc[:], in_=sc[:], func=AF.Silu), reads=[sc.b], writes=[sc.b])
            wm = [sb(st, "wm%d" % i, [128, 8, 512]) for i in range(2)]
            modrow = sb(st, "modrow", [1, 6 * D])
            brow = sb(st, "brow", [1, 6 * D])
            grow = sb(st, "grow", [1, 2 * D])
            outrow = sb(st, "outrow", [1, 6 * D])
            for li in range(NL):
                k.dma("sp", brow[:], b_mod[li:li + 1, :], brow.b, IN)
                k.dma("sp", grow[:], norm_g[li:li + 1].rearrange("o t d -> o (t d)"), grow.b, IN)
                for n in range(12):
                    w = wm[n % 2]
                    k.dma("sp", w[:], w_mod[li].rearrange("(k p) n -> p k n", p=128)[:, :, n * 512:(n + 1) * 512], w.b, IN)
                    p = ps[n % 2]
                    for kk in range(8):
                        k.op("pe", lambda h, kk=kk, w=w, p=p: h.matmul(p[0:1, :], lhsT=sc[:, kk:kk + 1], rhs=w[:, kk, :],
                                                                     start=(kk == 0), stop=(kk == 7)),
                             reads=[sc.b, w.b], writes=[p.b])
                    k.op("dve", lambda h, n=n, p=p: h.tensor_tensor(out=modrow[0:1, n * 512:(n + 1) * 512], in0=p[0:1, :],
                                                                  in1=brow[0:1, n * 512:(n + 1) * 512], op=ALU.add),
                         reads=[p.b, brow.b], writes=[modrow.b])
                for sub in range(2):
                    o = sub * 3 * D
                    k.op("dve", lambda h, o=o, sub=sub: h.scalar_tensor_tensor(
                        out=outrow[0:1, o:o + D], in0=modrow[0:1, o + D:o + 2 * D], scalar=1.0,
                        in1=grow[0:1, sub * D:(sub + 1) * D], op0=ALU.add, op1=ALU.mult),
                        reads=[modrow.b, grow.b], writes=[outrow.b])
                    k.op("dve", lambda h, o=o: h.tensor_copy(out=outrow[0:1, o + D:o + 2 * D], in_=modrow[0:1, o:o + D]),
                         reads=[modrow.b], writes=[outrow.b])
                    k.op("dve", lambda h, o=o: h.tensor_copy(out=outrow[0:1, o + 2 * D:o + 3 * D], in_=modrow[0:1, o + 2 * D:o + 3 * D]),
                         reads=[modrow.b], writes=[outrow.b])
                k.dma("pool", modD[li:li + 1].rearrange("o s d -> o (s d)"), outrow[:], DB("modD"), outrow.b)
            k.barrier()

        def load_cols(t, li, kind):
            src = bass.AP(tensor=modD.tensor, offset=(li * 6 + kind) * D, ap=[[1, 128], [128, 8]])
            k.dma("sp", t[:], src, t.b, DB("modD"), allow_slow_non_contiguous=True)

        def load_bcast(t, li, kind):
            src = bass.AP(tensor=modD.tensor, offset=(li * 6 + kind) * D, ap=[[0, 128], [1, D]])
            k.dma("sp", t[:], src, t.b, DB("modD"))

        class Norm:
            def __init__(self, st, li, sub, alloc_xt=True):
                self.xt = [sb(st, "xt%d" % i, [128, D]) for i in range(2)] if alloc_xt else None
                self.junk = sb(st, "junk", [128, D], BF16)
                self.xn = [sb(st, "xn%d" % i, [128, D]) for i in range(2)]
                self.ssq = [sb(st, "ssq%d" % i, [128, 1], strict=True) for i in range(2)]
                self.Ac = sb(st, "Ac", [128, 8])
                self.Bc = sb(st, "Bc", [128, 8])
                load_cols(self.Ac, li, sub * 3 + 0)
                load_cols(self.Bc, li, sub * 3 + 1)
                self.i = 0

            def rstd_of(self, xt, ssq, n_el):
                junk = self.junk
                k.op("act", lambda h: h.activation(out=junk[:], in_=xt[:], func=AF.Square, accum_out=ssq[:]),
                     reads=[xt.b], writes=[junk.b, ssq.b])
                k.op("act", lambda h: h.activation(out=ssq[:], in_=ssq[:], func=AF.Sqrt, scale=1.0 / n_el, bias=epsc[:, 0:1]),
                     reads=[ssq.b, epsc.b], writes=[ssq.b])
                k.op("dve", lambda h: h.reciprocal(out=ssq[:], in_=ssq[:]), reads=[ssq.b], writes=[ssq.b])

            def run(self, xsrc_ap, xsrc_buf, hT, col0, keep=None):
                i = self.i
                self.i += 1
                xt = keep if keep is not None else self.xt[i % 2]
                xn, ssq = self.xn[i % 2], self.ssq[i % 2]
                k.dma("sp", xt[:], xsrc_ap, xt.b, xsrc_buf)
                self.rstd_of(xt, ssq, D)
                k.op("act", lambda h: h.activation(out=xn[:], in_=xt[:], func=AF.Copy, scale=ssq[:, 0:1]),
                     reads=[xt.b, ssq.b], writes=[xn.b])
                for half in range(2):
                    p = ps[6 + half]
                    for j in range(4):
                        kk = half * 4 + j
                        k.op("pe", lambda h, kk=kk, j=j, p=p: h.transpose(p[:, j * 128:(j + 1) * 128], xn[:, kk * 128:(kk + 1) * 128], ident[:]),
                             reads=[xn.b, ident.b], writes=[p.b])
                    for j in range(4):
                        kk = half * 4 + j
                        k.op("act", lambda h, kk=kk, j=j, p=p: h.activation(
                            out=hT[:, kk, col0:col0 + 128], in_=p[:, j * 128:(j + 1) * 128], func=AF.Identity,
                            scale=self.Ac[:, kk:kk + 1], bias=self.Bc[:, kk:kk + 1]),
                            reads=[p.b, self.Ac.b, self.Bc.b], writes=[hT.b])

        def load_w_bf16(t, w_ap, ncols, kchunks=8, piece=1024):
            v = w_ap.rearrange("(k p) n -> p k n", p=128)
            for c0 in range(0, ncols, piece):
                c1 = min(ncols, c0 + piece)
                for k0 in range(0, kchunks, 8):
                    k.dma("pool", t[:, k0:k0 + 8, c0:c1], v[:, k0:k0 + 8, c0:c1], t.b, IN)

        def phase1(li, x_ap, x_buf, kind, j):
            groups = 3 if kind == 1 else 1
            for g in range(groups):
                with ExitStack() as st:
                    if kind == 0:
                        wsrc, nq, nk, nv, hd = a_w_qkv[j], 1024, 1024, 1024, 64
                    elif kind == 1:
                        wsrc, nq, nk, nv, hd = b_w_qkv[j][:, g * 3072:(g + 1) * 3072], 1024, 1024, 1024, 64
                    else:
                        wsrc, nq, nk, nv, hd = c_w_qkv[j], 1024, 256, 256, 128
                    ncol = nq + nk + nv
                    nqk = nq + nk
                    W = sb(st, "W", [128, 8, ncol], BF16)
                    load_w_bf16(W, wsrc, ncol)
                    rope = kind != 0
                    if rope:
                        Wp = sb(st, "Wp", [128, 8, nqk], BF16)
                        for kk in range(8):
                            for (base, ncs) in ((0, nq), (nq, nk)):
                                gi = ncs // 256
                                vi = W[:, kk, base:base + ncs].rearrange("p (g h t j) -> p g h t j", g=gi, h=4, t=2, j=32)
                                vo = Wp[:, kk, base:base + ncs].rearrange("p (g t h j) -> p g t h j", g=gi, h=4, t=2, j=32)
                                for t in range(2):
                                    k.op("dve" if t == 0 else "pool", lambda h, vi=vi, vo=vo, t=t: h.tensor_copy(out=vo[:, :, t, :, :], in_=vi[:, :, :, t, :]),
                                         reads=[W.b], writes=[Wp.b])
                        if kind == 2:
                            gcol = sb(st, "gcol", [128, 4])
                            k.dma("sp", gcol[:], c_gcol[j], gcol.b, IN)
                            Mblk = sb(st, "Mblk", [128, 128])
                            k.op("dve", lambda h: h.memset(Mblk[:], 0.0), writes=[Mblk.b])
                            k.op("dve", lambda h: h.memset(Mblk[0:64, 0:64], 1.0), writes=[Mblk.b])
                            k.op("dve", lambda h: h.memset(Mblk[64:128, 64:128], 1.0), writes=[Mblk.b])
                    nrm = Norm(st, li, 0)
                    hT = [sb(st, "hT%d" % i, [128, 8, 512], BF16) for i in range(2)]
                    qst = [sb(st, "qst%d" % i, [128, 512], BF16) for i in range(4)]
                    H = 16 if kind != 2 else 2
                    vst = [sb(st, "vst%d" % i, [128, H, hd + 1], BF16) for i in range(2)]
                    for v in vst:
                        k.op("dve", lambda h, v=v: h.memset(v[:], 1.0), writes=[v.b])
                    if rope:
                        cs = [sb(st, "cs%d" % i, [128, 512]) for i in range(2)]
                        sn = [sb(st, "sn%d" % i, [128, 512]) for i in range(2)]
                        tA = [sb(st, "tA%d" % i, [128, 512]) for i in range(2)]
                        tB = [sb(st, "tB%d" % i, [128, 512]) for i in range(2)]
                        tC = [sb(st, "tC%d" % i, [128, 512]) for i in range(2)]
                        tD = [sb(st, "tD%d" % i, [128, 512]) for i in range(2)]
                        if kind == 2:
                            tabs = [[sb(st, "tab%d_%d" % (i, n_), [128, 512]) for n_ in range(8)] for i in range(2)]
                            sqa = [sb(st, "sqa%d" % i, [128, 512]) for i in range(2)]
                            sqb = [sb(st, "sqb%d" % i, [128, 512]) for i in range(2)]
                            rs = [sb(st, "rs%d" % i, [128, 512]) for i in range(2)]
                    qi = 0
                    vi_ = 0
                    KTg, QTg = KT[g], QT[g]
                    def norm_block(b):
                        for tt in range(4):
                            t0 = b * 512 + tt * 128
                            nrm.run(x_ap[t0:t0 + 128, :], x_buf, hT[b % 2], tt * 128)
                    norm_block(0)
                    for b in range(NB):
                        h_ = hT[b % 2]
                        if not rope:
                            for c in range(nqk // 128):
                                isq = c < nq // 128
                                pq = ps[qi % 4]
                                for kk in range(8):
                                    k.op("pe", lambda h, kk=kk, c=c, pq=pq: h.matmul(pq[:, :], lhsT=W[:, kk, c * 128:(c + 1) * 128], rhs=h_[:, kk, :],
                                                                                   start=(kk == 0), stop=(kk == 7)),
                                         reads=[W.b, h_.b], writes=[pq.b])
                                q_ = qst[qi % 4]
                                if qi % 2:
                                    k.op("act", lambda h, q_=q_, pq=pq: h.copy(out=q_[:], in_=pq[:, :]), reads=[pq.b], writes=[q_.b])
                                else:
                                    k.op("dve", lambda h, q_=q_, pq=pq: h.tensor_copy(out=q_[:], in_=pq[:, :]), reads=[pq.b], writes=[q_.b])
                                if isq:
                                    k.dma("pool", QTg[c * 128:(c + 1) * 128, b * 512:(b + 1) * 512], q_[:], DB("QT%d" % g), q_.b)
                                else:
                                    ck = c - nq // 128
                                    k.dma("pool", KTg[ck * 128:(ck + 1) * 128, PADK + b * 512:PADK + (b + 1) * 512], q_[:], DB("KT%d" % g), q_.b)
                                qi += 1
                        else:
                            c_, s_ = cs[b % 2], sn[b % 2]
                            ctab, stab = (cosB, sinB) if kind == 1 else (cosC, sinC)
                            k.dma("sp", c_[:], ctab[:, b * 512:(b + 1) * 512], c_.b, IN)
                            k.dma("sp", s_[:], stab[:, b * 512:(b + 1) * 512], s_.b, IN)
                            if kind == 2:
                                tb_ = tabs[b % 2]
                                spec = [(c_, 0), (s_, 1), (s_, 0), (c_, 1), (c_, 2), (s_, 3), (s_, 2), (c_, 3)]
                                for n_, (src_, gc) in enumerate(spec):
                                    k.op("pool" if n_ % 2 else "dve", lambda h, n_=n_, src_=src_, gc=gc: h.tensor_scalar(
                                        out=tb_[n_][:], in0=src_[:], scalar1=gcol[:, gc:gc + 1], scalar2=None, op0=ALU.mult),
                                        reads=[src_.b, gcol.b], writes=[tb_[n_].b])
                            for pi in range(nqk // 256):
                                isq = pi < nq // 256
                                pl = pi if isq else pi - nq // 256
                                cb0 = pi * 256
                                pA = ps[(qi % 2) * 2]
                                pB = ps[(qi % 2) * 2 + 1]
                                for (pp, off) in ((pA, 0), (pB, 128)):
                                    for kk in range(8):
                                        k.op("pe", lambda h, kk=kk, pp=pp, off=off: h.matmul(pp[:, :], lhsT=Wp[:, kk, cb0 + off:cb0 + off + 128], rhs=h_[:, kk, :],
                                                                                            start=(kk == 0), stop=(kk == 7)),
                                             reads=[Wp.b, h_.b], writes=[pp.b])
                                a1, a2, a3, a4 = tA[qi % 2], tB[qi % 2], tC[qi % 2], tD[qi % 2]
                                q1, q2 = qst[(2 * qi) % 4], qst[(2 * qi + 1) % 4]
                                if kind == 1:
                                    m1, m2, m3, m4 = c_, s_, s_, c_
                                else:
                                    o8 = 0 if isq else 4
                                    m1, m2, m3, m4 = tb_[o8 + 0], tb_[o8 + 1], tb_[o8 + 2], tb_[o8 + 3]
                                    s2a, s2b, r2 = sqa[qi % 2], sqb[qi % 2], rs[qi % 2]
                                    k.op("act", lambda h: h.activation(out=s2a[:], in_=pA[:, :], func=AF.Square), reads=[pA.b], writes=[s2a.b])
                                    k.op("act", lambda h: h.activation(out=s2b[:], in_=pB[:, :], func=AF.Square), reads=[pB.b], writes=[s2b.b])
                                    pss = ps[4]
                                    k.op("pe", lambda h: h.matmul(pss[:, :], lhsT=Mblk[:], rhs=s2a[:], start=True, stop=False),
                                         reads=[Mblk.b, s2a.b], writes=[pss.b])
                                    k.op("pe", lambda h: h.matmul(pss[:, :], lhsT=Mblk[:], rhs=s2b[:], start=False, stop=True),
                                         reads=[Mblk.b, s2b.b], writes=[pss.b])
                                    k.op("act", lambda h: h.activation(out=r2[:], in_=pss[:, :], func=AF.Sqrt, scale=1.0 / 128, bias=epsc[:, 0:1]),
                                         reads=[pss.b, epsc.b], writes=[r2.b])
                                    k.op("dve", lambda h: h.reciprocal(out=r2[:], in_=r2[:]), reads=[r2.b], writes=[r2.b])
                                k.op("dve", lambda h: h.tensor_tensor(out=a1[:], in0=pA[:, :], in1=m1[:], op=ALU.mult), reads=[pA.b, m1.b], writes=[a1.b])
                                k.op("dve", lambda h: h.tensor_tensor(out=a2[:], in0=pB[:, :], in1=m2[:], op=ALU.mult), reads=[pB.b, m2.b], writes=[a2.b])
                                k.op("dve", lambda h: h.tensor_tensor(out=a3[:], in0=pA[:, :], in1=m3[:], op=ALU.mult), reads=[pA.b, m3.b], writes=[a3.b])
                                k.op("dve", lambda h: h.tensor_tensor(out=a4[:], in0=pB[:, :], in1=m4[:], op=ALU.mult), reads=[pB.b, m4.b], writes=[a4.b])
                                if kind == 1:
                                    k.op("dve", lambda h: h.tensor_tensor(out=q1[:], in0=a1[:], in1=a2[:], op=ALU.subtract), reads=[a1.b, a2.b], writes=[q1.b])
                                    k.op("pool", lambda h: h.tensor_tensor(out=q2[:], in0=a3[:], in1=a4[:], op=ALU.add), reads=[a3.b, a4.b], writes=[q2.b])
                                else:
                                    k.op("dve", lambda h: h.tensor_tensor(out=a1[:], in0=a1[:], in1=a2[:], op=ALU.subtract), reads=[a1.b, a2.b], writes=[a1.b])
                                    k.op("pool", lambda h: h.tensor_tensor(out=q1[:], in0=a1[:], in1=r2[:], op=ALU.mult), reads=[a1.b, r2.b], writes=[q1.b])
                                    k.op("dve", lambda h: h.tensor_tensor(out=a3[:], in0=a3[:], in1=a4[:], op=ALU.add), reads=[a3.b, a4.b], writes=[a3.b])
                                    k.op("pool", lambda h: h.tensor_tensor(out=q2[:], in0=a3[:], in1=r2[:], op=ALU.mult), reads=[a3.b, r2.b], writes=[q2.b])
                                for (qq, off) in ((q1, 0), (q2, 128)):
                                    r0_ = pl * 256 + off
                                    if isq:
                                        k.dma("pool", QTg[r0_:r0_ + 128, b * 512:(b + 1) * 512], qq[:], DB("QT%d" % g), qq.b)
                                    else:
                                        k.dma("pool", KTg[r0_:r0_ + 128, PADK + b * 512:PADK + (b + 1) * 512], qq[:], DB("KT%d" % g), qq.b)
                                qi += 1
                        if b + 1 < NB:
                            norm_block(b + 1)
                        for tt in range(4):
                            v_ = vst[vi_ % 2]
                            vi_ += 1
                            t0 = b * 512 + tt * 128
                            for cc in range(max(1, nv // 512)):
                                w_ = min(512, nv)
                                pv = ps[5] if kind == 2 else ps[4 + (2 * tt + cc) % 2]
                                for kk in range(8):
                                    k.op("pe", lambda h, kk=kk, cc=cc, pv=pv, w_=w_: h.matmul(
                                        pv[:, 0:w_], lhsT=h_[:, kk, tt * 128:(tt + 1) * 128], rhs=W[:, kk, nqk + cc * 512:nqk + cc * 512 + w_],
                                        start=(kk == 0), stop=(kk == 7)), reads=[W.b, h_.b], writes=[pv.b])
                                nh = w_ // hd
                                k.op("act", lambda h, v_=v_, pv=pv, cc=cc, nh=nh, w_=w_: h.copy(
                                    out=v_[:, cc * nh:(cc + 1) * nh, 0:hd], in_=pv[:, 0:w_].rearrange("p (h d) -> p h d", d=hd)),
                                    reads=[pv.b], writes=[v_.b])
                            if kind == 2:
                                k.dma("pool", VXC[t0:t0 + 128, :], v_[:].rearrange("p h e -> p (h e)"), DB("VXC"), v_.b)
                            else:
                                k.dma("pool", VX[g][PADK + t0:PADK + t0 + 128, :], v_[:].rearrange("p h e -> p (h e)"), DB("VX%d" % g), v_.b)
                    k.barrier()

        def load_split(t, src, cols, blocks, dbname):
            for i, blk in enumerate(blocks):
                pi, b4 = blk // 4, blk % 4
                for two in range(2):
                    r0_ = pi * 256 + two * 128 + b4 * 32
                    k.dma("sp", t[i * 64 + two * 32:i * 64 + two * 32 + 32, :], src[r0_:r0_ + 32, cols], t.b, DB(dbname))

        def pipeline(items, stage1, stage2, depth):
            n = len(items)
            for i in range(min(depth, n)):
                stage1(items[i])
            for i in range(n):
                if i + depth < n:
                    stage1(items[i + depth])
                stage2(items[i])

        def attn_A(j):
            scale = 64 ** -0.5
            R2 = R // 2
            with ExitStack() as st:
                nbuf = 2 if L <= 4096 else 1
                KTc = [sb(st, "KTc%d" % i, [128, L], BF16) for i in range(nbuf)] * (2 // nbuf)
                QTc = [sb(st, "QTc%d" % i, [128, L], BF16) for i in range(nbuf)] * (2 // nbuf)
                VpE = [sb(st, "VpE%d" % i, [128, R2, 130], BF16) for i in range(nbuf)] * (2 // nbuf)
                VpO = [sb(st, "VpO%d" % i, [128, R2, 130], BF16) for i in range(nbuf)] * (2 // nbuf)
                bt = sb(st, "bt", [128, 2, 14 * 64])
                Et = [sb(st, "Et%d" % i, [128, 2, 14 * 64], BF16) for i in range(2)]
                pt = [sb(st, "pt%d" % i, [128, 6 * 64], BF16) for i in range(4)]
                ost = [sb(st, "ost%d" % i, [64, 2, 65]) for i in range(3)]
                itc = [0]
                for pr in range(8):
                    kt_, qt_, ve_, vo_, et_ = KTc[pr % 2], QTc[pr % 2], VpE[pr % 2], VpO[pr % 2], Et[pr % 2]
                    k.dma("sp", kt_[:], KT[0][pr * 128:(pr + 1) * 128, PADK:PADK + L], kt_.b, DB("KT0"))
                    k.dma("sp", qt_[:], QT[0][pr * 128:(pr + 1) * 128, :], qt_.b, DB("QT0"))
                    vsE = VX[0][PADK:PADK + L, pr * 130:(pr + 1) * 130].rearrange("(i p) e -> p i e", p=128)
                    vsO = VX[0][PADK + 64:PADK + 64 + L, pr * 130:(pr + 1) * 130].rearrange("(i p) e -> p i e", p=128)
                    for r0 in range(0, R2, 8):
                        k.dma("sp", ve_[:, r0:r0 + 8, :], vsE[:, r0:r0 + 8, :], ve_.b, DB("VX0"))
                        k.dma("sp", vo_[:, r0:r0 + 8, :], vsO[:, r0:r0 + 8, :], vo_.b, DB("VX0"))
                    for hh in range(2):
                        k.dma("sp", bt[0:64, hh, :], a_biasT[j, pr * 2 + hh][:, 0:14 * 64], bt.b, IN)
                        k.dma("sp", bt[64:128, hh, :], a_biasT[j, pr * 2 + hh][:, 64:15 * 64], bt.b, IN)
                    k.op("act", lambda h, et_=et_: h.activation(out=et_[:], in_=bt[:], func=AF.Exp), reads=[bt.b], writes=[et_.b])
                    items = []
                    for r in range(R):
                        rs0 = min(max(r - 4, 0), R - 8)
                        S = list(range(rs0, rs0 + 8))
                        tags = {kr: 0 for kr in S}
                        if RB < R and RB - 3 <= r <= RB - 1:
                            P = list(range(RB - 8, RB))
                            for kr in S:
                                if kr not in P:
                                    tags[kr] = 1
                            for kr in P:
                                if kr not in tags:
                                    tags[kr] = 2
                        krs = sorted(tags)
                        n = len(krs)
                        assert krs == list(range(krs[0], krs[0] + n)) and n <= 11
                        if n % 2:
                            tags[krs[-1] + 1] = 3
                            krs = krs + [krs[-1] + 1]
                            n += 1
                            assert krs[-1] < R
                        npair = n // 2
                        dr0 = krs[0] - r + 7
                        assert 0 <= dr0 and dr0 + 2 * (npair - 1) <= 13
                        for hh in range(2):
                            it = itc[0]
                            itc[0] += 1
                            items.append(dict(r=r, hh=hh, krs=krs, npair=npair, dr0=dr0, tags=tags, p_s=ps[it % 3],
                                              p_o=ps[3 + it % 2], p_=pt[it % 4], o_=ost[(it // 2) % 3]))

                    def stage1(d):
                        r, hh, krs, npair, dr0, tags, p_s, p_ = d["r"], d["hh"], d["krs"], d["npair"], d["dr0"], d["tags"], d["p_s"], d["p_"]
                        for i in range(npair):
                            a_ = krs[2 * i]
                            k.op("pe", lambda h: h.matmul(
                                p_s[:, i * 64:(i + 1) * 64], lhsT=kt_[hh * 64:(hh + 1) * 64, a_ * 64:a_ * 64 + 128],
                                rhs=qt_[hh * 64:(hh + 1) * 64, r * 64:(r + 1) * 64], start=True, stop=True),
                                reads=[kt_.b, qt_.b], writes=[p_s.b])
                        k.op("act", lambda h: h.activation(out=p_[:, 0:npair * 64], in_=p_s[:, 0:npair * 64], func=AF.Exp, scale=scale),
                             reads=[p_s.b], writes=[p_.b])
                        ev = et_[:, hh, :].rearrange("p (d q) -> p d q", q=64)[:, dr0:dr0 + 2 * (npair - 1) + 1:2, :]
                        k.op("dve", lambda h: h.tensor_tensor(
                            out=p_[:, 0:npair * 64].rearrange("p (i q) -> p i q", q=64), in0=p_[:, 0:npair * 64].rearrange("p (i q) -> p i q", q=64),
                            in1=ev, op=ALU.mult), reads=[p_.b, et_.b], writes=[p_.b])
                        for jj, kr in enumerate(krs):
                            tg = tags[kr]
                            if tg:
                                i, half = jj // 2, jj % 2
                                blk = p_[half * 64:(half + 1) * 64, i * 64:(i + 1) * 64]
                                if tg == 3:
                                    k.op("dve", lambda h: h.memset(blk, 0.0), writes=[p_.b])
                                else:
                                    fc = flg[half * 64:(half + 1) * 64, tg - 1:tg]
                                    k.op("dve", lambda h: h.tensor_scalar(out=blk, in0=blk, scalar1=fc, scalar2=None, op0=ALU.mult),
                                         reads=[p_.b, flg.b], writes=[p_.b])

                    def stage2(d):
                        r, hh, krs, npair, p_o, p_, o_ = d["r"], d["hh"], d["krs"], d["npair"], d["p_o"], d["p_"], d["o_"]
                        for i in range(npair):
                            a_ = krs[2 * i]
                            vt = ve_[:, a_ // 2, hh * 65:(hh + 1) * 65] if a_ % 2 == 0 else vo_[:, (a_ - 1) // 2, hh * 65:(hh + 1) * 65]
                            vb_ = ve_.b if a_ % 2 == 0 else vo_.b
                            k.op("pe", lambda h: h.matmul(p_o[0:64, 0:65], lhsT=p_[:, i * 64:(i + 1) * 64], rhs=vt,
                                                          start=(i == 0), stop=(i == npair - 1)), reads=[p_.b, vb_], writes=[p_o.b])
                        k.op("act", lambda h: h.copy(out=o_[:, hh, :], in_=p_o[0:64, 0:65]), reads=[p_o.b], writes=[o_.b])
                        if hh == 1:
                            k.dma("pool", ON[0][r * 64:(r + 1) * 64, pr * 130:(pr + 1) * 130], o_[:].rearrange("p h e -> p (h e)"), DB("ON0"), o_.b)
                    pipeline(items, stage1, stage2, 2)
                k.barrier()

        def attn_B(j):
            scale = 64 ** -0.5
            with ExitStack() as st:
                KTc = [sb(st, "KTc%d" % i, [128, L + 2 * PADK], BF16) for i in range(2)]
                QTc = [sb(st, "QTc%d" % i, [128, L], BF16) for i in range(2)]
                VH = (NT + 2) // 2
                Vall = [sb(st, "Vall%d" % i, [128, VH, 130], BF16) for i in range(2)]
                band = sb(st, "band", [128, 256], BF16)
                bandf = sb(st, "bandf", [128, 256])
                k.dma("sp", bandf[:], bandm[:, :], bandf.b, IN)
                k.op("dve", lambda h: h.tensor_copy(out=band[:], in_=bandf[:]), reads=[bandf.b], writes=[band.b])
                bb = sb(st, "bb", [128, 3 * NQT * 2])
                k.dma("sp", bb[:], bbias[:, :], bb.b, IN)
                pt = [sb(st, "pt%d" % i, [128, 256], BF16) for i in range(4)]
                NQB = 4
                ost = [sb(st, "ost%d" % i, [128, NQB, 130]) for i in range(3)]
                itc = [0]
                ci = 0
                rc = [0]
                sgc = [0]
                for pr in range(8):
                    for g, dil in enumerate((1, 4, 16)):
                        kt_, qt_ = KTc[ci % 2], QTc[ci % 2]
                        ci += 1
                        load_split(kt_, KT[g], slice(0, L + 2 * PADK), [2 * pr, 2 * pr + 1], "KT%d" % g)
                        load_split(qt_, QT[g], slice(0, L), [2 * pr, 2 * pr + 1], "QT%d" % g)
                        Mc = L // dil
                        nmb = Mc // 128
                        nqb = min(NQB, nmb)
                        items = []
                        for r in range(dil):
                            if dil == 1:
                                vmap = lambda s_: (Vall[0], s_) if s_ < VH else (Vall[1], s_ - VH)
                            else:
                                vb = Vall[rc[0] % 2]
                                rc[0] += 1
                                vmap = lambda s_, vb=vb: (vb, s_)
                            for mb in range(nmb):
                                if mb % nqb == 0:
                                    sgc[0] += 1
                                for hh in range(2):
                                    it = itc[0]
                                    itc[0] += 1
                                    items.append(dict(r=r, mb=mb, hh=hh, qtid=r * nmb + mb, vmap=vmap, o_=ost[sgc[0] % 3],
                                                      p_s=ps[it % 3], p_o=ps[3 + it % 2], p_=pt[it % 4]))

                        def load_v(r, vmap):
                            nt1 = nmb + 1
                            s0 = 0
                            while s0 < nt1:
                                vb, loc = vmap(s0)
                                n_ = min(8, nt1 - s0, VH - loc)
                                tok0 = PADK + dil * (128 * s0 - 64) + r
                                vsrc = bass.AP(tensor=VX[g].tensor, offset=tok0 * 1040 + pr * 130,
                                               ap=[[dil * 1040, 128], [128 * dil * 1040, n_], [1, 130]])
                                k.dma("sp", vb[:, loc:loc + n_, :], vsrc, vb.b, DB("VX%d" % g))
                                s0 += n_

                        def stage1(d):
                            r, mb, hh, qtid, p_s, p_ = d["r"], d["mb"], d["hh"], d["qtid"], d["p_s"], d["p_"]
                            if hh == 0 and mb == 0:
                                load_v(r, d["vmap"])
                            q0 = dil * 128 * mb + r
                            qap = qt_[hh * 64:(hh + 1) * 64, q0:q0 + 127 * dil + 1:dil]
                            for slot in range(2):
                                k0 = PADK + dil * (128 * mb - 64 + 128 * slot) + r
                                kap = kt_[hh * 64:(hh + 1) * 64, k0:k0 + 127 * dil + 1:dil]
                                k.op("pe", lambda h: h.matmul(p_s[:, slot * 128:(slot + 1) * 128], lhsT=kap, rhs=qap, start=True, stop=True),
                                     reads=[kt_.b, qt_.b], writes=[p_s.b])
                            for slot in range(2):
                                bc = bb[:, (g * NQT + qtid) * 2 + slot:(g * NQT + qtid) * 2 + slot + 1]
                                k.op("act", lambda h: h.activation(
                                    out=p_[:, slot * 128:(slot + 1) * 128], in_=p_s[:, slot * 128:(slot + 1) * 128],
                                    func=AF.Exp, scale=scale, bias=bc), reads=[p_s.b, bb.b], writes=[p_.b])
                            k.op("dve", lambda h: h.tensor_tensor(out=p_[:], in0=p_[:], in1=band[:], op=ALU.mult),
                                 reads=[p_.b, band.b], writes=[p_.b])

                        def stage2(d):
                            r, mb, hh, p_o, p_, o_ = d["r"], d["mb"], d["hh"], d["p_o"], d["p_"], d["o_"]
                            for slot in range(2):
                                vb, loc = d["vmap"](mb + slot)
                                k.op("pe", lambda h: h.matmul(
                                    p_o[:, 0:65], lhsT=p_[:, slot * 128:(slot + 1) * 128], rhs=vb[:, loc, hh * 65:(hh + 1) * 65],
                                    start=(slot == 0), stop=(slot == 1)), reads=[p_.b, vb.b], writes=[p_o.b])
                            qi_ = mb % nqb
                            k.op("act", lambda h: h.copy(out=o_[:, qi_, hh * 65:(hh + 1) * 65], in_=p_o[:, 0:65]), reads=[p_o.b], writes=[o_.b])
                            if hh == 1 and qi_ == nqb - 1:
                                mb0 = mb - (nqb - 1)
                                odst = bass.AP(tensor=ON[g].tensor, offset=(dil * 128 * mb0 + r) * 1040 + pr * 130,
                                               ap=[[dil * 1040, 128], [128 * dil * 1040, nqb], [1, 130]])
                                k.dma("pool", odst, o_[:, 0:nqb, :], DB("ON%d" % g), o_.b)
                        pipeline(items, stage1, stage2, 2)
                k.barrier()

        def attn_C(j):
            scale = 128 ** -0.5
            with ExitStack() as st:
                KTc = sb(st, "KTc", [128, L], BF16)
                Vc = sb(st, "Vc", [128, NT, 129], BF16)
                QTc = [sb(st, "QTc%d" % i, [128, L], BF16) for i in range(2)]
                cb = sb(st, "cb", [128, NT])
                k.dma("sp", cb[:], cbias[:, :], cb.b, IN)
                pt = [sb(st, "pt%d" % i, [128, 512], BF16) for i in range(4)]
                ost = [sb(st, "ost%d" % i, [128, 4, 129]) for i in range(2)]
                itc = [0]
                oic = [0]
                for kh in range(2):
                    load_split(KTc, KT[0], slice(PADK, PADK + L), [2 * kh, 2 * kh + 1], "KT0")
                    vcs = VXC[:, kh * 129:(kh + 1) * 129].rearrange("(t p) e -> p t e", p=128)
                    for t0_ in range(0, NT, 8):
                        k.dma("sp", Vc[:, t0_:t0_ + 8, :], vcs[:, t0_:t0_ + 8, :], Vc.b, DB("VXC"))
                    for qh in range(4):
                        head = kh * 4 + qh
                        qt_ = QTc[head % 2]
                        load_split(qt_, QT[0], slice(0, L), [2 * head, 2 * head + 1], "QT0")
                        items = []
                        for qb in range(NB):
                            for kt in range(NT):
                                it = itc[0]
                                itc[0] += 1
                                items.append(dict(qb=qb, kt=kt, p_s=ps[4 + it % 4], p_=pt[it % 4]))

                        def stage1(d):
                            qb, kt, p_s, p_ = d["qb"], d["kt"], d["p_s"], d["p_"]
                            k.op("pe", lambda h: h.matmul(
                                p_s[:, :], lhsT=KTc[:, kt * 128:(kt + 1) * 128], rhs=qt_[:, qb * 512:(qb + 1) * 512], start=True, stop=True),
                                reads=[KTc.b, qt_.b], writes=[p_s.b])
                            k.op("act", lambda h: h.activation(out=p_[:], in_=p_s[:, :], func=AF.Exp, scale=scale, bias=cb[:, kt:kt + 1]),
                                 reads=[p_s.b, cb.b], writes=[p_.b])

                        def stage2(d):
                            qb, kt, p_ = d["qb"], d["kt"], d["p_"]
                            for jq in range(4):
                                k.op("pe", lambda h: h.matmul(
                                    ps[jq][:, 0:129], lhsT=p_[:, jq * 128:(jq + 1) * 128], rhs=Vc[:, kt, :],
                                    start=(kt == 0), stop=(kt == NT - 1)), reads=[p_.b, Vc.b], writes=[ps[jq].b])
                            if kt == NT - 1:
                                o_ = ost[oic[0] % 2]
                                oic[0] += 1
                                for jq in range(4):
                                    k.op("dve", lambda h: h.tensor_copy(out=o_[:, jq, :], in_=ps[jq][:, 0:129]),
                                         reads=[ps[jq].b], writes=[o_.b])
                                odst = ONC[qb * 512:(qb + 1) * 512, head * 129:(head + 1) * 129].rearrange("(j p) e -> p j e", p=128)
                                k.dma("pool", odst, o_[:], DB("ONC"), o_.b)
                        pipeline(items, stage1, stage2, 2)
                k.barrier()

        def phase2b(li, kind, j, xs_ap, xs_buf, xd_ap, xd_buf):
            with ExitStack() as st:
                narr = 3 if kind == 1 else 1
                H, hd = (8, 128) if kind == 2 else (16, 64)
                Wd = H * (hd + 1)
                wo_src = (a_w_o, b_w_o, c_w_o)[kind][j]
                Wo = sb(st, "Wo", [128, 8, D], BF16)
                load_w_bf16(Wo, wo_src, D)
                G = sb(st, "G", [128, D])
                load_bcast(G, li, 2)
                nb_ = [[sb(st, "nb%d_%d" % (a, i), [128, H, hd + 1]) for a in range(narr)] for i in range(3)]
                rl = [sb(st, "rl%d" % i, [128, H], strict=True) for i in range(3)]
                Of = [sb(st, "Of%d" % i, [128, D]) for i in range(3)]
                oT = [sb(st, "oT%d" % i, [128, 8, 128], BF16) for i in range(3)]
                xt = [sb(st, "xt%d" % i, [128, D]) for i in range(3)]
                tmp = [sb(st, "tmp%d" % i, [128, D]) for i in range(3)]
                def stageA(tt):
                    i = tt % 3
                    t0 = tt * 128
                    for a in range(narr):
                        src = (ONC if kind == 2 else ON[a])[t0:t0 + 128, 0:Wd]
                        k.dma("sp", nb_[i][a][:].rearrange("p h e -> p (h e)"), src, nb_[i][a].b, DB("ONC" if kind == 2 else "ON%d" % a))
                    k.dma("sp", xt[i][:], xs_ap[t0:t0 + 128, :], xt[i].b, xs_buf)
                    acc = nb_[i][0]
                    for a in range(1, narr):
                        k.op("dve", lambda h, acc=acc, o=nb_[i][a]: h.tensor_tensor(out=acc[:], in0=acc[:], in1=o[:], op=ALU.add),
                             reads=[acc.b, nb_[i][a].b], writes=[acc.b])
                    r_ = rl[i]
                    k.op("dve", lambda h, r_=r_, acc=acc: h.tensor_scalar(out=r_[:], in0=acc[:, :, hd], scalar1=1e-30, scalar2=None, op0=ALU.max),
                         reads=[acc.b], writes=[r_.b])
                    k.op("dve", lambda h, r_=r_: h.reciprocal(out=r_[:], in_=r_[:]), reads=[r_.b], writes=[r_.b])
                    o_ = Of[i]
                    rb = bass.AP(tensor=r_[:].tensor, offset=r_[:].offset, ap=[list(r_[:].ap[0]), [1, H], [0, hd]])
                    k.op("dve", lambda h, o_=o_, acc=acc, rb=rb: h.tensor_tensor(
                        out=o_[:].rearrange("p (h d) -> p h d", d=hd), in0=acc[:, :, 0:hd], in1=rb, op=ALU.mult),
                        reads=[acc.b, r_.b], writes=[o_.b])
                    oT_ = oT[i]
                    for half in range(2):
                        p = ps[4 + (2 * tt + half) % 4]
                        for jj in range(4):
                            kk = half * 4 + jj
                            k.op("pe", lambda h, kk=kk, jj=jj, p=p, o_=o_: h.transpose(p[:, jj * 128:(jj + 1) * 128], o_[:, kk * 128:(kk + 1) * 128], ident[:]),
                                 reads=[o_.b, ident.b], writes=[p.b])
                        k.op("act", lambda h, half=half, p=p, oT_=oT_: h.copy(out=oT_[:, half * 4:(half + 1) * 4, :].rearrange("p k t -> p (k t)"), in_=p[:, :]),
                             reads=[p.b], writes=[oT_.b])
                def stageB(tt):
                    i = tt % 3
                    t0 = tt * 128
                    oT_ = oT[i]
                    tm = tmp[i]
                    for half in range(2):
                        p = ps[(2 * tt + half) % 4]
                        for kk in range(8):
                            k.op("pe", lambda h, kk=kk, half=half, p=p, oT_=oT_: h.matmul(p[:, :], lhsT=oT_[:, kk, :], rhs=Wo[:, kk, half * 512:(half + 1) * 512],
                                                                                       start=(kk == 0), stop=(kk == 7)), reads=[oT_.b, Wo.b], writes=[p.b])
                        k.op("dve", lambda h, half=half, p=p, tm=tm: h.tensor_tensor(out=tm[:, half * 512:(half + 1) * 512], in0=p[:, :],
                                                                                   in1=G[:, half * 512:(half + 1) * 512], op=ALU.mult),
                             reads=[p.b, G.b], writes=[tm.b])
                    k.op("pool", lambda h, tm=tm, x_=xt[i]: h.tensor_tensor(out=tm[:], in0=tm[:], in1=x_[:], op=ALU.add),
                         reads=[tm.b, xt[i].b], writes=[tm.b])
                    k.dma("pool", xd_ap[t0:t0 + 128, :], tm[:], xd_buf, tm.b)
                stageA(0)
                for tt in range(NT):
                    if tt + 1 < NT:
                        stageA(tt + 1)
                    stageB(tt)
                k.barrier()

        def phase3(li, xs_ap, xs_buf, xd_ap, xd_buf):
            TB = 256
            with ExitStack() as st:
                W1 = sb(st, "W1", [128, 8, DFF], BF16)
                W2 = sb(st, "W2", [128, 32, D], BF16)
                load_w_bf16(W1, mlp_w1[li], DFF)
                load_w_bf16(W2, mlp_w2[li], D, kchunks=32)
                G = sb(st, "G", [128, D])
                load_bcast(G, li, 5)
                nrm = Norm(st, li, 1, alloc_xt=False)
                hT = [sb(st, "hT%d" % i, [128, 8, TB], BF16) for i in range(2)]
                uT = sb(st, "uT", [128, 32, TB], BF16)
                xk = [[sb(st, "xk%d_%d" % (i, t), [128, D]) for t in range(TB // 128)] for i in range(2)]
                rb_ = [sb(st, "rb%d" % i, [128, TB]) for i in range(2)]
                tmp = [sb(st, "tmp%d" % i, [128, D]) for i in range(2)]
                ti = 0
                def norm_block3(b):
                    for tt in range(TB // 128):
                        t0 = b * TB + tt * 128
                        nrm.run(xs_ap[t0:t0 + 128, :], xs_buf, hT[b % 2], tt * 128, keep=xk[b % 2][tt])
                norm_block3(0)
                for b in range(L // TB):
                    h_ = hT[b % 2]
                    for f in range(32):
                        p = ps[f % 2]
                        r_ = rb_[f % 2]
                        for kk in range(8):
                            k.op("pe", lambda h, kk=kk, f=f, p=p: h.matmul(p[:, 0:TB], lhsT=W1[:, kk, f * 128:(f + 1) * 128], rhs=h_[:, kk, :],
                                                                         start=(kk == 0), stop=(kk == 7)), reads=[W1.b, h_.b], writes=[p.b])
                        k.op("act", lambda h, p=p, r_=r_: h.activation(out=r_[:], in_=p[:, 0:TB], func=AF.Relu), reads=[p.b], writes=[r_.b])
                        k.op("dve" if f % 2 else "pool", lambda h, f=f, r_=r_: h.tensor_tensor(out=uT[:, f, :], in0=r_[:], in1=r_[:], op=ALU.mult),
                             reads=[r_.b], writes=[uT.b])
                    if b + 1 < L // TB:
                        norm_block3(b + 1)
                    for tt in range(TB // 128):
                        t0 = b * TB + tt * 128
                        tm = tmp[ti % 2]
                        ti += 1
                        for half in range(2):
                            p = ps[2 + half]
                            for f in range(32):
                                k.op("pe", lambda h, f=f, half=half, p=p, tt=tt: h.matmul(p[:, :], lhsT=uT[:, f, tt * 128:(tt + 1) * 128],
                                                                                       rhs=W2[:, f, half * 512:(half + 1) * 512],
                                                                                       start=(f == 0), stop=(f == 31)), reads=[uT.b, W2.b], writes=[p.b])
                            k.op("dve", lambda h, half=half, p=p, tm=tm: h.tensor_tensor(out=tm[:, half * 512:(half + 1) * 512], in0=p[:, :],
                                                                                       in1=G[:, half * 512:(half + 1) * 512], op=ALU.mult),
                                 reads=[p.b, G.b], writes=[tm.b])
                        x_ = xk[b % 2][tt]
                        k.op("pool", lambda h, tm=tm, x_=x_: h.tensor_tensor(out=tm[:], in0=tm[:], in1=x_[:], op=ALU.add),
                             reads=[tm.b, x_.b], writes=[tm.b])
                        k.dma("pool", xd_ap[t0:t0 + 128, :], tm[:], xd_buf, tm.b)
                k.barrier()

        def final_phase(xs_ap, xs_buf):
            with ExitStack() as st:
                fg = sb(st, "fg", [128, D])
                k.dma("sp", fg[:], bass.AP(tensor=final_g.tensor, offset=0, ap=[[0, 128], [1, D]]), fg.b, IN)
                xt = [sb(st, "xt%d" % i, [128, D]) for i in range(2)]
                junk = sb(st, "junk", [128, D])
                ssq = [sb(st, "ssq%d" % i, [128, 1], strict=True) for i in range(2)]
                yo = [sb(st, "yo%d" % i, [128, D]) for i in range(2)]
                for tt in range(NT):
                    i = tt % 2
                    t0 = tt * 128
                    x_, s_, y_ = xt[i], ssq[i], yo[i]
                    k.dma("sp", x_[:], xs_ap[t0:t0 + 128, :], x_.b, xs_buf)
                    k.op("act", lambda h, x_=x_, s_=s_: h.activation(out=junk[:], in_=x_[:], func=AF.Square, accum_out=s_[:]),
                         reads=[x_.b], writes=[junk.b, s_.b])
                    k.op("act", lambda h, s_=s_: h.activation(out=s_[:], in_=s_[:], func=AF.Sqrt, scale=1.0 / D, bias=epsc[:, 0:1]),
                         reads=[s_.b, epsc.b], writes=[s_.b])
                    k.op("dve", lambda h, s_=s_: h.reciprocal(out=s_[:], in_=s_[:]), reads=[s_.b], writes=[s_.b])
                    k.op("act", lambda h, x_=x_, s_=s_, y_=y_: h.activation(out=y_[:], in_=x_[:], func=AF.Copy, scale=s_[:, 0:1]),
                         reads=[x_.b, s_.b], writes=[y_.b])
                    k.op("pool", lambda h, y_=y_: h.tensor_tensor(out=y_[:], in0=y_[:], in1=fg[:], op=ALU.mult), reads=[y_.b, fg.b], writes=[y_.b])
                    k.dma("pool", y_out[t0:t0 + 128, :], y_[:], DB("y"), y_.b)
                k.barrier()

        cur_ap, cur_buf = x_in, IN
        cnt = {0: 0, 1: 0, 2: 0}
        for li, kind in enumerate(kinds):
            j = cnt[kind]
            cnt[kind] += 1
            phase1(li, cur_ap, cur_buf, kind, j)
            (attn_A, attn_B, attn_C)[kind](j)
            phase2b(li, kind, j, cur_ap, cur_buf, xA, DB("xA"))
            phase3(li, xA, DB("xA"), xB, DB("xB"))
            cur_ap, cur_buf = xB, DB("xB")
        final_phase(cur_ap, cur_buf)
    return nc


def host_tables(L, LT, is_prompt):
    t = np.arange(L)
    inv32 = (10000.0 ** (-np.arange(32, dtype=np.float32) / 32)).astype(np.float32)
    d = np.arange(128)
    angB = t[None, :].astype(np.float32) * inv32[d % 32][:, None]
    row, col = (t // 64).astype(np.float32), (t % 64).astype(np.float32)
    inv16 = (10000.0 ** (-np.arange(32, dtype=np.float32) / 32)).astype(np.float32)
    angC = np.where(((d // 32) % 2 == 0)[:, None], row[None, :] * inv16[d % 32][:, None], col[None, :] * inv16[d % 32][:, None]).astype(np.float32)
    out = {
        "cosB": np.cos(angB).astype(np.float32), "sinB": np.sin(angB).astype(np.float32),
        "cosC": np.cos(angC).astype(np.float32), "sinC": np.sin(angC).astype(np.float32),
        "ident": np.eye(128, dtype=np.float32),
    }
    p = np.arange(128)[:, None]
    i = np.arange(128)[None, :]
    out["bandm"] = np.concatenate([(p >= i), (p <= i)], axis=1).astype(np.float32)
    NQT = L // 128
    bb = np.zeros((128, 3, NQT, 2), np.float32)
    for g, dil in enumerate((1, 4, 16)):
        nmb = (L // dil) // 128
        for r in range(dil):
            for mb in range(nmb):
                for slot in range(2):
                    tok = dil * (128 * mb - 64 + 128 * slot + np.arange(128)) + r
                    bb[:, g, r * nmb + mb, slot] = np.where((tok >= 0) & (tok < LT), 0.0, NEG)
    out["bbias"] = bb.reshape(128, -1)
    NT = L // 128
    cb = np.zeros((128, NT), np.float32)
    cb[:, LT // 128:] = NEG
    out["cbias"] = cb
    fl = np.zeros((128, 2), np.float32)
    fl[:, 0] = 0.0 if is_prompt else 1.0
    fl[:, 1] = 1.0 if is_prompt else 0.0
    out["flags"] = fl
    return out


def a_bias_layout(a_rpb):
    n = a_rpb.shape[0]
    kc = np.arange(64)[:, None]
    qc = np.arange(64)[None, :]
    cs = np.clip(qc - 8, 0, 48)
    inwin = (kc >= cs) & (kc < cs + 16)
    dc = np.clip(kc - qc + 15, 0, 30)
    g = a_rpb[:, :, :, dc]
    g = np.where(inwin[None, None, None], g, np.float32(NEG)).astype(np.float32)
    g = np.transpose(g, (0, 1, 3, 2, 4)).reshape(n, 16, 64, 15 * 64)
    return np.ascontiguousarray(g)


_CACHE = {}


def run_slots(slots, weights, L, LP, kinds, n_cores):
    key = (L, LP, tuple(kinds))
    if key not in _CACHE:
        _CACHE[key] = build({"L": L, "LP": LP, "kinds": list(kinds)})
    nc = _CACHE[key]
    common = dict(weights)
    common["a_biasT"] = a_bias_layout(np.asarray(weights["a_rpb"], np.float32))
    del common["a_rpb"]
    pidx = np.arange(128)
    i1 = ((pidx // 32) % 2) * 64 + (pidx % 32)
    qg, kg = common.pop("c_q_g"), common.pop("c_k_g")
    common["c_gcol"] = np.ascontiguousarray(np.stack([qg[:, i1], qg[:, i1 + 32], kg[:, i1], kg[:, i1 + 32]], axis=-1).astype(np.float32))
    in_maps = []
    for (x, c, isp) in slots:
        LT = x.shape[0]
        xs = np.zeros((L, D), np.float32)
        xs[:LT] = x
        m = dict(common)
        m["x"] = xs
        m["cT"] = np.ascontiguousarray(np.asarray(c, np.float32).reshape(8, 128).T)
        m.update(host_tables(L, LT, isp))
        in_maps.append(m)
    while len(in_maps) < n_cores:
        in_maps.append(in_maps[-1])
    res = run_bass_kernel_spmd(nc, in_maps, core_ids=list(range(n_cores)))
    return [r["y"] for r in res.results]


def kernel(x_prompt, x_sample, c_prompt, c_sample, w_mod, b_mod, norm_g, final_g,
           a_w_qkv, a_rpb, a_w_o, b_w_qkv, b_w_o, c_w_qkv, c_q_g, c_k_g, c_w_o, mlp_w1, mlp_w2):
    f = lambda a: np.ascontiguousarray(np.asarray(a, np.float32))
    weights = dict(w_mod=f(w_mod), b_mod=f(b_mod), norm_g=f(norm_g), final_g=f(final_g), a_w_qkv=f(a_w_qkv),
                   a_rpb=f(a_rpb), a_w_o=f(a_w_o), b_w_qkv=f(b_w_qkv), b_w_o=f(b_w_o), c_w_qkv=f(c_w_qkv),
                   c_q_g=f(c_q_g), c_k_g=f(c_k_g), c_w_o=f(c_w_o), mlp_w1=f(mlp_w1), mlp_w2=f(mlp_w2))
    x_prompt, x_sample = f(x_prompt), f(x_sample)
    c_prompt, c_sample = f(c_prompt), f(c_sample)
    L = x_sample.shape[1]
    LP = x_prompt.shape[1]
    slots = [(x_sample[b], c_sample[b], False) for b in range(x_sample.shape[0])]
    slots += [(x_prompt[b], c_prompt[b], True) for b in range(x_prompt.shape[0])]
    ys = run_slots(slots, weights, L, LP, [0, 1, 2, 0], 8)
    ns = x_sample.shape[0]
    y_sample = np.stack([ys[b] for b in range(ns)]).astype(np.float32)
    y_prompt = np.stack([ys[ns + b][:LP] for b in range(x_prompt.shape[0])]).astype(np.float32)
    return (y_prompt, y_sample)
```

```python
import numpy as np
from contextlib import ExitStack
import concourse.bass as bass
import concourse.mybir as mybir
from concourse.bass_utils import run_bass_kernel_spmd

F32 = mybir.dt.float32
BF16 = mybir.dt.bfloat16
AF = mybir.ActivationFunctionType
ALU = mybir.AluOpType
D = 1024
DFF = 4096
NEG = -30000.0
EPS = 1e-6
PADK = 1024


class Sem:
    def __init__(self, h):
        self.h = h
        self.total = 0


class Eng:
    def __init__(self, name, h, sem):
        self.name, self.h, self.sem, self.n, self.waited = name, h, sem, 0, {}


class Buf:
    def __init__(self, name, dram=False):
        self.name, self.dram = name, dram
        self.w = None
        self.rd = {}
        self.dw = set()
        self.dr = set()
        self.sem = None
        self.strict = False


class K:
    def __init__(self, nc, stack, n_dma_sems=90):
        self.nc = nc
        self.engs = {}
        for name, h in (("pe", nc.tensor), ("act", nc.scalar), ("dve", nc.vector),
                        ("pool", nc.gpsimd), ("sp", nc.sync)):
            s = Sem(stack.enter_context(nc.semaphore("s_" + name)))
            self.engs[name] = Eng(name, h, s)
        self.dsems = [Sem(stack.enter_context(nc.semaphore("d%d" % i))) for i in range(n_dma_sems)]
        self.next_ds = 0

    def _get_sem(self, b):
        if b.sem is None:
            b.sem = self.dsems[self.next_ds % len(self.dsems)]
            self.next_ds += 1
        return b.sem

    def _emit_waits(self, E, need):
        for sem, val in need.items():
            if E.waited.get(sem, 0) < val:
                E.h.wait_ge(sem.h, val)
                E.waited[sem] = val

    def op(self, e, fn, reads=(), writes=()):
        E = self.engs[e]
        need = {}

        def add(sem, val):
            if need.get(sem, 0) < val:
                need[sem] = val
        for b in reads:
            if b.w is not None and (b.w[0] is not E or b.strict or E.name != "pe"):
                add(b.w[0].sem, b.w[1])
            for s in b.dw:
                add(s, s.total)
        for b in writes:
            if b.w is not None and (b.w[0] is not E or b.strict):
                add(b.w[0].sem, b.w[1])
            for s in b.dw:
                add(s, s.total)
        for b in writes:
            for F, n in b.rd.items():
                if F is not E or b.strict:
                    add(F.sem, n)
            for s in b.dr:
                add(s, s.total)
        self._emit_waits(E, need)
        inst = fn(E.h)
        inst.then_inc(E.sem.h, 1)
        E.n += 1
        E.sem.total = E.n
        for b in writes:
            b.w = (E, E.n)
            b.rd = {}
            b.dw = set()
            b.dr = set()
        for b in reads:
            if b not in writes:
                b.rd[E] = E.n

    def dma(self, q, out, in_, dst, src, **kw):
        Q = self.engs[q]
        need = {}

        def add(sem, val):
            if need.get(sem, 0) < val:
                need[sem] = val
        if src.w is not None:
            add(src.w[0].sem, src.w[1])
        for s in src.dw:
            add(s, s.total)
        if not dst.dram:
            if dst.w is not None:
                add(dst.w[0].sem, dst.w[1])
            for s in dst.dw:
                add(s, s.total)
            for F, n in dst.rd.items():
                add(F.sem, n)
            for s in dst.dr:
                add(s, s.total)
        self._emit_waits(Q, need)
        sem = self._get_sem(src if dst.dram else dst)
        inst = Q.h.dma_start(out=out, in_=in_, **kw)
        inst.then_inc(sem.h, 16)
        sem.total += 16
        if dst.dram:
            dst.dw.add(sem)
        else:
            dst.w = None
            dst.rd = {}
            dst.dr = set()
            dst.dw = {sem}
        if not src.dram:
            src.dr.add(sem)

    def barrier(self):
        for E in self.engs.values():
            need = {}
            for Fe in self.engs.values():
                if Fe is not E and Fe.n > 0:
                    need[Fe.sem] = Fe.n
            for s in self.dsems:
                if s.total > 0:
                    need[s] = s.total
            self._emit_waits(E, need)


class T:
    def __init__(self, h, name, strict=False):
        self.h = h
        self.b = Buf(name)
        self.b.strict = strict

    def __getitem__(self, k):
        return self.h[k]


def build(cfg):
    L = cfg["L"]
    LP = cfg["LP"]
    kinds = cfg["kinds"]
    NL = len(kinds)
    R = L // 64
    RB = LP // 64
    NT = L // 128
    NB = L // 512
    nc = bass.Bass("TRN2", target_bir_lowering=False)

    def din(name, shape, dt=F32):
        return nc.dram_tensor(name, list(shape), dt, kind="ExternalInput").ap()

    def dsc(name, shape, dt):
        return nc.dram_tensor(name, list(shape), dt, kind="Internal").ap()

    nA = sum(1 for k in kinds if k == 0)
    nB = sum(1 for k in kinds if k == 1)
    nC = sum(1 for k in kinds if k == 2)
    x_in = din("x", [L, D])
    cT = din("cT", [128, 8])
    w_mod = din("w_mod", [NL, D, 6 * D])
    b_mod = din("b_mod", [NL, 6 * D])
    norm_g = din("norm_g", [NL, 2, D])
    final_g = din("final_g", [D])
    a_w_qkv = din("a_w_qkv", [max(nA, 1), D, 3 * D])
    a_biasT = din("a_biasT", [max(nA, 1), 16, 64, 15 * 64])
    a_w_o = din("a_w_o", [max(nA, 1), D, D])
    b_w_qkv = din("b_w_qkv", [max(nB, 1), D, 9 * D])
    b_w_o = din("b_w_o", [max(nB, 1), D, D])
    c_w_qkv = din("c_w_qkv", [max(nC, 1), D, 1536])
    c_gcol = din("c_gcol", [max(nC, 1), 128, 4])
    c_w_o = din("c_w_o", [max(nC, 1), D, D])
    mlp_w1 = din("mlp_w1", [NL, D, DFF])
    mlp_w2 = din("mlp_w2", [NL, DFF, D])
    ident_in = din("ident", [128, 128])
    cosB = din("cosB", [128, L])
    sinB = din("sinB", [128, L])
    cosC = din("cosC", [128, L])
    sinC = din("sinC", [128, L])
    bandm = din("bandm", [128, 256])
    NQT = L // 128
    bbias = din("bbias", [128, 3 * NQT * 2])
    cbias = din("cbias", [128, NT])
    flags = din("flags", [128, 2])
    y_out = nc.dram_tensor("y", [L, D], F32, kind="ExternalOutput").ap()

    xA = dsc("xA", [L, D], F32)
    xB = dsc("xB", [L, D], F32)
    modD = dsc("modD", [NL, 6, D], F32)
    QT = [dsc("QT%d" % g, [D, L], BF16) for g in range(3)]
    KT = [dsc("KT%d" % g, [D, L + 2 * PADK], BF16) for g in range(3)]
    VX = [dsc("VX%d" % g, [L + 2 * PADK, 1040], BF16) for g in range(3)]
    ON = [dsc("ON%d" % g, [L, 1040], F32) for g in range(3)]
    VXC = dsc("VXC", [L, 258], BF16)
    ONC = dsc("ONC", [L, 1032], F32)

    with ExitStack() as top:
        k = K(nc, top)
        dbuf = {}

        def DB(ap_name):
            if ap_name not in dbuf:
                dbuf[ap_name] = Buf(ap_name, dram=True)
            return dbuf[ap_name]
        IN = DB("inputs")

        uid = [0]

        def sb(stack, name, shape, dt=F32, strict=False):
            uid[0] += 1
            nm = "t%d_%s" % (uid[0], name)
            return T(stack.enter_context(nc.sbuf_tensor(nm, list(shape), dt)), nm, strict)

        ps = [T(top.enter_context(nc.psum_tensor("ps%d" % i, [128, 512], F32)), "ps%d" % i) for i in range(8)]
        ident = sb(top, "ident", [128, 128])
        k.dma("sp", ident[:], ident_in[:, :], ident.b, IN)
        ones_f = sb(top, "ones_f", [128, 128])
        k.op("dve", lambda h: h.memset(ones_f[:], 1.0), writes=[ones_f.b])
        epsc = sb(top, "epsc", [128, 1])
        k.op("dve", lambda h: h.memset(epsc[:], EPS), writes=[epsc.b])
        flg = sb(top, "flg", [128, 2])
        k.dma("sp", flg[:], flags[:, :], flg.b, IN)

        with ExitStack() as st:
            zt = sb(st, "zt", [128, 1040], BF16)
            k.op("dve", lambda h: h.memset(zt[:], 0.0), writes=[zt.b])
            for g in range(3):
                for side in range(2):
                    c0 = side * (PADK + L)
                    for ch in range(8):
                        k.dma("pool", KT[g][ch * 128:(ch + 1) * 128, c0:c0 + PADK], zt[:, 0:PADK],
                              DB("KT%d" % g), zt.b)
                        k.dma("pool", VX[g][c0 + ch * 128:c0 + (ch + 1) * 128, :], zt[:, :],
                              DB("VX%d" % g), zt.b)
            sc = sb(st, "sc", [128, 8], strict=True)
            k.dma("sp", sc[:], cT[:, :], sc.b, IN)
            k.op("act", lambda h: h.activation(out=sc[:], in_=sc[:], func=AF.Silu), reads=[sc.b], writes=[sc.b])
            wm = [sb(st, "wm%d" % i, [128, 8, 512]) for i in range(2)]
            modrow = sb(st, "modrow", [1, 6 * D])
            brow = sb(st, "brow", [1, 6 * D])
            grow = sb(st, "grow", [1, 2 * D])
            outrow = sb(st, "outrow", [1, 6 * D])
            for li in range(NL):
                k.dma("sp", brow[:], b_mod[li:li + 1, :], brow.b, IN)
                k.dma("sp", grow[:], norm_g[li:li + 1].rearrange("o t d -> o (t d)"), grow.b, IN)
                for n in range(12):
                    w = wm[n % 2]
                    k.dma("sp", w[:], w_mod[li].rearrange("(k p) n -> p k n", p=128)[:, :, n * 512:(n + 1) * 512], w.b, IN)
                    p = ps[n % 2]
                    for kk in range(8):
                        k.op("pe", lambda h, kk=kk, w=w, p=p: h.matmul(p[0:1, :], lhsT=sc[:, kk:kk + 1], rhs=w[:, kk, :],
                                                                     start=(kk == 0), stop=(kk == 7)),
                             reads=[sc.b, w.b], writes=[p.b])
                    k.op("dve", lambda h, n=n, p=p: h.tensor_tensor(out=modrow[0:1, n * 512:(n + 1) * 512], in0=p[0:1, :],
                                                                  in1=brow[0:1, n * 512:(n + 1) * 512], op=ALU.add),
                         reads=[p.b, brow.b], writes=[modrow.b])
                for sub in range(2):
                    o = sub * 3 * D
                    k.op("dve", lambda h, o=o, sub=sub: h.scalar_tensor_tensor(
                        out=outrow[0:1, o:o + D], in0=modrow[0:1, o + D:o + 2 * D], scalar=1.0,
                        in1=grow[0:1, sub * D:(sub + 1) * D], op0=ALU.add, op1=ALU.mult),
                        reads=[modrow.b, grow.b], writes=[outrow.b])
                    k.op("dve", lambda h, o=o: h.tensor_copy(out=outrow[0:1, o + D:o + 2 * D], in_=modrow[0:1, o:o + D]),
                         reads=[modrow.b], writes=[outrow.b])
                    k.op("dve", lambda h, o=o: h.tensor_copy(out=outrow[0:1, o + 2 * D:o + 3 * D], in_=modrow[0:1, o + 2 * D:o + 3 * D]),
                         reads=[modrow.b], writes=[outrow.b])
                k.dma("pool", modD[li:li + 1].rearrange("o s d -> o (s d)"), outrow[:], DB("modD"), outrow.b)
            k.barrier()

        def load_cols(t, li, kind):
            src = bass.AP(tensor=modD.tensor, offset=(li * 6 + kind) * D, ap=[[1, 128], [128, 8]])
            k.dma("sp", t[:], src, t.b, DB("modD"), allow_slow_non_contiguous=True)

        def load_bcast(t, li, kind):
            src = bass.AP(tensor=modD.tensor, offset=(li * 6 + kind) * D, ap=[[0, 128], [1, D]])
            k.dma("sp", t[:], src, t.b, DB("modD"))

        class Norm:
            def __init__(self, st, li, sub, alloc_xt=True):
                self.xt = [sb(st, "xt%d" % i, [128, D]) for i in range(2)] if alloc_xt else None
                self.junk = sb(st, "junk", [128, D], BF16)
                self.xn = [sb(st, "xn%d" % i, [128, D]) for i in range(2)]
                self.ssq = [sb(st, "ssq%d" % i, [128, 1], strict=True) for i in range(2)]
                self.Ac = sb(st, "Ac", [128, 8])
                self.Bc = sb(st, "Bc", [128, 8])
                load_cols(self.Ac, li, sub * 3 + 0)
                load_cols(self.Bc, li, sub * 3 + 1)
                self.i = 0

            def rstd_of(self, xt, ssq, n_el):
                junk = self.junk
                k.op("act", lambda h: h.activation(out=junk[:], in_=xt[:], func=AF.Square, accum_out=ssq[:]),
                     reads=[xt.b], writes=[junk.b, ssq.b])
                k.op("act", lambda h: h.activation(out=ssq[:], in_=ssq[:], func=AF.Sqrt, scale=1.0 / n_el, bias=epsc[:, 0:1]),
                     reads=[ssq.b, epsc.b], writes=[ssq.b])
                k.op("dve", lambda h: h.reciprocal(out=ssq[:], in_=ssq[:]), reads=[ssq.b], writes=[ssq.b])

            def run(self, xsrc_ap, xsrc_buf, hT, col0, keep=None):
                i = self.i
                self.i += 1
                xt = keep if keep is not None else self.xt[i % 2]
                xn, ssq = self.xn[i % 2], self.ssq[i % 2]
                k.dma("sp", xt[:], xsrc_ap, xt.b, xsrc_buf)
                self.rstd_of(xt, ssq, D)
                k.op("act", lambda h: h.activation(out=xn[:], in_=xt[:], func=AF.Copy, scale=ssq[:, 0:1]),
                     reads=[xt.b, ssq.b], writes=[xn.b])
                for half in range(2):
                    p = ps[6 + half]
                    for j in range(4):
                        kk = half * 4 + j
                        k.op("pe", lambda h, kk=kk, j=j, p=p: h.transpose(p[:, j * 128:(j + 1) * 128], xn[:, kk * 128:(kk + 1) * 128], ident[:]),
                             reads=[xn.b, ident.b], writes=[p.b])
                    for j in range(4):
                        kk = half * 4 + j
                        k.op("act", lambda h, kk=kk, j=j, p=p: h.activation(
                            out=hT[:, kk, col0:col0 + 128], in_=p[:, j * 128:(j + 1) * 128], func=AF.Identity,
                            scale=self.Ac[:, kk:kk + 1], bias=self.Bc[:, kk:kk + 1]),
                            reads=[p.b, self.Ac.b, self.Bc.b], writes=[hT.b])

        def load_w_bf16(t, w_ap, ncols, kchunks=8, piece=1024):
            v = w_ap.rearrange("(k p) n -> p k n", p=128)
            for c0 in range(0, ncols, piece):
                c1 = min(ncols, c0 + piece)
                for k0 in range(0, kchunks, 8):
                    k.dma("pool", t[:, k0:k0 + 8, c0:c1], v[:, k0:k0 + 8, c0:c1], t.b, IN)

        def phase1(li, x_ap, x_buf, kind, j):
            groups = 3 if kind == 1 else 1
            for g in range(groups):
                with ExitStack() as st:
                    if kind == 0:
                        wsrc, nq, nk, nv, hd = a_w_qkv[j], 1024, 1024, 1024, 64
                    elif kind == 1:
                        wsrc, nq, nk, nv, hd = b_w_qkv[j][:, g * 3072:(g + 1) * 3072], 1024, 1024, 1024, 64
                    else:
                        wsrc, nq, nk, nv, hd = c_w_qkv[j], 1024, 256, 256, 128
                    ncol = nq + nk + nv
                    nqk = nq + nk
                    W = sb(st, "W", [128, 8, ncol], BF16)
                    load_w_bf16(W, wsrc, ncol)
                    rope = kind != 0
                    if rope:
                        Wp = sb(st, "Wp", [128, 8, nqk], BF16)
                        for kk in range(8):
                            for (base, ncs) in ((0, nq), (nq, nk)):
                                gi = ncs // 256
                                vi = W[:, kk, base:base + ncs].rearrange("p (g h t j) -> p g h t j", g=gi, h=4, t=2, j=32)
                                vo = Wp[:, kk, base:base + ncs].rearrange("p (g t h j) -> p g t h j", g=gi, h=4, t=2, j=32)
                                for t in range(2):
                                    k.op("dve" if t == 0 else "pool", lambda h, vi=vi, vo=vo, t=t: h.tensor_copy(out=vo[:, :, t, :, :], in_=vi[:, :, :, t, :]),
                                         reads=[W.b], writes=[Wp.b])
                        if kind == 2:
                            gcol = sb(st, "gcol", [128, 4])
                            k.dma("sp", gcol[:], c_gcol[j], gcol.b, IN)
                            Mblk = sb(st, "Mblk", [128, 128])
                            k.op("dve", lambda h: h.memset(Mblk[:], 0.0), writes=[Mblk.b])
                            k.op("dve", lambda h: h.memset(Mblk[0:64, 0:64], 1.0), writes=[Mblk.b])
                            k.op("dve", lambda h: h.memset(Mblk[64:128, 64:128], 1.0), writes=[Mblk.b])
                    nrm = Norm(st, li, 0)
                    hT = [sb(st, "hT%d" % i, [128, 8, 512], BF16) for i in range(2)]
                    qst = [sb(st, "qst%d" % i, [128, 512], BF16) for i in range(4)]
                    H = 16 if kind != 2 else 2
                    vst = [sb(st, "vst%d" % i, [128, H, hd + 1], BF16) for i in range(2)]
                    for v in vst:
                        k.op("dve", lambda h, v=v: h.memset(v[:], 1.0), writes=[v.b])
                    if rope:
                        cs = [sb(st, "cs%d" % i, [128, 512]) for i in range(2)]
                        sn = [sb(st, "sn%d" % i, [128, 512]) for i in range(2)]
                        tA = [sb(st, "tA%d" % i, [128, 512]) for i in range(2)]
                        tB = [sb(st, "tB%d" % i, [128, 512]) for i in range(2)]
                        tC = [sb(st, "tC%d" % i, [128, 512]) for i in range(2)]
                        tD = [sb(st, "tD%d" % i, [128, 512]) for i in range(2)]
                        if kind == 2:
                            tabs = [[sb(st, "tab%d_%d" % (i, n_), [128, 512]) for n_ in range(8)] for i in range(2)]
                            sqa = [sb(st, "sqa%d" % i, [128, 512]) for i in range(2)]
                            sqb = [sb(st, "sqb%d" % i, [128, 512]) for i in range(2)]
                            rs = [sb(st, "rs%d" % i, [128, 512]) for i in range(2)]
                    qi = 0
                    vi_ = 0
                    KTg, QTg = KT[g], QT[g]
                    def norm_block(b):
                        for tt in range(4):
                            t0 = b * 512 + tt * 128
                            nrm.run(x_ap[t0:t0 + 128, :], x_buf, hT[b % 2], tt * 128)
                    norm_block(0)
                    for b in range(NB):
                        h_ = hT[b % 2]
                        if not rope:
                            for c in range(nqk // 128):
                                isq = c < nq // 128
                                pq = ps[qi % 4]
                                for kk in range(8):
                                    k.op("pe", lambda h, kk=kk, c=c, pq=pq: h.matmul(pq[:, :], lhsT=W[:, kk, c * 128:(c + 1) * 128], rhs=h_[:, kk, :],
                                                                                   start=(kk == 0), stop=(kk == 7)),
                                         reads=[W.b, h_.b], writes=[pq.b])
                                q_ = qst[qi % 4]
                                if qi % 2:
                                    k.op("act", lambda h, q_=q_, pq=pq: h.copy(out=q_[:], in_=pq[:, :]), reads=[pq.b], writes=[q_.b])
                                else:
                                    k.op("dve", lambda h, q_=q_, pq=pq: h.tensor_copy(out=q_[:], in_=pq[:, :]), reads=[pq.b], writes=[q_.b])
                                if isq:
                                    k.dma("pool", QTg[c * 128:(c + 1) * 128, b * 512:(b + 1) * 512], q_[:], DB("QT%d" % g), q_.b)
                                else:
                                    ck = c - nq // 128
                                    k.dma("pool", KTg[ck * 128:(ck + 1) * 128, PADK + b * 512:PADK + (b + 1) * 512], q_[:], DB("KT%d" % g), q_.b)
                                qi += 1
                        else:
                            c_, s_ = cs[b % 2], sn[b % 2]
                            ctab, stab = (cosB, sinB) if kind == 1 else (cosC, sinC)
                            k.dma("sp", c_[:], ctab[:, b * 512:(b + 1) * 512], c_.b, IN)
                            k.dma("sp", s_[:], stab[:, b * 512:(b + 1) * 512], s_.b, IN)
                            if kind == 2:
                                tb_ = tabs[b % 2]
                                spec = [(c_, 0), (s_, 1), (s_, 0), (c_, 1), (c_, 2), (s_, 3), (s_, 2), (c_, 3)]
                                for n_, (src_, gc) in enumerate(spec):
                                    if n_ % 2:
                                        k.op("act", lambda h, n_=n_, src_=src_, gc=gc: h.activation(
                                            out=tb_[n_][:], in_=src_[:], func=AF.Copy, scale=gcol[:, gc:gc + 1]),
                                            reads=[src_.b, gcol.b], writes=[tb_[n_].b])
                                    else:
                                        k.op("dve", lambda h, n_=n_, src_=src_, gc=gc: h.tensor_scalar(
                                            out=tb_[n_][:], in0=src_[:], scalar1=gcol[:, gc:gc + 1], scalar2=None, op0=ALU.mult),
                                            reads=[src_.b, gcol.b], writes=[tb_[n_].b])
                            for pi in range(nqk // 256):
                                isq = pi < nq // 256
                                pl = pi if isq else pi - nq // 256
                                cb0 = pi * 256
                                pA = ps[(qi % 2) * 2]
                                pB = ps[(qi % 2) * 2 + 1]
                                for (pp, off) in ((pA, 0), (pB, 128)):
                                    for kk in range(8):
                                        k.op("pe", lambda h, kk=kk, pp=pp, off=off: h.matmul(pp[:, :], lhsT=Wp[:, kk, cb0 + off:cb0 + off + 128], rhs=h_[:, kk, :],
                                                                                            start=(kk == 0), stop=(kk == 7)),
                                             reads=[Wp.b, h_.b], writes=[pp.b])
                                a1, a2, a3, a4 = tA[qi % 2], tB[qi % 2], tC[qi % 2], tD[qi % 2]
                                q1, q2 = qst[(2 * qi) % 4], qst[(2 * qi + 1) % 4]
                                if kind == 1:
                                    m1, m2, m3, m4 = c_, s_, s_, c_
                                else:
                                    o8 = 0 if isq else 4
                                    m1, m2, m3, m4 = tb_[o8 + 0], tb_[o8 + 1], tb_[o8 + 2], tb_[o8 + 3]
                                    s2a, s2b, r2 = sqa[qi % 2], sqb[qi % 2], rs[qi % 2]
                                    k.op("act", lambda h: h.activation(out=s2a[:], in_=pA[:, :], func=AF.Square), reads=[pA.b], writes=[s2a.b])
                                    k.op("act", lambda h: h.activation(out=s2b[:], in_=pB[:, :], func=AF.Square), reads=[pB.b], writes=[s2b.b])
                                    pss = ps[4]
                                    k.op("pe", lambda h: h.matmul(pss[:, :], lhsT=Mblk[:], rhs=s2a[:], start=True, stop=False),
                                         reads=[Mblk.b, s2a.b], writes=[pss.b])
                                    k.op("pe", lambda h: h.matmul(pss[:, :], lhsT=Mblk[:], rhs=s2b[:], start=False, stop=True),
                                         reads=[Mblk.b, s2b.b], writes=[pss.b])
                                    k.op("act", lambda h: h.activation(out=r2[:], in_=pss[:, :], func=AF.Sqrt, scale=1.0 / 128, bias=epsc[:, 0:1]),
                                         reads=[pss.b, epsc.b], writes=[r2.b])
                                    k.op("dve", lambda h: h.reciprocal(out=r2[:], in_=r2[:]), reads=[r2.b], writes=[r2.b])
                                k.op("dve", lambda h: h.tensor_tensor(out=a1[:], in0=pA[:, :], in1=m1[:], op=ALU.mult), reads=[pA.b, m1.b], writes=[a1.b])
                                k.op("dve", lambda h: h.tensor_tensor(out=a2[:], in0=pB[:, :], in1=m2[:], op=ALU.mult), reads=[pB.b, m2.b], writes=[a2.b])
                                k.op("dve", lambda h: h.tensor_tensor(out=a3[:], in0=pA[:, :], in1=m3[:], op=ALU.mult), reads=[pA.b, m3.b], writes=[a3.b])
                                k.op("dve", lambda h: h.tensor_tensor(out=a4[:], in0=pB[:, :], in1=m4[:], op=ALU.mult), reads=[pB.b, m4.b], writes=[a4.b])
                                if kind == 1:
                                    k.op("dve", lambda h: h.tensor_tensor(out=q1[:], in0=a1[:], in1=a2[:], op=ALU.subtract), reads=[a1.b, a2.b], writes=[q1.b])
                                    k.op("pool", lambda h: h.tensor_tensor(out=q2[:], in0=a3[:], in1=a4[:], op=ALU.add), reads=[a3.b, a4.b], writes=[q2.b])
                                else:
                                    k.op("dve", lambda h: h.tensor_tensor(out=a1[:], in0=a1[:], in1=a2[:], op=ALU.subtract), reads=[a1.b, a2.b], writes=[a1.b])
                                    k.op("pool", lambda h: h.tensor_tensor(out=q1[:], in0=a1[:], in1=r2[:], op=ALU.mult), reads=[a1.b, r2.b], writes=[q1.b])
                                    k.op("dve", lambda h: h.tensor_tensor(out=a3[:], in0=a3[:], in1=a4[:], op=ALU.add), reads=[a3.b, a4.b], writes=[a3.b])
                                    k.op("pool", lambda h: h.tensor_tensor(out=q2[:], in0=a3[:], in1=r2[:], op=ALU.mult), reads=[a3.b, r2.b], writes=[q2.b])
                                for (qq, off) in ((q1, 0), (q2, 128)):
                                    r0_ = pl * 256 + off
                                    if isq:
                                        k.dma("pool", QTg[r0_:r0_ + 128, b * 512:(b + 1) * 512], qq[:], DB("QT%d" % g), qq.b)
                                    else:
                                        k.dma("pool", KTg[r0_:r0_ + 128, PADK + b * 512:PADK + (b + 1) * 512], qq[:], DB("KT%d" % g), qq.b)
                                qi += 1
                        if b + 1 < NB:
                            norm_block(b + 1)
                        for tt in range(4):
                            v_ = vst[vi_ % 2]
                            vi_ += 1
                            t0 = b * 512 + tt * 128
                            for cc in range(max(1, nv // 512)):
                                w_ = min(512, nv)
                                pv = ps[5] if kind == 2 else ps[4 + (2 * tt + cc) % 2]
                                for kk in range(8):
                                    k.op("pe", lambda h, kk=kk, cc=cc, pv=pv, w_=w_: h.matmul(
                                        pv[:, 0:w_], lhsT=h_[:, kk, tt * 128:(tt + 1) * 128], rhs=W[:, kk, nqk + cc * 512:nqk + cc * 512 + w_],
                                        start=(kk == 0), stop=(kk == 7)), reads=[W.b, h_.b], writes=[pv.b])
                                nh = w_ // hd
                                k.op("act", lambda h, v_=v_, pv=pv, cc=cc, nh=nh, w_=w_: h.copy(
                                    out=v_[:, cc * nh:(cc + 1) * nh, 0:hd], in_=pv[:, 0:w_].rearrange("p (h d) -> p h d", d=hd)),
                                    reads=[pv.b], writes=[v_.b])
                            if kind == 2:
                                k.dma("pool", VXC[t0:t0 + 128, :], v_[:].rearrange("p h e -> p (h e)"), DB("VXC"), v_.b)
                            else:
                                k.dma("pool", VX[g][PADK + t0:PADK + t0 + 128, :], v_[:].rearrange("p h e -> p (h e)"), DB("VX%d" % g), v_.b)
                    k.barrier()

        def load_split(t, src, cols, blocks, dbname):
            for i, blk in enumerate(blocks):
                pi, b4 = blk // 4, blk % 4
                for two in range(2):
                    r0_ = pi * 256 + two * 128 + b4 * 32
                    k.dma("sp", t[i * 64 + two * 32:i * 64 + two * 32 + 32, :], src[r0_:r0_ + 32, cols], t.b, DB(dbname))

        def pipeline(items, stage1, stage2, depth):
            n = len(items)
            for i in range(min(depth, n)):
                stage1(items[i])
            for i in range(n):
                if i + depth < n:
                    stage1(items[i + depth])
                stage2(items[i])

        def attn_A(j):
            scale = 64 ** -0.5
            R2 = R // 2
            with ExitStack() as st:
                nbuf = 2 if L <= 4096 else 1
                KTc = [sb(st, "KTc%d" % i, [128, L], BF16) for i in range(nbuf)] * (2 // nbuf)
                QTc = [sb(st, "QTc%d" % i, [128, L], BF16) for i in range(nbuf)] * (2 // nbuf)
                VpE = [sb(st, "VpE%d" % i, [128, R2, 130], BF16) for i in range(nbuf)] * (2 // nbuf)
                VpO = [sb(st, "VpO%d" % i, [128, R2, 130], BF16) for i in range(nbuf)] * (2 // nbuf)
                bt = sb(st, "bt", [128, 2, 14 * 64])
                Et = [sb(st, "Et%d" % i, [128, 2, 14 * 64], BF16) for i in range(2)]
                pt = [sb(st, "pt%d" % i, [128, 6 * 64], BF16) for i in range(4)]
                ost = [sb(st, "ost%d" % i, [64, 2, 65]) for i in range(3)]
                itc = [0]
                for pr in range(8):
                    kt_, qt_, ve_, vo_, et_ = KTc[pr % 2], QTc[pr % 2], VpE[pr % 2], VpO[pr % 2], Et[pr % 2]
                    k.dma("sp", kt_[:], KT[0][pr * 128:(pr + 1) * 128, PADK:PADK + L], kt_.b, DB("KT0"))
                    k.dma("sp", qt_[:], QT[0][pr * 128:(pr + 1) * 128, :], qt_.b, DB("QT0"))
                    vsE = VX[0][PADK:PADK + L, pr * 130:(pr + 1) * 130].rearrange("(i p) e -> p i e", p=128)
                    vsO = VX[0][PADK + 64:PADK + 64 + L, pr * 130:(pr + 1) * 130].rearrange("(i p) e -> p i e", p=128)
                    for r0 in range(0, R2, 8):
                        k.dma("sp", ve_[:, r0:r0 + 8, :], vsE[:, r0:r0 + 8, :], ve_.b, DB("VX0"))
                        k.dma("sp", vo_[:, r0:r0 + 8, :], vsO[:, r0:r0 + 8, :], vo_.b, DB("VX0"))
                    for hh in range(2):
                        k.dma("sp", bt[0:64, hh, :], a_biasT[j, pr * 2 + hh][:, 0:14 * 64], bt.b, IN)
                        k.dma("sp", bt[64:128, hh, :], a_biasT[j, pr * 2 + hh][:, 64:15 * 64], bt.b, IN)
                    k.op("act", lambda h, et_=et_: h.activation(out=et_[:], in_=bt[:], func=AF.Exp), reads=[bt.b], writes=[et_.b])
                    items = []
                    for r in range(R):
                        rs0 = min(max(r - 4, 0), R - 8)
                        S = list(range(rs0, rs0 + 8))
                        tags = {kr: 0 for kr in S}
                        if RB < R and RB - 3 <= r <= RB - 1:
                            P = list(range(RB - 8, RB))
                            for kr in S:
                                if kr not in P:
                                    tags[kr] = 1
                            for kr in P:
                                if kr not in tags:
                                    tags[kr] = 2
                        krs = sorted(tags)
                        n = len(krs)
                        assert krs == list(range(krs[0], krs[0] + n)) and n <= 11
                        if n % 2:
                            tags[krs[-1] + 1] = 3
                            krs = krs + [krs[-1] + 1]
                            n += 1
                            assert krs[-1] < R
                        npair = n // 2
                        dr0 = krs[0] - r + 7
                        assert 0 <= dr0 and dr0 + 2 * (npair - 1) <= 13
                        for hh in range(2):
                            it = itc[0]
                            itc[0] += 1
                            items.append(dict(r=r, hh=hh, krs=krs, npair=npair, dr0=dr0, tags=tags, p_s=ps[it % 3],
                                              p_o=ps[3 + it % 2], p_=pt[it % 4], o_=ost[(it // 2) % 3]))

                    def stage1(d):
                        r, hh, krs, npair, dr0, tags, p_s, p_ = d["r"], d["hh"], d["krs"], d["npair"], d["dr0"], d["tags"], d["p_s"], d["p_"]
                        for i in range(npair):
                            a_ = krs[2 * i]
                            k.op("pe", lambda h: h.matmul(
                                p_s[:, i * 64:(i + 1) * 64], lhsT=kt_[hh * 64:(hh + 1) * 64, a_ * 64:a_ * 64 + 128],
                                rhs=qt_[hh * 64:(hh + 1) * 64, r * 64:(r + 1) * 64], start=True, stop=True),
                                reads=[kt_.b, qt_.b], writes=[p_s.b])
                        k.op("act", lambda h: h.activation(out=p_[:, 0:npair * 64], in_=p_s[:, 0:npair * 64], func=AF.Exp, scale=scale),
                             reads=[p_s.b], writes=[p_.b])
                        ev = et_[:, hh, :].rearrange("p (d q) -> p d q", q=64)[:, dr0:dr0 + 2 * (npair - 1) + 1:2, :]
                        k.op("dve", lambda h: h.tensor_tensor(
                            out=p_[:, 0:npair * 64].rearrange("p (i q) -> p i q", q=64), in0=p_[:, 0:npair * 64].rearrange("p (i q) -> p i q", q=64),
                            in1=ev, op=ALU.mult), reads=[p_.b, et_.b], writes=[p_.b])
                        for jj, kr in enumerate(krs):
                            tg = tags[kr]
                            if tg:
                                i, half = jj // 2, jj % 2
                                blk = p_[half * 64:(half + 1) * 64, i * 64:(i + 1) * 64]
                                if tg == 3:
                                    k.op("dve", lambda h: h.memset(blk, 0.0), writes=[p_.b])
                                else:
                                    fc = flg[half * 64:(half + 1) * 64, tg - 1:tg]
                                    k.op("dve", lambda h: h.tensor_scalar(out=blk, in0=blk, scalar1=fc, scalar2=None, op0=ALU.mult),
                                         reads=[p_.b, flg.b], writes=[p_.b])

                    def stage2(d):
                        r, hh, krs, npair, p_o, p_, o_ = d["r"], d["hh"], d["krs"], d["npair"], d["p_o"], d["p_"], d["o_"]
                        for i in range(npair):
                            a_ = krs[2 * i]
                            vt = ve_[:, a_ // 2, hh * 65:(hh + 1) * 65] if a_ % 2 == 0 else vo_[:, (a_ - 1) // 2, hh * 65:(hh + 1) * 65]
                            vb_ = ve_.b if a_ % 2 == 0 else vo_.b
                            k.op("pe", lambda h: h.matmul(p_o[0:64, 0:65], lhsT=p_[:, i * 64:(i + 1) * 64], rhs=vt,
                                                          start=(i == 0), stop=(i == npair - 1)), reads=[p_.b, vb_], writes=[p_o.b])
                        k.op("act", lambda h: h.copy(out=o_[:, hh, :], in_=p_o[0:64, 0:65]), reads=[p_o.b], writes=[o_.b])
                        if hh == 1:
                            k.dma("pool", ON[0][r * 64:(r + 1) * 64, pr * 130:(pr + 1) * 130], o_[:].rearrange("p h e -> p (h e)"), DB("ON0"), o_.b)
                    pipeline(items, stage1, stage2, 2)
                k.barrier()

        def attn_B(j):
            scale = 64 ** -0.5
            with ExitStack() as st:
                KTc = [sb(st, "KTc%d" % i, [128, L + 2 * PADK], BF16) for i in range(2)]
                QTc = [sb(st, "QTc%d" % i, [128, L], BF16) for i in range(2)]
                VH = (NT + 2) // 2
                Vall = [sb(st, "Vall%d" % i, [128, VH, 130], BF16) for i in range(2)]
                band = sb(st, "band", [128, 256], BF16)
                bandf = sb(st, "bandf", [128, 256])
                k.dma("sp", bandf[:], bandm[:, :], bandf.b, IN)
                k.op("dve", lambda h: h.tensor_copy(out=band[:], in_=bandf[:]), reads=[bandf.b], writes=[band.b])
                bb = sb(st, "bb", [128, 3 * NQT * 2])
                k.dma("sp", bb[:], bbias[:, :], bb.b, IN)
                pt = [sb(st, "pt%d" % i, [128, 256], BF16) for i in range(4)]
                NQB = 4
                ost = [sb(st, "ost%d" % i, [128, NQB, 130]) for i in range(3)]
                itc = [0]
                ci = 0
                rc = [0]
                sgc = [0]
                for pr in range(8):
                    for g, dil in enumerate((1, 4, 16)):
                        kt_, qt_ = KTc[ci % 2], QTc[ci % 2]
                        ci += 1
                        load_split(kt_, KT[g], slice(0, L + 2 * PADK), [2 * pr, 2 * pr + 1], "KT%d" % g)
                        load_split(qt_, QT[g], slice(0, L), [2 * pr, 2 * pr + 1], "QT%d" % g)
                        Mc = L // dil
                        nmb = Mc // 128
                        nqb = min(NQB, nmb)
                        items = []
                        for r in range(dil):
                            if dil == 1:
                                vmap = lambda s_: (Vall[0], s_) if s_ < VH else (Vall[1], s_ - VH)
                            else:
                                vb = Vall[rc[0] % 2]
                                rc[0] += 1
                                vmap = lambda s_, vb=vb: (vb, s_)
                            for mb in range(nmb):
                                if mb % nqb == 0:
                                    sgc[0] += 1
                                for hh in range(2):
                                    it = itc[0]
                                    itc[0] += 1
                                    items.append(dict(r=r, mb=mb, hh=hh, qtid=r * nmb + mb, vmap=vmap, o_=ost[sgc[0] % 3],
                                                      p_s=ps[it % 3], p_o=ps[3 + it % 2], p_=pt[it % 4]))

                        def load_v(r, vmap):
                            nt1 = nmb + 1
                            s0 = 0
                            while s0 < nt1:
                                vb, loc = vmap(s0)
                                n_ = min(8, nt1 - s0, VH - loc)
                                tok0 = PADK + dil * (128 * s0 - 64) + r
                                vsrc = bass.AP(tensor=VX[g].tensor, offset=tok0 * 1040 + pr * 130,
                                               ap=[[dil * 1040, 128], [128 * dil * 1040, n_], [1, 130]])
                                k.dma("sp", vb[:, loc:loc + n_, :], vsrc, vb.b, DB("VX%d" % g))
                                s0 += n_

                        def stage1(d):
                            r, mb, hh, qtid, p_s, p_ = d["r"], d["mb"], d["hh"], d["qtid"], d["p_s"], d["p_"]
                            if hh == 0 and mb == 0:
                                load_v(r, d["vmap"])
                            q0 = dil * 128 * mb + r
                            qap = qt_[hh * 64:(hh + 1) * 64, q0:q0 + 127 * dil + 1:dil]
                            for slot in range(2):
                                k0 = PADK + dil * (128 * mb - 64 + 128 * slot) + r
                                kap = kt_[hh * 64:(hh + 1) * 64, k0:k0 + 127 * dil + 1:dil]
                                k.op("pe", lambda h: h.matmul(p_s[:, slot * 128:(slot + 1) * 128], lhsT=kap, rhs=qap, start=True, stop=True),
                                     reads=[kt_.b, qt_.b], writes=[p_s.b])
                            for slot in range(2):
                                bc = bb[:, (g * NQT + qtid) * 2 + slot:(g * NQT + qtid) * 2 + slot + 1]
                                k.op("act", lambda h: h.activation(
                                    out=p_[:, slot * 128:(slot + 1) * 128], in_=p_s[:, slot * 128:(slot + 1) * 128],
                                    func=AF.Exp, scale=scale, bias=bc), reads=[p_s.b, bb.b], writes=[p_.b])
                            k.op("dve", lambda h: h.tensor_tensor(out=p_[:], in0=p_[:], in1=band[:], op=ALU.mult),
                                 reads=[p_.b, band.b], writes=[p_.b])

                        def stage2(d):
                            r, mb, hh, p_o, p_, o_ = d["r"], d["mb"], d["hh"], d["p_o"], d["p_"], d["o_"]
                            for slot in range(2):
                                vb, loc = d["vmap"](mb + slot)
                                k.op("pe", lambda h: h.matmul(
                                    p_o[:, 0:65], lhsT=p_[:, slot * 128:(slot + 1) * 128], rhs=vb[:, loc, hh * 65:(hh + 1) * 65],
                                    start=(slot == 0), stop=(slot == 1)), reads=[p_.b, vb.b], writes=[p_o.b])
                            qi_ = mb % nqb
                            k.op("act", lambda h: h.copy(out=o_[:, qi_, hh * 65:(hh + 1) * 65], in_=p_o[:, 0:65]), reads=[p_o.b], writes=[o_.b])
                            if hh == 1 and qi_ == nqb - 1:
                                mb0 = mb - (nqb - 1)
                                odst = bass.AP(tensor=ON[g].tensor, offset=(dil * 128 * mb0 + r) * 1040 + pr * 130,
                                               ap=[[dil * 1040, 128], [128 * dil * 1040, nqb], [1, 130]])
                                k.dma("pool", odst, o_[:, 0:nqb, :], DB("ON%d" % g), o_.b)
                        pipeline(items, stage1, stage2, 2)
                k.barrier()

        def attn_C(j):
            scale = 128 ** -0.5
            with ExitStack() as st:
                KTc = sb(st, "KTc", [128, L], BF16)
                Vc = sb(st, "Vc", [128, NT, 129], BF16)
                QTc = [sb(st, "QTc%d" % i, [128, L], BF16) for i in range(2)]
                cb = sb(st, "cb", [128, NT])
                k.dma("sp", cb[:], cbias[:, :], cb.b, IN)
                pt = [sb(st, "pt%d" % i, [128, 512], BF16) for i in range(4)]
                ost = [sb(st, "ost%d" % i, [128, 4, 129]) for i in range(2)]
                itc = [0]
                oic = [0]
                for kh in range(2):
                    load_split(KTc, KT[0], slice(PADK, PADK + L), [2 * kh, 2 * kh + 1], "KT0")
                    vcs = VXC[:, kh * 129:(kh + 1) * 129].rearrange("(t p) e -> p t e", p=128)
                    for t0_ in range(0, NT, 8):
                        k.dma("sp", Vc[:, t0_:t0_ + 8, :], vcs[:, t0_:t0_ + 8, :], Vc.b, DB("VXC"))
                    for qh in range(4):
                        head = kh * 4 + qh
                        qt_ = QTc[head % 2]
                        load_split(qt_, QT[0], slice(0, L), [2 * head, 2 * head + 1], "QT0")
                        items = []
                        for qb in range(NB):
                            for kt in range(NT):
                                it = itc[0]
                                itc[0] += 1
                                items.append(dict(qb=qb, kt=kt, p_s=ps[4 + it % 4], p_=pt[it % 4]))

                        def stage1(d):
                            qb, kt, p_s, p_ = d["qb"], d["kt"], d["p_s"], d["p_"]
                            k.op("pe", lambda h: h.matmul(
                                p_s[:, :], lhsT=KTc[:, kt * 128:(kt + 1) * 128], rhs=qt_[:, qb * 512:(qb + 1) * 512], start=True, stop=True),
                                reads=[KTc.b, qt_.b], writes=[p_s.b])
                            k.op("act", lambda h: h.activation(out=p_[:], in_=p_s[:, :], func=AF.Exp, scale=scale, bias=cb[:, kt:kt + 1]),
                                 reads=[p_s.b, cb.b], writes=[p_.b])

                        def stage2(d):
                            qb, kt, p_ = d["qb"], d["kt"], d["p_"]
                            for jq in range(4):
                                k.op("pe", lambda h: h.matmul(
                                    ps[jq][:, 0:129], lhsT=p_[:, jq * 128:(jq + 1) * 128], rhs=Vc[:, kt, :],
                                    start=(kt == 0), stop=(kt == NT - 1)), reads=[p_.b, Vc.b], writes=[ps[jq].b])
                            if kt == NT - 1:
                                o_ = ost[oic[0] % 2]
                                oic[0] += 1
                                for jq in range(4):
                                    k.op("dve", lambda h: h.tensor_copy(out=o_[:, jq, :], in_=ps[jq][:, 0:129]),
                                         reads=[ps[jq].b], writes=[o_.b])
                                odst = ONC[qb * 512:(qb + 1) * 512, head * 129:(head + 1) * 129].rearrange("(j p) e -> p j e", p=128)
                                k.dma("pool", odst, o_[:], DB("ONC"), o_.b)
                        pipeline(items, stage1, stage2, 2)
                k.barrier()

        def phase2b(li, kind, j, xs_ap, xs_buf, xd_ap, xd_buf):
            with ExitStack() as st:
                narr = 3 if kind == 1 else 1
                H, hd = (8, 128) if kind == 2 else (16, 64)
                Wd = H * (hd + 1)
                wo_src = (a_w_o, b_w_o, c_w_o)[kind][j]
                Wo = sb(st, "Wo", [128, 8, D], BF16)
                load_w_bf16(Wo, wo_src, D)
                G = sb(st, "G", [128, D])
                load_bcast(G, li, 2)
                nb_ = [[sb(st, "nb%d_%d" % (a, i), [128, H, hd + 1]) for a in range(narr)] for i in range(3)]
                rl = [sb(st, "rl%d" % i, [128, H], strict=True) for i in range(3)]
                Of = [sb(st, "Of%d" % i, [128, D]) for i in range(3)]
                oT = [sb(st, "oT%d" % i, [128, 8, 128], BF16) for i in range(3)]
                xt = [sb(st, "xt%d" % i, [128, D]) for i in range(3)]
                tmp = [sb(st, "tmp%d" % i, [128, D]) for i in range(3)]
                def stageA(tt):
                    i = tt % 3
                    t0 = tt * 128
                    for a in range(narr):
                        src = (ONC if kind == 2 else ON[a])[t0:t0 + 128, 0:Wd]
                        k.dma("sp", nb_[i][a][:].rearrange("p h e -> p (h e)"), src, nb_[i][a].b, DB("ONC" if kind == 2 else "ON%d" % a))
                    k.dma("sp", xt[i][:], xs_ap[t0:t0 + 128, :], xt[i].b, xs_buf)
                    acc = nb_[i][0]
                    for a in range(1, narr):
                        k.op("dve", lambda h, acc=acc, o=nb_[i][a]: h.tensor_tensor(out=acc[:], in0=acc[:], in1=o[:], op=ALU.add),
                             reads=[acc.b, nb_[i][a].b], writes=[acc.b])
                    r_ = rl[i]
                    k.op("dve", lambda h, r_=r_, acc=acc: h.tensor_scalar(out=r_[:], in0=acc[:, :, hd], scalar1=1e-30, scalar2=None, op0=ALU.max),
                         reads=[acc.b], writes=[r_.b])
                    k.op("dve", lambda h, r_=r_: h.reciprocal(out=r_[:], in_=r_[:]), reads=[r_.b], writes=[r_.b])
                    o_ = Of[i]
                    rb = bass.AP(tensor=r_[:].tensor, offset=r_[:].offset, ap=[list(r_[:].ap[0]), [1, H], [0, hd]])
                    k.op("dve", lambda h, o_=o_, acc=acc, rb=rb: h.tensor_tensor(
                        out=o_[:].rearrange("p (h d) -> p h d", d=hd), in0=acc[:, :, 0:hd], in1=rb, op=ALU.mult),
                        reads=[acc.b, r_.b], writes=[o_.b])
                    oT_ = oT[i]
                    for half in range(2):
                        p = ps[4 + (2 * tt + half) % 4]
                        for jj in range(4):
                            kk = half * 4 + jj
                            k.op("pe", lambda h, kk=kk, jj=jj, p=p, o_=o_: h.transpose(p[:, jj * 128:(jj + 1) * 128], o_[:, kk * 128:(kk + 1) * 128], ident[:]),
                                 reads=[o_.b, ident.b], writes=[p.b])
                        k.op("act", lambda h, half=half, p=p, oT_=oT_: h.copy(out=oT_[:, half * 4:(half + 1) * 4, :].rearrange("p k t -> p (k t)"), in_=p[:, :]),
                             reads=[p.b], writes=[oT_.b])
                def stageB(tt):
                    i = tt % 3
                    t0 = tt * 128
                    oT_ = oT[i]
                    tm = tmp[i]
                    for half in range(2):
                        p = ps[(2 * tt + half) % 4]
                        for kk in range(8):
                            k.op("pe", lambda h, kk=kk, half=half, p=p, oT_=oT_: h.matmul(p[:, :], lhsT=oT_[:, kk, :], rhs=Wo[:, kk, half * 512:(half + 1) * 512],
                                                                                       start=(kk == 0), stop=(kk == 7)), reads=[oT_.b, Wo.b], writes=[p.b])
                        k.op("dve", lambda h, half=half, p=p, tm=tm: h.tensor_tensor(out=tm[:, half * 512:(half + 1) * 512], in0=p[:, :],
                                                                                   in1=G[:, half * 512:(half + 1) * 512], op=ALU.mult),
                             reads=[p.b, G.b], writes=[tm.b])
                    k.op("pool", lambda h, tm=tm, x_=xt[i]: h.tensor_tensor(out=tm[:], in0=tm[:], in1=x_[:], op=ALU.add),
                         reads=[tm.b, xt[i].b], writes=[tm.b])
                    k.dma("pool", xd_ap[t0:t0 + 128, :], tm[:], xd_buf, tm.b)
                stageA(0)
                for tt in range(NT):
                    if tt + 1 < NT:
                        stageA(tt + 1)
                    stageB(tt)
                k.barrier()

        def phase3(li, xs_ap, xs_buf, xd_ap, xd_buf):
            TB = 256
            with ExitStack() as st:
                W1 = sb(st, "W1", [128, 8, DFF], BF16)
                W2 = sb(st, "W2", [128, 32, D], BF16)
                load_w_bf16(W1, mlp_w1[li], DFF)
                load_w_bf16(W2, mlp_w2[li], D, kchunks=32)
                G = sb(st, "G", [128, D])
                load_bcast(G, li, 5)
                nrm = Norm(st, li, 1, alloc_xt=False)
                hT = [sb(st, "hT%d" % i, [128, 8, TB], BF16) for i in range(2)]
                uT = sb(st, "uT", [128, 32, TB], BF16)
                xk = [[sb(st, "xk%d_%d" % (i, t), [128, D]) for t in range(TB // 128)] for i in range(2)]
                rb_ = [sb(st, "rb%d" % i, [128, TB]) for i in range(4)]
                tmp = [sb(st, "tmp%d" % i, [128, D]) for i in range(2)]
                ti = 0
                def norm_block3(b):
                    for tt in range(TB // 128):
                        t0 = b * TB + tt * 128
                        nrm.run(xs_ap[t0:t0 + 128, :], xs_buf, hT[b % 2], tt * 128, keep=xk[b % 2][tt])
                norm_block3(0)
                for b in range(L // TB):
                    h_ = hT[b % 2]
                    for f in range(32):
                        p = ps[f % 4]
                        r_ = rb_[f % 4]
                        for kk in range(8):
                            k.op("pe", lambda h, kk=kk, f=f, p=p: h.matmul(p[:, 0:TB], lhsT=W1[:, kk, f * 128:(f + 1) * 128], rhs=h_[:, kk, :],
                                                                         start=(kk == 0), stop=(kk == 7)), reads=[W1.b, h_.b], writes=[p.b])
                        k.op("act", lambda h, p=p, r_=r_: h.activation(out=r_[:], in_=p[:, 0:TB], func=AF.Relu), reads=[p.b], writes=[r_.b])
                        k.op("dve" if f % 2 else "pool", lambda h, f=f, r_=r_: h.tensor_tensor(out=uT[:, f, :], in0=r_[:], in1=r_[:], op=ALU.mult),
                             reads=[r_.b], writes=[uT.b])
                    if b + 1 < L // TB:
                        norm_block3(b + 1)
                    for tt in range(TB // 128):
                        t0 = b * TB + tt * 128
                        tm = tmp[ti % 2]
                        ti += 1
                        for half in range(2):
                            p = ps[4 + half]
                            for f in range(32):
                                k.op("pe", lambda h, f=f, half=half, p=p, tt=tt: h.matmul(p[:, :], lhsT=uT[:, f, tt * 128:(tt + 1) * 128],
                                                                                       rhs=W2[:, f, half * 512:(half + 1) * 512],
                                                                                       start=(f == 0), stop=(f == 31)), reads=[uT.b, W2.b], writes=[p.b])
                            k.op("dve", lambda h, half=half, p=p, tm=tm: h.tensor_tensor(out=tm[:, half * 512:(half + 1) * 512], in0=p[:, :],
                                                                                       in1=G[:, half * 512:(half + 1) * 512], op=ALU.mult),
                                 reads=[p.b, G.b], writes=[tm.b])
                        x_ = xk[b % 2][tt]
                        k.op("pool", lambda h, tm=tm, x_=x_: h.tensor_tensor(out=tm[:], in0=tm[:], in1=x_[:], op=ALU.add),
                             reads=[tm.b, x_.b], writes=[tm.b])
                        k.dma("pool", xd_ap[t0:t0 + 128, :], tm[:], xd_buf, tm.b)
                k.barrier()

        def final_phase(xs_ap, xs_buf):
            with ExitStack() as st:
                fg = sb(st, "fg", [128, D])
                k.dma("sp", fg[:], bass.AP(tensor=final_g.tensor, offset=0, ap=[[0, 128], [1, D]]), fg.b, IN)
                xt = [sb(st, "xt%d" % i, [128, D]) for i in range(2)]
                junk = sb(st, "junk", [128, D])
                ssq = [sb(st, "ssq%d" % i, [128, 1], strict=True) for i in range(2)]
                yo = [sb(st, "yo%d" % i, [128, D]) for i in range(2)]
                for tt in range(NT):
                    i = tt % 2
                    t0 = tt * 128
                    x_, s_, y_ = xt[i], ssq[i], yo[i]
                    k.dma("sp", x_[:], xs_ap[t0:t0 + 128, :], x_.b, xs_buf)
                    k.op("act", lambda h, x_=x_, s_=s_: h.activation(out=junk[:], in_=x_[:], func=AF.Square, accum_out=s_[:]),
                         reads=[x_.b], writes=[junk.b, s_.b])
                    k.op("act", lambda h, s_=s_: h.activation(out=s_[:], in_=s_[:], func=AF.Sqrt, scale=1.0 / D, bias=epsc[:, 0:1]),
                         reads=[s_.b, epsc.b], writes=[s_.b])
                    k.op("dve", lambda h, s_=s_: h.reciprocal(out=s_[:], in_=s_[:]), reads=[s_.b], writes=[s_.b])
                    k.op("act", lambda h, x_=x_, s_=s_, y_=y_: h.activation(out=y_[:], in_=x_[:], func=AF.Copy, scale=s_[:, 0:1]),
                         reads=[x_.b, s_.b], writes=[y_.b])
                    k.op("pool", lambda h, y_=y_: h.tensor_tensor(out=y_[:], in0=y_[:], in1=fg[:], op=ALU.mult), reads=[y_.b, fg.b], writes=[y_.b])
                    k.dma("pool", y_out[t0:t0 + 128, :], y_[:], DB("y"), y_.b)
                k.barrier()

        cur_ap, cur_buf = x_in, IN
        cnt = {0: 0, 1: 0, 2: 0}
        for li, kind in enumerate(kinds):
            j = cnt[kind]
            cnt[kind] += 1
            phase1(li, cur_ap, cur_buf, kind, j)
            (attn_A, attn_B, attn_C)[kind](j)
            phase2b(li, kind, j, cur_ap, cur_buf, xA, DB("xA"))
            phase3(li, xA, DB("xA"), xB, DB("xB"))
            cur_ap, cur_buf = xB, DB("xB")
        final_phase(cur_ap, cur_buf)
    return nc


def host_tables(L, LT, is_prompt):
    t = np.arange(L)
    inv32 = (10000.0 ** (-np.arange(32, dtype=np.float32) / 32)).astype(np.float32)
    d = np.arange(128)
    angB = t[None, :].astype(np.float32) * inv32[d % 32][:, None]
    row, col = (t // 64).astype(np.float32), (t % 64).astype(np.float32)
    inv16 = (10000.0 ** (-np.arange(32, dtype=np.float32) / 32)).astype(np.float32)
    angC = np.where(((d // 32) % 2 == 0)[:, None], row[None, :] * inv16[d % 32][:, None], col[None, :] * inv16[d % 32][:, None]).astype(np.float32)
    out = {
        "cosB": np.cos(angB).astype(np.float32), "sinB": np.sin(angB).astype(np.float32),
        "cosC": np.cos(angC).astype(np.float32), "sinC": np.sin(angC).astype(np.float32),
        "ident": np.eye(128, dtype=np.float32),
    }
    p = np.arange(128)[:, None]
    i = np.arange(128)[None, :]
    out["bandm"] = np.concatenate([(p >= i), (p <= i)], axis=1).astype(np.float32)
    NQT = L // 128
    bb = np.zeros((128, 3, NQT, 2), np.float32)
    for g, dil in enumerate((1, 4, 16)):
        nmb = (L // dil) // 128
        for r in range(dil):
            for mb in range(nmb):
                for slot in range(2):
                    tok = dil * (128 * mb - 64 + 128 * slot + np.arange(128)) + r
                    bb[:, g, r * nmb + mb, slot] = np.where((tok >= 0) & (tok < LT), 0.0, NEG)
    out["bbias"] = bb.reshape(128, -1)
    NT = L // 128
    cb = np.zeros((128, NT), np.float32)
    cb[:, LT // 128:] = NEG
    out["cbias"] = cb
    fl = np.zeros((128, 2), np.float32)
    fl[:, 0] = 0.0 if is_prompt else 1.0
    fl[:, 1] = 1.0 if is_prompt else 0.0
    out["flags"] = fl
    return out


def a_bias_layout(a_rpb):
    n = a_rpb.shape[0]
    kc = np.arange(64)[:, None]
    qc = np.arange(64)[None, :]
    cs = np.clip(qc - 8, 0, 48)
    inwin = (kc >= cs) & (kc < cs + 16)
    dc = np.clip(kc - qc + 15, 0, 30)
    g = a_rpb[:, :, :, dc]
    g = np.where(inwin[None, None, None], g, np.float32(NEG)).astype(np.float32)
    g = np.transpose(g, (0, 1, 3, 2, 4)).reshape(n, 16, 64, 15 * 64)
    return np.ascontiguousarray(g)


_CACHE = {}


def run_slots(slots, weights, L, LP, kinds, n_cores):
    key = (L, LP, tuple(kinds))
    if key not in _CACHE:
        _CACHE[key] = build({"L": L, "LP": LP, "kinds": list(kinds)})
    nc = _CACHE[key]
    common = dict(weights)
    common["a_biasT"] = a_bias_layout(np.asarray(weights["a_rpb"], np.float32))
    del common["a_rpb"]
    pidx = np.arange(128)
    i1 = ((pidx // 32) % 2) * 64 + (pidx % 32)
    qg, kg = common.pop("c_q_g"), common.pop("c_k_g")
    common["c_gcol"] = np.ascontiguousarray(np.stack([qg[:, i1], qg[:, i1 + 32], kg[:, i1], kg[:, i1 + 32]], axis=-1).astype(np.float32))
    in_maps = []
    for (x, c, isp) in slots:
        LT = x.shape[0]
        xs = np.zeros((L, D), np.float32)
        xs[:LT] = x
        m = dict(common)
        m["x"] = xs
        m["cT"] = np.ascontiguousarray(np.asarray(c, np.float32).reshape(8, 128).T)
        m.update(host_tables(L, LT, isp))
        in_maps.append(m)
    while len(in_maps) < n_cores:
        in_maps.append(in_maps[-1])
    res = run_bass_kernel_spmd(nc, in_maps, core_ids=list(range(n_cores)))
    return [r["y"] for r in res.results]


def kernel(x_prompt, x_sample, c_prompt, c_sample, w_mod, b_mod, norm_g, final_g,
           a_w_qkv, a_rpb, a_w_o, b_w_qkv, b_w_o, c_w_qkv, c_q_g, c_k_g, c_w_o, mlp_w1, mlp_w2):
    f = lambda a: np.ascontiguousarray(np.asarray(a, np.float32))
    weights = dict(w_mod=f(w_mod), b_mod=f(b_mod), norm_g=f(norm_g), final_g=f(final_g), a_w_qkv=f(a_w_qkv),
                   a_rpb=f(a_rpb), a_w_o=f(a_w_o), b_w_qkv=f(b_w_qkv), b_w_o=f(b_w_o), c_w_qkv=f(c_w_qkv),
                   c_q_g=f(c_q_g), c_k_g=f(c_k_g), c_w_o=f(c_w_o), mlp_w1=f(mlp_w1), mlp_w2=f(mlp_w2))
    x_prompt, x_sample = f(x_prompt), f(x_sample)
    c_prompt, c_sample = f(c_prompt), f(c_sample)
    L = x_sample.shape[1]
    LP = x_prompt.shape[1]
    slots = [(x_sample[b], c_sample[b], False) for b in range(x_sample.shape[0])]
    slots += [(x_prompt[b], c_prompt[b], True) for b in range(x_prompt.shape[0])]
    ys = run_slots(slots, weights, L, LP, [0, 1, 2, 0], 8)
    ns = x_sample.shape[0]
    y_sample = np.stack([ys[b] for b in range(ns)]).astype(np.float32)
    y_prompt = np.stack([ys[ns + b][:LP] for b in range(x_prompt.shape[0])]).astype(np.float32)
    return (y_prompt, y_sample)
```
